# Optimizing a Trainium2 kernel written in Bass

```python
import math
import jax
import jax.numpy as jnp
from jax import lax
import numpy as np

D_MODEL = 1024
BATCH = 4
SEQ = 8192
DEPTH = 4

N_EVEN = (DEPTH + 1) // 2
N_ODD = DEPTH // 2
PLE_DIM = 256
NORM_EPS = 1e-6
F32 = jnp.float32

A_HEADS = 4
A_HEAD_DIM = 128
A_WIDTH = A_HEADS * A_HEAD_DIM
A_CONV = 5
A_CHUNK = 64
A_COLS = 4 * A_WIDTH + 4 * A_HEADS

B_HEADS = 8
B_HEAD_DIM = 64
B_WIDTH = B_HEADS * B_HEAD_DIM
B_DECAY_LORA = 32
B_AAA_LORA = 32
B_GATE_LORA = 96
B_GN_EPS = 64e-5
B_DECAY_SCALE = 0.606531
B_COLS = 3 * B_WIDTH + 2 * B_DECAY_LORA + 2 * B_AAA_LORA + B_GATE_LORA

EVEN_IN = A_COLS + B_COLS
EVEN_MIX = A_WIDTH + B_WIDTH

C_CHUNK = 128
C_GROUPS = 8
C_WIDTH = 1024
C_GROUP_DIM = C_WIDTH // C_GROUPS
C_LN_EPS = 1e-5

D_FF = -(-8 * D_MODEL // (3 * 256)) * 256

kernel_name = 'hybrid_deltanet_rwkv7_gmlp_encoder'


def rmsnorm(x, w, eps=NORM_EPS):
    xf = x.astype(F32)
    y = xf * lax.rsqrt(jnp.mean(xf * xf, axis=-1, keepdims=True) + eps)
    return (y * w).astype(x.dtype)


def layernorm(x, w, b, eps):
    xf = x.astype(F32)
    mu = jnp.mean(xf, axis=-1, keepdims=True)
    xc = xf - mu
    var = jnp.mean(xc * xc, axis=-1, keepdims=True)
    return (xc * lax.rsqrt(var + eps) * w + b).astype(x.dtype)


def l2norm(x):
    xf = x.astype(F32)
    return xf * lax.rsqrt(jnp.sum(xf * xf, axis=-1, keepdims=True) + 1e-6)


def dwconv_centred(x, w):
    K = w.shape[0]
    return lax.conv_general_dilated(x, w[:, None, :], window_strides=(1,), padding=[(K // 2, K // 2)],
                                    dimension_numbers=('NWC', 'WIO', 'NWC'), feature_group_count=x.shape[-1])


def centred_shift(x):
    xp = jnp.pad(x, ((0, 0), (1, 1), (0, 0)))
    return 0.5 * (xp[:, :-2] + xp[:, 2:])


def stack_dirs(fwd, bwd):
    return jnp.concatenate([fwd, jnp.flip(bwd, axis=1)], axis=0)


def merge_dirs(y, n):
    return y[:n] + jnp.flip(y[n:], axis=1)


def delta_chunk_step(S, inp):
    u, w, qk, q_dec, k_dec, g_last = inp
    v_new = u - jnp.einsum('bhcd,bhde->bhce', w, S)
    o = jnp.einsum('bhcd,bhde->bhce', q_dec, S) + jnp.einsum('bhij,bhje->bhie', qk, v_new)
    S = S * jnp.exp(g_last)[..., None, None] + jnp.einsum('bhcd,bhce->bhde', k_dec, v_new)
    return S, o


def gated_delta_chunked(q, k, v, g, beta):
    Bd, T, H, DK = q.shape
    DV = v.shape[-1]
    N = T // A_CHUNK

    def chunk(t):
        t = t.astype(F32).reshape((Bd, N, A_CHUNK, H) + t.shape[3:])
        return jnp.moveaxis(t, 3, 1)

    q = chunk(q) * (DK ** -0.5)
    k, v, g, beta = chunk(k), chunk(v), chunk(g), chunk(beta)
    gc = jnp.cumsum(g, axis=-1)
    incl = jnp.tril(jnp.ones((A_CHUNK, A_CHUNK), bool))
    strict = jnp.tril(jnp.ones((A_CHUNK, A_CHUNK), bool), -1)
    diff = gc[..., :, None] - gc[..., None, :]
    decay = jnp.where(incl, jnp.exp(jnp.where(incl, diff, 0.0)), 0.0)
    m = jnp.where(strict, beta[..., :, None] * jnp.einsum('bhnid,bhnjd->bhnij', k, k) * decay, 0.0)
    eye = jnp.eye(A_CHUNK, dtype=F32)
    t_inv = lax.linalg.triangular_solve(eye + m, jnp.broadcast_to(eye, m.shape), left_side=True, lower=True)
    u = jnp.einsum('bhnij,bhnjd->bhnid', t_inv, v * beta[..., None])
    w = jnp.einsum('bhnij,bhnjd->bhnid', t_inv, k * (beta * jnp.exp(gc))[..., None])
    qk = jnp.einsum('bhnid,bhnjd->bhnij', q, k) * decay
    q_dec = q * jnp.exp(gc)[..., None]
    g_last = gc[..., -1]
    k_dec = k * jnp.exp(g_last[..., None] - gc)[..., None]
    xs = tuple(jnp.moveaxis(t, 2, 0) for t in (u, w, qk, q_dec, k_dec, g_last))
    S0 = jnp.zeros((Bd, H, DK, DV), F32)
    _, o = lax.scan(delta_chunk_step, S0, xs)
    return jnp.transpose(o, (1, 0, 3, 2, 4)).reshape(Bd, T, H, DV)


def gated_deltanet_mixer(pa, conv_w, a_log, dt_bias, norm_w):
    Bb, T, _ = pa.shape
    heads = lambda t: t.reshape(t.shape[:-1] + (A_HEADS, A_HEAD_DIM))
    qkv = jax.nn.silu(dwconv_centred(pa[..., :3 * A_WIDTH], conv_w))
    q = l2norm(heads(qkv[..., :A_WIDTH]))
    k = l2norm(heads(qkv[..., A_WIDTH:2 * A_WIDTH]))
    v = heads(qkv[..., 2 * A_WIDTH:])
    z = heads(pa[..., 3 * A_WIDTH:4 * A_WIDTH])
    ab = pa[..., 4 * A_WIDTH:].astype(F32).reshape(Bb, T, 4, A_HEADS)
    g = -jnp.exp(a_log.astype(F32)) * jax.nn.softplus(ab[:, :, :2] + dt_bias)
    beta = jax.nn.sigmoid(ab[:, :, 2:])
    o = gated_delta_chunked(stack_dirs(q, q), stack_dirs(k, k), stack_dirs(v, v),
                            stack_dirs(g[:, :, 0], g[:, :, 1]), stack_dirs(beta[:, :, 0], beta[:, :, 1]))
    o = merge_dirs(o, Bb)
    o = rmsnorm(o, norm_w) * jax.nn.silu(z)
    return o.reshape(Bb, T, A_WIDTH).astype(pa.dtype)


def rwkv7_step(S, inp):
    r, w, k, v, kk, a = inp
    sa = jnp.einsum('bhvk,bhk->bhv', S, kk)
    S = S * w[:, :, None, :] - sa[..., None] * (kk * a)[:, :, None, :] + v[..., None] * k[:, :, None, :]
    return S, jnp.einsum('bhvk,bhk->bhv', S, r)


def rwkv7_mixer(pb, mu, w0, w2, a0, a2, g2, k_k, k_a, r_k, ln_w, ln_b):
    Bb, T, _ = pb.shape
    heads = lambda t: t.reshape(t.shape[:-1] + (B_HEADS, B_HEAD_DIM))
    pb = pb + (centred_shift(pb) - pb) * mu
    r = pb[..., :B_WIDTH]
    k = pb[..., B_WIDTH:2 * B_WIDTH]
    v = pb[..., 2 * B_WIDTH:3 * B_WIDTH]
    o = 3 * B_WIDTH
    wd = pb[..., o:o + 2 * B_DECAY_LORA].reshape(Bb, T, 2, B_DECAY_LORA)
    o += 2 * B_DECAY_LORA
    ad = pb[..., o:o + 2 * B_AAA_LORA].reshape(Bb, T, 2, B_AAA_LORA)
    o += 2 * B_AAA_LORA
    gd = pb[..., o:]
    w = jnp.exp(-B_DECAY_SCALE * jax.nn.sigmoid((w0 + jnp.einsum('btel,elc->btec', jnp.tanh(wd), w2)).astype(F32)))
    a = jax.nn.sigmoid(a0 + jnp.einsum('btel,elc->btec', ad, a2))
    g = jax.nn.sigmoid(gd) @ g2
    kk = l2norm(heads(k * k_k))
    k_dir = k[:, :, None, :] * (1.0 + (a - 1.0) * k_a)
    r_h, v_h = heads(r), heads(v)
    xs = (stack_dirs(r_h, r_h),
          stack_dirs(heads(w[:, :, 0]), heads(w[:, :, 1])),
          stack_dirs(heads(k_dir[:, :, 0]), heads(k_dir[:, :, 1])),
          stack_dirs(v_h, v_h),
          stack_dirs(kk, kk),
          stack_dirs(heads(a[:, :, 0]), heads(a[:, :, 1])))
    xs = tuple(jnp.moveaxis(t.astype(F32), 1, 0) for t in xs)
    S0 = jnp.zeros((2 * Bb, B_HEADS, B_HEAD_DIM, B_HEAD_DIM), F32)
    _, y = lax.scan(rwkv7_step, S0, xs)
    y = merge_dirs(jnp.moveaxis(y, 0, 1), Bb)
    y = layernorm(y, heads(ln_w), heads(ln_b), B_GN_EPS)
    k_b = heads(0.5 * (k_dir[:, :, 0] + k_dir[:, :, 1])).astype(F32)
    bonus = jnp.sum(r_h * k_b * r_k, axis=-1, keepdims=True) * v_h
    return ((y + bonus).reshape(Bb, T, B_WIDTH) * g).astype(pb.dtype)


def even_mixer(h, w_in, w_out, a_conv, a_log, a_dt_bias, a_norm, b_mu, b_w0, b_w2, b_a0, b_a2, b_g2,
               b_k_k, b_k_a, b_r_k, b_ln_w, b_ln_b):
    proj = h @ w_in
    ya = gated_deltanet_mixer(proj[..., :A_COLS], a_conv, a_log, a_dt_bias, a_norm)
    yb = rwkv7_mixer(proj[..., A_COLS:], b_mu, b_w0, b_w2, b_a0, b_a2, b_g2, b_k_k, b_k_a, b_r_k, b_ln_w, b_ln_b)
    return jnp.concatenate([ya, yb], axis=-1) @ w_out


def odd_mixer(h, w_in, ln_w, ln_b, ws, bs, w_out):
    Bb, T, _ = h.shape
    uv = jax.nn.gelu(h @ w_in)
    u, v = uv[..., :C_WIDTH], uv[..., C_WIDTH:]
    v = layernorm(v, ln_w, ln_b, C_LN_EPS).reshape(Bb, T // C_CHUNK, C_CHUNK, C_GROUPS, C_GROUP_DIM)
    sv = jnp.einsum('gij,bnjgd->bnigd', ws, v) + bs.T[None, None, :, :, None]
    return (u * sv.reshape(Bb, T, C_WIDTH)) @ w_out


def swiglu(h, w_gate, w_up, w_down):
    return (jax.nn.silu(h @ w_gate) * (h @ w_up)) @ w_down


def setup_inputs(seed: int = 0) -> dict:
    key = jax.random.key(seed)
    ks = iter(jax.random.split(key, 48))

    def nrm(shape, scale):
        return jax.random.normal(next(ks), shape, F32) * scale

    def gain(shape):
        return 1.0 + nrm(shape, 0.05)

    def unif(shape, lo, hi):
        return jax.random.uniform(next(ks), shape, F32, lo, hi)

    dt = jnp.exp(unif((N_EVEN, 2, A_HEADS), math.log(1e-3), math.log(1e-1)))
    return {
        'x': nrm((BATCH, SEQ, D_MODEL), 1.0),
        'p': nrm((DEPTH, BATCH, SEQ, PLE_DIM), 1.0),
        'norm_mix': gain((DEPTH, D_MODEL)),
        'norm_ffn': gain((DEPTH, D_MODEL)),
        'norm_ple': gain((DEPTH, D_MODEL)),
        'norm_final': gain((D_MODEL,)),
        'w_in_even': nrm((N_EVEN, D_MODEL, EVEN_IN), D_MODEL ** -0.5),
        'w_out_even': nrm((N_EVEN, EVEN_MIX, D_MODEL), EVEN_MIX ** -0.5),
        'a_conv': nrm((N_EVEN, A_CONV, 3 * A_WIDTH), A_CONV ** -0.5),
        'a_log': jnp.log(unif((N_EVEN, 2, A_HEADS), 1.0, 16.0)),
        'a_dt_bias': dt + jnp.log(-jnp.expm1(-dt)),
        'a_norm': gain((N_EVEN, A_HEAD_DIM)),
        'b_mu': unif((N_EVEN, B_COLS), 0.0, 1.0),
        'b_w0': nrm((N_EVEN, 2, B_WIDTH), 0.5),
        'b_w2': nrm((N_EVEN, 2, B_DECAY_LORA, B_WIDTH), B_DECAY_LORA ** -0.5),
        'b_a0': nrm((N_EVEN, 2, B_WIDTH), 0.1),
        'b_a2': nrm((N_EVEN, 2, B_AAA_LORA, B_WIDTH), B_AAA_LORA ** -0.5),
        'b_g2': nrm((N_EVEN, B_GATE_LORA, B_WIDTH), B_GATE_LORA ** -0.5),
        'b_k_k': 0.85 + nrm((N_EVEN, B_WIDTH), 0.05),
        'b_k_a': gain((N_EVEN, B_WIDTH)),
        'b_r_k': nrm((N_EVEN, B_HEADS, B_HEAD_DIM), 0.1),
        'b_ln_w': gain((N_EVEN, B_WIDTH)),
        'b_ln_b': nrm((N_EVEN, B_WIDTH), 0.02),
        'w_in_odd': nrm((N_ODD, D_MODEL, 2 * C_WIDTH), D_MODEL ** -0.5),
        'c_ln_w': gain((N_ODD, C_WIDTH)),
        'c_ln_b': nrm((N_ODD, C_WIDTH), 0.02),
        'c_ws': nrm((N_ODD, C_GROUPS, C_CHUNK, C_CHUNK), C_CHUNK ** -0.5),
        'c_bs': 1.0 + nrm((N_ODD, C_GROUPS, C_CHUNK), 0.1),
        'w_out_odd': nrm((N_ODD, C_WIDTH, D_MODEL), C_WIDTH ** -0.5),
        'w_gate': nrm((DEPTH, D_MODEL, D_FF), D_MODEL ** -0.5),
        'w_up': nrm((DEPTH, D_MODEL, D_FF), D_MODEL ** -0.5),
        'w_down': nrm((DEPTH, D_FF, D_MODEL), D_FF ** -0.5),
        'w_ple': nrm((DEPTH, PLE_DIM, D_MODEL), PLE_DIM ** -0.5),
        'w_ple_gate': nrm((DEPTH, D_MODEL, D_MODEL), D_MODEL ** -0.5),
    }


def reference(x, p, norm_mix, norm_ffn, norm_ple, norm_final, w_in_even, w_out_even, a_conv, a_log,
              a_dt_bias, a_norm, b_mu, b_w0, b_w2, b_a0, b_a2, b_g2, b_k_k, b_k_a, b_r_k, b_ln_w, b_ln_b,
              w_in_odd, c_ln_w, c_ln_b, c_ws, c_bs, w_out_odd, w_gate, w_up, w_down, w_ple, w_ple_gate):
    h = x
    for i in range(DEPTH):
        j = i // 2
        hn = rmsnorm(h, norm_mix[i])
        if i % 2 == 0:
            h = h + even_mixer(hn, w_in_even[j], w_out_even[j], a_conv[j], a_log[j], a_dt_bias[j], a_norm[j],
                               b_mu[j], b_w0[j], b_w2[j], b_a0[j], b_a2[j], b_g2[j], b_k_k[j], b_k_a[j],
                               b_r_k[j], b_ln_w[j], b_ln_b[j])
        else:
            h = h + odd_mixer(hn, w_in_odd[j], c_ln_w[j], c_ln_b[j], c_ws[j], c_bs[j], w_out_odd[j])
        h = h + swiglu(rmsnorm(h, norm_ffn[i]), w_gate[i], w_up[i], w_down[i])
        h = h + (p[i] @ w_ple[i]) * jax.nn.sigmoid(rmsnorm(h, norm_ple[i]) @ w_ple_gate[i])
    return rmsnorm(h, norm_final)
```

```python
import bisect
import numpy as np
import concourse.bass as bass
import concourse.mybir as mybir
from concourse.bass_utils import run_bass_kernel_spmd

F32 = mybir.dt.float32
F32R = mybir.dt.float32r
AF = mybir.ActivationFunctionType
ALU = mybir.AluOpType
AX = mybir.AxisListType

GEN = 30000
DGEN = 1800


class Buf:
    __slots__ = ("name", "last_w", "readers", "lane_in", "lane_out")

    def __init__(self, name):
        self.name = name
        self.last_w = None
        self.readers = []
        self.lane_in = None
        self.lane_out = None


class Op:
    __slots__ = ("eng", "fn", "deps", "stream", "sidx", "signal", "val", "waits", "oid")


class Prog:
    CE = ("pe", "act", "dve", "pool")

    def __init__(self, nc):
        self.nc = nc
        self.ops = []
        self.streams = {}
        self.pending_bar = {}

    def _stream(self, name):
        return self.streams.setdefault(name, [])

    def add(self, eng, fn, reads=(), writes=(), dma=None):
        op = Op()
        op.oid = len(self.ops)
        op.eng = eng
        op.fn = fn
        deps = {}
        for b in reads:
            if b.last_w is not None:
                deps[b.last_w] = "raw"
        for b in writes:
            if b.last_w is not None:
                deps.setdefault(b.last_w, "waw")
            for r in b.readers:
                deps.setdefault(r, "war")
        if dma is not None:
            kind, lb = dma
            op.stream = "dma_%s_%s" % (kind, lb.name)
        else:
            op.stream = eng
        pruned = {}
        for d, k in deps.items():
            p = self.ops[d]
            if p.stream == op.stream:
                if dma is not None:
                    continue
                if eng == "pe":
                    continue
            pruned[d] = k
        if eng in self.pending_bar:
            for d in self.pending_bar.pop(eng):
                if self.ops[d].stream != op.stream or dma is not None:
                    pruned[d] = "bar"
        op.deps = pruned
        st = self._stream(op.stream)
        op.sidx = len(st)
        st.append(op.oid)
        op.signal = dma is not None
        self.ops.append(op)
        for b in reads:
            b.readers.append(op.oid)
        for b in writes:
            b.last_w = op.oid
            b.readers = []
        return op.oid

    def barrier(self):
        last = set(lst[-1] for lst in self.streams.values() if lst)
        for e in ("pe", "act", "dve", "pool", "sp"):
            self.pending_bar[e] = set(last) | self.pending_bar.get(e, set())

    def finalize_and_emit(self, final_waits=()):
        nc = self.nc
        ops = self.ops
        waited = {}
        for op in ops:
            need = {}
            for d in op.deps:
                p = ops[d]
                if p.stream.startswith("dma_"):
                    lane = self.streams[p.stream]
                    cnt = bisect.bisect_left(lane, op.oid)
                    need[p.stream] = max(need.get(p.stream, -1), cnt - 1)
                else:
                    need[p.stream] = max(need.get(p.stream, -1), p.sidx)
            op.waits = []
            for s, idx in need.items():
                key = (op.eng, s)
                if waited.get(key, -1) >= idx:
                    continue
                waited[key] = idx
                op.waits.append((s, idx))
                if not s.startswith("dma_"):
                    ops[self.streams[s][idx]].signal = True
        fin = []
        for s_, lst_ in self.streams.items():
            if s_.startswith("dma_") and lst_:
                fin.append((s_, len(lst_) - 1))
        sem_of = {}
        import contextlib
        stack = contextlib.ExitStack()
        with stack:
            valmap = {}
            for s, lst in self.streams.items():
                if s.startswith("dma_"):
                    sem = None
                    for i, o in enumerate(lst):
                        if i % DGEN == 0:
                            sem = stack.enter_context(nc.semaphore("s_%s_%d" % (s, i // DGEN)))
                        valmap[(s, i)] = (sem, 16 * (i % DGEN + 1))
                        ops[o].val = (sem, 16)
                else:
                    cnt = 0
                    gen = 0
                    sem = stack.enter_context(nc.semaphore("s_%s_%d" % (s, gen)))
                    last = None
                    for i, o in enumerate(lst):
                        if ops[o].signal:
                            if cnt >= GEN:
                                gen += 1
                                cnt = 0
                                sem = stack.enter_context(nc.semaphore("s_%s_%d" % (s, gen)))
                            cnt += 1
                            ops[o].val = (sem, 1)
                            valmap[(s, i)] = (sem, cnt)
            self.n_sems = sum(1 for _ in sem_of)
            per_eng = {e: [] for e in ("pe", "act", "dve", "pool", "sp")}
            for op in ops:
                per_eng[op.eng].append(op)

            def run_engine(eobj, lst, is_sp=False):
                for op in lst:
                    for (s, idx) in op.waits:
                        sem, v = valmap[(s, idx)]
                        eobj.wait_ge(sem, v)
                    ins = op.fn(eobj)
                    if op.signal:
                        sem, inc = op.val
                        ins.then_inc(sem, inc)
                if is_sp:
                    for (s, idx) in fin:
                        sem, v = valmap[(s, idx)]
                        eobj.wait_ge(sem, v)

            with nc.Block() as block:
                @block.tensor
                def _(e):
                    run_engine(e, per_eng["pe"])

                @block.scalar
                def _(e):
                    run_engine(e, per_eng["act"])

                @block.vector
                def _(e):
                    run_engine(e, per_eng["dve"])

                @block.gpsimd
                def _(e):
                    run_engine(e, per_eng["pool"])

                @block.sync
                def _(e):
                    run_engine(e, per_eng["sp"], is_sp=True)


D = 1024
KD = 8
DEPTH = 4
PLE = 256
DFF = 2816
NFF = 22
A_COLS = 2064
B_COLS = 1760
EVEN_IN = A_COLS + B_COLS
TS = 512
NEG = -30000.0


class Pack:
    def __init__(self):
        self.cols = []
        self.off = {}
        self.n = 0

    def add(self, name, arr):
        arr = np.ascontiguousarray(arr, dtype=np.float32)
        assert arr.ndim == 2 and arr.shape[0] <= 128, (name, arr.shape)
        if arr.shape[0] < 128:
            pad = np.zeros((128, arr.shape[1]), np.float32)
            pad[: arr.shape[0]] = arr
            arr = pad
        self.off[name] = (self.n, arr.shape[1])
        self.cols.append(arr)
        self.n += arr.shape[1]

    def array(self):
        return np.concatenate(self.cols, axis=1)


def chunked(v, nch=None):
    v = np.asarray(v, np.float32).reshape(-1, 128)
    return np.ascontiguousarray(v.T)


def rowrep(v):
    v = np.asarray(v, np.float32).reshape(1, -1)
    return np.ascontiguousarray(np.broadcast_to(v, (128, v.shape[1])))


class Ring:
    def __init__(self, K, name, shape, dtype, n):
        self.tiles = []
        for i in range(n):
            t = K.sb("%s%d" % (name, i), shape, dtype)
            self.tiles.append((t, Buf("%s%d" % (name, i))))
        self.i = 0

    def next(self):
        r = self.tiles[self.i % len(self.tiles)]
        self.i += 1
        return r


class KB:
    def __init__(self, nc, st):
        self.nc = nc
        self.st = st
        self.P = Prog(nc)
        self.ps_tiles = []
        for i in range(8):
            t = st.enter_context(nc.psum_tensor("ps%d" % i, [128, 512], F32))
            self.ps_tiles.append((t, Buf("ps%d" % i)))
        self.ps_i = 0
        self.dq = 0

    def init_arena(self, nfloats):
        self.arena_t = self.st.enter_context(self.nc.sbuf_tensor("arena", [128, nfloats], F32))
        self.arena = self.arena_t
        self.arena_base = self.nc.lookup_mloc(self.arena_t).addr
        self.arena_n = nfloats
        self.arena_p = 0
        self.alias_n = 0

    def sb(self, name, shape, dtype=F32):
        n = 1
        for d in shape[1:]:
            n *= d
        off = (self.arena_p + 7) // 8 * 8
        assert off + n <= self.arena_n, ("arena overflow", name, off, n, self.arena_n)
        self.arena_p = off + n
        self.alias_n += 1
        t = self.nc.alloc_sbuf_tensor_at("%s_m%d" % (name, self.alias_n), [128, n], dtype, offset=self.arena_base + 4 * off)
        ap = t[0:shape[0], 0:n]
        if len(shape) == 3:
            ap = ap.rearrange("p (a b) -> p a b", a=shape[1])
        elif len(shape) == 4:
            ap = ap.rearrange("p (a b c) -> p a b c", a=shape[1], b=shape[2])
        return ap

    def mark(self):
        return self.arena_p

    def release(self, m):
        self.arena_p = m

    def psum(self):
        r = self.ps_tiles[self.ps_i % 8]
        self.ps_i += 1
        return r

    def dma_in(self, out_ap, in_ap, buf, rd=(), eng="sp"):
        self.P.add(eng, lambda e: e.dma_start(out=out_ap, in_=in_ap), reads=list(rd), writes=[buf], dma=("in", buf))

    def dma_out(self, out_ap, in_ap, buf, wr=(), eng="sp"):
        self.P.add(eng, lambda e: e.dma_start(out=out_ap, in_=in_ap), reads=[buf], writes=list(wr), dma=("out", buf))

    def mm(self, out_ap, lhsT, rhs, start, stop, rd, wr):
        self.P.add("pe", lambda e: e.matmul(out_ap, lhsT=lhsT, rhs=rhs, start=start, stop=stop), reads=rd, writes=wr)

    def act(self, out_ap, in_ap, func, rd, wr, bias=None, scale=None, accum_out=None):
        kw = {}
        if bias is not None:
            kw["bias"] = bias
        if scale is not None:
            kw["scale"] = scale
        if accum_out is not None:
            kw["accum_out"] = accum_out
        self.P.add("act", lambda e: e.activation(out=out_ap, in_=in_ap, func=func, **kw), reads=rd, writes=wr)

    def tt(self, out_ap, in0, in1, op, rd, wr, eng="dve"):
        self.P.add(eng, lambda e: e.tensor_tensor(out=out_ap, in0=in0, in1=in1, op=op), reads=rd, writes=wr)

    def ts(self, out_ap, in0, s1, op0, rd, wr, s2=None, op1=None, eng="dve", accum_out=None):
        kw = {}
        if accum_out is not None:
            kw["accum_out"] = accum_out
        if op1 is None:
            self.P.add(eng, lambda e: e.tensor_scalar(out=out_ap, in0=in0, scalar1=s1, scalar2=None, op0=op0, **kw), reads=rd, writes=wr)
        else:
            self.P.add(eng, lambda e: e.tensor_scalar(out=out_ap, in0=in0, scalar1=s1, scalar2=s2, op0=op0, op1=op1, **kw), reads=rd, writes=wr)

    def stt(self, out_ap, in0, scalar, in1, op0, op1, rd, wr, eng="dve"):
        self.P.add(eng, lambda e: e.scalar_tensor_tensor(out=out_ap, in0=in0, scalar=scalar, in1=in1, op0=op0, op1=op1), reads=rd, writes=wr)

    def copy(self, out_ap, in_ap, rd, wr, eng="dve"):
        if eng == "act":
            self.P.add("act", lambda e: e.activation(out=out_ap, in_=in_ap, func=AF.Copy), reads=rd, writes=wr)
        else:
            self.P.add(eng, lambda e: e.tensor_copy(out=out_ap, in_=in_ap), reads=rd, writes=wr)

    def recip(self, out_ap, in_ap, rd, wr):
        self.P.add("dve", lambda e: e.reciprocal(out=out_ap, in_=in_ap), reads=rd, writes=wr)

    def memset(self, ap, val, wr, eng="pool"):
        self.P.add(eng, lambda e: e.memset(ap, val), reads=[], writes=wr)


class DT:
    def __init__(self, nc, name, shape, kind="Internal", dtype=F32):
        self.name = name
        self.t = nc.dram_tensor(name, list(shape), dtype, kind=kind)
        self.ap = self.t.ap()
        self._b = {}

    def bufs(self, t0, t1):
        t0 = max(t0, 0)
        return [self._b.setdefault(i, Buf("%s_b%d" % (self.name, i))) for i in range(t0 // 128, (t1 + 127) // 128)]


def rmsnorm_fm(K, C, h, hbuf, wcol, out, obuf, sq, sqbuf, nk=KD, n=TS, eps_name="eps6", dscale=1.0 / D):
    K.act(sq[:, 0:nk * n], h[:, 0:nk * n], AF.Square, [hbuf], [sqbuf])
    ps, pb = K.psum()
    for c in range(nk):
        K.mm(ps[:, 0:n], C["ones_r"], sq[:, c * n:(c + 1) * n], c == 0, c == nk - 1, [sqbuf, C["buf"]], [pb])
    rs, rb = C["rstd"].next()
    K.act(rs[:, 0:n], ps[:, 0:n], AF.Sqrt, [pb, C["buf"]], [rb], bias=C[eps_name], scale=dscale)
    K.recip(rs[:, 0:n], rs[:, 0:n], [rb], [rb])
    for c in range(nk):
        K.stt(out[:, c * n:(c + 1) * n], h[:, c * n:(c + 1) * n], wcol[:, c:c + 1], rs[:, 0:n], ALU.mult, ALU.mult,
              [hbuf, rb, C["buf"]], [obuf])


def load_w(K, C, wap, r0, nk, c0, ncols, eng="pool"):
    wt, wb = C["wring"].next()
    view = wt[:, 0:nk * ncols].rearrange("p (k c) -> p k c", k=nk)
    src = wap[r0:r0 + nk * 128, c0:c0 + ncols].rearrange("(k p) c -> p k c", p=128)
    K.dma_in(view, src.bitcast(F32R), wb, eng=eng)
    return view, wb


def linear_fm(K, C, wap, nk, col_blocks, xin, xbuf, consumer, n=TS, r0=0):
    for (c0, ncols, chunks) in col_blocks:
        wv, wb = load_w(K, C, wap, r0, nk, c0, ncols)
        for (cid, off, M) in chunks:
            ps, pb = K.psum()
            for k in range(nk):
                K.mm(ps[0:M, 0:n], wv[:, k, off:off + M], xin[:, k * n:(k + 1) * n], k == 0, k == nk - 1, [wb, xbuf], [pb])
            consumer(cid, M, ps, pb)


def blocks_of(total, bs, cbase=0, m=128):
    out = []
    c0 = 0
    cid = cbase
    while c0 < total:
        nc_ = min(bs, total - c0)
        chunks = []
        off = 0
        while off < nc_:
            mm_ = min(m, nc_ - off)
            chunks.append((cid, off, mm_))
            cid += 1
            off += mm_
        out.append((c0, nc_, chunks))
        c0 += nc_
    return out


def build_packs(inp):
    P = Pack()
    P.add("ident", np.eye(128, dtype=np.float32))
    P.add("ones", np.ones((128, 128), np.float32))
    P.add("eps6", np.full((128, 1), 1e-6, np.float32))
    P.add("eps5", np.full((128, 1), 1e-5, np.float32))
    P.add("epsgn", np.full((128, 1), 64e-5, np.float32))
    for i in range(DEPTH):
        P.add("nmix%d" % i, chunked(inp["norm_mix"][i]))
        P.add("nffn%d" % i, chunked(inp["norm_ffn"][i]))
        P.add("nple%d" % i, chunked(inp["norm_ple"][i]))
    P.add("nfin", chunked(inp["norm_final"]))
    mask_pack(P)
    return P


class Ctx(dict):
    pass


def setup_consts(K, cp_dram, pack):
    C = Ctx()
    n = pack.n
    cpt = K.sb("cpack", [128, n])
    cb = Buf("cpack")
    K.dma_in(cpt[:, 0:n], cp_dram, cb)
    C["buf"] = cb
    C["cp"] = cpt
    for name, (off, w) in pack.off.items():
        C[name] = cpt[:, off:off + w]
    ones_r = K.sb("ones_r", [128, 128], F32R)
    K.P.add("pool", lambda e: e.tensor_copy(out=ones_r, in_=C["ones"]), reads=[cb], writes=[cb])
    C["ones_r"] = ones_r
    blk_r = K.sb("blk_r", [128, 128], F32R)
    K.P.add("pool", lambda e: e.tensor_copy(out=blk_r, in_=C["BLK"]), reads=[cb], writes=[cb])
    C["blk_r"] = blk_r
    C["rstd"] = Ring(K, "rstd", [128, TS], F32, 2)
    C["lc"] = (K.sb("lc", [128, 4096], F32), Buf("lc"))
    C["dv"] = (K.sb("dv", [128, 64], F32), Buf("dv"))
    return C


class Ring:
    def __init__(self, K, name, shape, dtype, n):
        self.tiles = []
        for i in range(n):
            t = K.sb("%s%d" % (name, i), shape, dtype)
            self.tiles.append((t, Buf("%s%d" % (name, i))))
        self.i = 0

    def next(self):
        r = self.tiles[self.i % len(self.tiles)]
        self.i += 1
        return r


def alloc_tl(K, C):
    A = {}
    for nm in ("h", "hn", "sq"):
        dt = F32 if nm in ("h",) else F32R
        A[nm] = (K.sb("tl_" + nm, [128, KD * TS], dt), Buf("tl_" + nm))
    A["ua"] = A["sq"]
    A["lc"] = C["lc"]
    A["dv"] = C["dv"]
    C["wring"] = Ring(K, "wring", [128, 4224], F32R, 2)
    A["act"] = (K.sb("tl_act", [128, NFF * TS], F32R), Buf("tl_act"))
    A["tmp"] = Ring(K, "tl_tmp", [128, TS], F32, 3)
    A["tok"] = Ring(K, "tl_tok", [128, 1024], F32, 2)
    A["tok2"] = Ring(K, "tl_tok2", [128, 1024], F32, 2)
    A["st"] = Ring(K, "tl_st", [128, 8], F32, 4)
    A["pt"] = (K.sb("tl_pt", [128, 2 * TS], F32R), Buf("tl_pt"))
    return A


def load_h_from_x(K, C, A, x_ap, t0):
    h, hb = A["h"]
    for sub in range(TS // 128):
        xt, xb = A["tok"].next()
        K.dma_in(xt[:, 0:1024], x_ap[t0 + sub * 128:t0 + (sub + 1) * 128, :], xb)
        for half in range(2):
            ps, pb = K.psum()
            for c4 in range(4):
                c = half * 4 + c4
                K.mm(ps[:, c4 * 128:(c4 + 1) * 128], xt[:, c * 128:(c + 1) * 128], C["ident"], True, True, [xb, C["buf"]], [pb])
            dst = h[:, half * 4 * TS:(half + 1) * 4 * TS].rearrange("p (c t) -> p c t", c=4)[:, :, sub * 128:(sub + 1) * 128]
            src = ps[:, 0:512].rearrange("p (c t) -> p c t", c=4)
            K.copy(dst, src, [pb], [hb], eng="act" if half else "dve")


def load_h(K, C, A, hT, t0):
    h, hb = A["h"]
    src = hT.ap[:, t0:t0 + TS].rearrange("(c p) t -> p c t", p=128)
    K.dma_in(h[:, 0:KD * TS].rearrange("p (c t) -> p c t", c=KD), src, hb, rd=hT.bufs(t0, t0 + TS))


def store_h(K, C, A, hT, t0):
    h, hb = A["h"]
    dst = hT.ap[:, t0:t0 + TS].rearrange("(c p) t -> p c t", p=128)
    K.dma_out(dst, h[:, 0:KD * TS].rearrange("p (c t) -> p c t", c=KD), hb, wr=hT.bufs(t0, t0 + TS))


def ffn_ple(K, C, A, W, li, p_ap, t0):
    h, hb = A["h"]
    hn, hnb = A["hn"]
    sq, sqb = A["sq"]
    act, ab = A["act"]
    rmsnorm_fm(K, C, h, hb, C["nffn%d" % li], hn, hnb, sq, sqb)
    gate_ps = {}

    def cons_gate(cid, M, ps, pb):
        gate_ps[cid] = (ps, pb)

    def cons_up(cid, M, ps, pb):
        gps, gpb = gate_ps.pop(cid)
        tmp, tb = A["tmp"].next()
        K.act(tmp[:, 0:TS], gps[:, 0:TS], AF.Silu, [gpb], [tb])
        K.tt(act[:, cid * TS:(cid + 1) * TS], tmp[:, 0:TS], ps[:, 0:TS], ALU.mult, [tb, pb], [ab])

    for (c0, ncols, chunks) in blocks_of(DFF, 384):
        linear_fm(K, C, W["w_gate"][li], KD, [(c0, ncols, chunks)], hn, hnb, cons_gate)
        linear_fm(K, C, W["w_up"][li], KD, [(c0, ncols, chunks)], hn, hnb, cons_up)

    def cons_down(cid, M, ps, pb):
        K.tt(h[:, cid * TS:(cid + 1) * TS], h[:, cid * TS:(cid + 1) * TS], ps[:, 0:TS], ALU.add, [hb, pb], [hb])

    linear_fm(K, C, W["w_down"][li], NFF, blocks_of(D, 128), act, ab, cons_down)
    rmsnorm_fm(K, C, h, hb, C["nple%d" % li], hn, hnb, sq, sqb)
    pt, ptb = A["pt"]
    for sub in range(TS // 128):
        xt, xb = A["tok"].next()
        K.dma_in(xt[:, 0:PLE], p_ap[t0 + sub * 128:t0 + (sub + 1) * 128, :], xb)
        ps, pb = K.psum()
        for c in range(2):
            K.mm(ps[:, c * 128:(c + 1) * 128], xt[:, c * 128:(c + 1) * 128], C["ident"], True, True, [xb, C["buf"]], [pb])
        dst = pt[:, 0:2 * TS].rearrange("p (c t) -> p c t", c=2)[:, :, sub * 128:(sub + 1) * 128]
        K.copy(dst, ps[:, 0:256].rearrange("p (c t) -> p c t", c=2), [pb], [ptb], eng="act")
    e_ps = {}

    def cons_e(cid, M, ps, pb):
        e_ps[cid] = (ps, pb)

    def cons_g(cid, M, ps, pb):
        eps_, epb = e_ps.pop(cid)
        tmp, tb = A["tmp"].next()
        K.act(tmp[:, 0:TS], ps[:, 0:TS], AF.Sigmoid, [pb], [tb])
        K.tt(tmp[:, 0:TS], tmp[:, 0:TS], eps_[:, 0:TS], ALU.mult, [tb, epb], [tb])
        K.tt(h[:, cid * TS:(cid + 1) * TS], h[:, cid * TS:(cid + 1) * TS], tmp[:, 0:TS], ALU.add, [hb, tb], [hb])

    for (c0, ncols, chunks) in blocks_of(D, 256):
        linear_fm(K, C, W["w_ple"][li], 2, [(c0, ncols, chunks)], pt, ptb, cons_e)
        linear_fm(K, C, W["w_ple_gate"][li], KD, [(c0, ncols, chunks)], hn, hnb, cons_g)


def final_out(K, C, A, out_ap, t0):
    h, hb = A["h"]
    hn, hnb = A["hn"]
    sq, sqb = A["sq"]
    rmsnorm_fm(K, C, h, hb, C["nfin"], hn, hnb, sq, sqb)
    hn32 = hn.bitcast(F32)
    for sub in range(TS // 128):
        ot, ob = A["tok"].next()
        for half in range(2):
            ps, pb = K.psum()
            for c4 in range(4):
                c = half * 4 + c4
                K.mm(ps[:, c4 * 128:(c4 + 1) * 128], hn32[:, c * TS + sub * 128:c * TS + (sub + 1) * 128], C["ident"], True, True,
                     [hnb, C["buf"]], [pb])
            K.copy(ot[:, half * 512:(half + 1) * 512], ps[:, 0:512], [pb], [ob], eng="act" if half else "dve")
        K.dma_out(out_ap[t0 + sub * 128:t0 + (sub + 1) * 128, :], ot[:, 0:1024], ob)


def odd_mixer(K, C, A, W, j, li):
    h, hb = A["h"]
    hn, hnb = A["hn"]
    sq, sqb = A["sq"]
    ua, uab = A["ua"]
    rmsnorm_fm(K, C, h, hb, C["nmix%d" % li], hn, hnb, sq, sqb)
    win = W["w_in_odd"][j]
    lc, lcb = A["lc"]
    c_lnw, c_lnb, c_wsT, c_bs = lc[:, 0:1024], lc[:, 1024:2048], lc[:, 2048:3072], lc[:, 3072:4096]

    def cons_u(cid, M, ps, pb):
        K.act(ua[:, cid * TS:(cid + 1) * TS], ps[:, 0:TS], AF.Gelu_apprx_tanh, [pb], [uab])

    linear_fm(K, C, win, KD, blocks_of(1024, 512), hn, hnb, cons_u)
    wv = []
    for half in range(2):
        wv.append(load_w(K, C, win, 0, KD, 1024 + half * 512, 512))
    for sub in range(TS // 128):
        vt, vb = A["tok"].next()
        for half in range(2):
            wview, wb = wv[half]
            ps, pb = K.psum()
            for k in range(KD):
                K.mm(ps[:, 0:512], hn[:, k * TS + sub * 128:k * TS + (sub + 1) * 128], wview[:, k, :], k == 0, k == KD - 1, [hnb, wb], [pb])
            K.act(vt[:, half * 512:(half + 1) * 512], ps[:, 0:512], AF.Gelu_apprx_tanh, [pb], [vb])
        st, sb_ = A["st"].next()
        v2, v2b = A["tok2"].next()
        K.P.add("dve", lambda e, st=st, vt=vt: e.reduce_sum(out=st[:, 0:1], in_=vt[:, 0:1024], axis=AX.X), reads=[vb], writes=[sb_])
        K.act(v2[:, 0:1024], vt[:, 0:1024], AF.Square, [vb], [v2b])
        K.P.add("dve", lambda e, st=st, v2=v2: e.reduce_sum(out=st[:, 1:2], in_=v2[:, 0:1024], axis=AX.X), reads=[v2b], writes=[sb_])
        K.ts(st[:, 2:3], st[:, 0:1], 1.0 / 1024, ALU.mult, [sb_], [sb_])
        K.tt(st[:, 3:4], st[:, 2:3], st[:, 2:3], ALU.mult, [sb_], [sb_])
        K.stt(st[:, 4:5], st[:, 1:2], 1.0 / 1024, st[:, 3:4], ALU.mult, ALU.subtract, [sb_], [sb_])
        K.act(st[:, 5:6], st[:, 4:5], AF.Sqrt, [sb_, C["buf"]], [sb_], bias=C["eps5"], scale=1.0)
        K.recip(st[:, 5:6], st[:, 5:6], [sb_], [sb_])
        K.ts(v2[:, 0:1024], vt[:, 0:1024], st[:, 2:3], ALU.subtract, [vb, sb_, v2b], [v2b], s2=st[:, 5:6], op1=ALU.mult)
        K.tt(v2[:, 0:1024], v2[:, 0:1024], c_lnw, ALU.mult, [v2b, lcb], [v2b])
        K.tt(v2[:, 0:1024], v2[:, 0:1024], c_lnb, ALU.add, [v2b, lcb], [v2b])
        for g2 in range(2):
            ps, pb = K.psum()
            for g4 in range(4):
                g = g2 * 4 + g4
                K.mm(ps[:, g4 * 128:(g4 + 1) * 128], v2[:, g * 128:(g + 1) * 128], c_wsT[:, g * 128:(g + 1) * 128], True, False,
                     [v2b, lcb], [pb])
                K.mm(ps[:, g4 * 128:(g4 + 1) * 128], C["ones"][0:1, :], c_bs[0:1, g * 128:(g + 1) * 128], False, True,
                     [C["buf"], lcb], [pb])
            dst = ua[:, g2 * 4 * TS:(g2 + 1) * 4 * TS].rearrange("p (c t) -> p c t", c=4)[:, :, sub * 128:(sub + 1) * 128]
            K.tt(dst, dst.bitcast(F32), ps[:, 0:512].rearrange("p (c t) -> p c t", c=4), ALU.mult, [uab, pb], [uab])

    def cons_o(cid, M, ps, pb):
        K.tt(h[:, cid * TS:(cid + 1) * TS], h[:, cid * TS:(cid + 1) * TS], ps[:, 0:TS], ALU.add, [hb, pb], [hb])

    linear_fm(K, C, W["w_out_odd"][j], KD, blocks_of(D, 512), ua, uab, cons_o)


def odd_pack(inp, j):
    out = np.zeros((128, 4096), np.float32)
    out[:, 0:1024] = rowrep(inp["c_ln_w"][j])
    out[:, 1024:2048] = rowrep(inp["c_ln_b"][j])
    out[:, 2048:3072] = np.transpose(np.asarray(inp["c_ws"][j], np.float32), (2, 0, 1)).reshape(128, 1024)
    out[0, 3072:4096] = np.asarray(inp["c_bs"][j], np.float32).reshape(1024)
    return out


WEIGHT_SHAPES = {
    "w_in_even": (2, D, EVEN_IN), "w_out_even": (2, D, D), "w_in_odd": (2, D, 2048), "w_out_odd": (2, D, D),
    "w_gate": (4, D, DFF), "w_up": (4, D, DFF), "w_down": (4, DFF, D), "w_ple": (4, PLE, D), "w_ple_gate": (4, D, D),
}


def mask_pack(P):
    idx = np.arange(128)
    same = (idx[:, None] // 64) == (idx[None, :] // 64)
    for d in range(2):
        before = (idx[:, None] < idx[None, :]) if d == 0 else (idx[:, None] > idx[None, :])
        strict = (same & before).astype(np.float32)
        incl = (same & (before | (idx[:, None] == idx[None, :]))).astype(np.float32)
        P.add("INCL%d" % d, incl)
        P.add("STRICT%d" % d, strict)
        P.add("AFTER%d" % d, strict.T.copy())
        P.add("NEGINCL%d" % d, np.where(incl.T > 0, 0.0, NEG).astype(np.float32))
    P.add("OFFD", 1.0 - np.eye(128, dtype=np.float32))
    P.add("nones", -np.ones((128, 128), np.float32))
    for cc in range(2):
        P.add("CMROW%d" % cc, np.broadcast_to(((idx // 64) == cc).astype(np.float32)[:, None], (128, 128)).copy())
    P.add("CHI", np.stack([(idx // 64) == 0, (idx // 64) == 1], 1).astype(np.float32))
    P.add("BLK", same.astype(np.float32))
    P.add("HSEL", np.stack([idx < 64, idx >= 64], 1).astype(np.float32))
    P.add("one1", np.ones((128, 1), np.float32))
    P.add("two1", np.full((128, 1), 2.0, np.float32))


EV = {}


def even_pack(inp, j):
    P = Pack()
    cw = np.asarray(inp["a_conv"][j], np.float32)
    P.add("cw", cw.reshape(5, 12, 128).transpose(2, 1, 0).reshape(128, 60))
    mu = np.asarray(inp["b_mu"][j], np.float32)
    P.add("mu_rkv", chunked(mu[0:1536]))
    P.add("mu_wd", mu[1536:1600].reshape(64, 1))
    P.add("mu_ad", mu[1600:1664].reshape(64, 1))
    P.add("mu_gd", mu[1664:1760].reshape(96, 1))
    P.add("k_k", chunked(inp["b_k_k"][j]))
    P.add("k_a", chunked(inp["b_k_a"][j]))
    P.add("r_k", chunked(np.asarray(inp["b_r_k"][j]).reshape(512)))
    P.add("a0", np.concatenate([chunked(inp["b_a0"][j][d]) for d in range(2)], 1))
    P.add("dtb", rowrep(np.asarray(inp["a_dt_bias"][j]).reshape(8)))
    P.add("alog", rowrep(np.asarray(inp["a_log"][j]).reshape(8)))
    P.add("anorm", rowrep(np.tile(np.asarray(inp["a_norm"][j], np.float32), 4)))
    P.add("lnw", rowrep(inp["b_ln_w"][j]))
    P.add("lnb", rowrep(inp["b_ln_b"][j]))
    P.add("w2s", np.asarray(inp["b_w2"][j], np.float32).reshape(64, 512))
    w0s = np.zeros((64, 512), np.float32)
    w0s[0] = inp["b_w0"][j][0]
    w0s[32] = inp["b_w0"][j][1]
    P.add("w0s", w0s)
    P.add("a2s", np.asarray(inp["b_a2"][j], np.float32).reshape(64, 512))
    P.add("g2", np.asarray(inp["b_g2"][j], np.float32))
    EV.update(P.off)
    arr = P.array()
    out = np.zeros((128, 4096), np.float32)
    out[:, : arr.shape[1]] = arr
    return out


def ev(lc, name):
    off, w = EV[name]
    return lc[:, off:off + w]


def even_in(K, C, A, W, S, j, li, t0):
    h, hb = A["h"]
    hn, hnb = A["hn"]
    sq, sqb = A["sq"]
    rmsnorm_fm(K, C, h, hb, C["nmix%d" % li], hn, hnb, sq, sqb)
    win = W["w_in_even"][j]
    cnt = [0]

    def store_to(dt, rowmap):
        def f(cid, M, ps, pb):
            tmp, tb = A["tmp"].next()
            cnt[0] += 1
            K.copy(tmp[0:M, 0:TS], ps[0:M, 0:TS], [pb], [tb], eng="act" if cnt[0] % 2 else "dve")
            r0 = rowmap(cid)
            K.dma_out(dt.ap[r0:r0 + M, t0:t0 + TS], tmp[0:M, 0:TS], tb, wr=dt.bufs(t0, t0 + TS))
        return f

    for b in range(3):
        chunks = [(b * 4 + c, c * 128, 128) for c in range(4)]
        linear_fm(K, C, win, KD, [(b * 512, 512, chunks)], hn, hnb, store_to(S["rawA"], lambda cid: cid * 128))
    for b in range(3):
        chunks = [(b * 4 + c, c * 128, 128) for c in range(4)]
        linear_fm(K, C, win, KD, [(A_COLS + b * 512, 512, chunks)], hn, hnb, store_to(S["rawB"], lambda cid: cid * 128))
    rows = {12: 1536, 13: 1600, 14: 1664}
    linear_fm(K, C, win, KD, [(A_COLS + 1536, 224, [(12, 0, 64), (13, 64, 64), (14, 128, 96)])], hn, hnb,
              store_to(S["rawB"], lambda cid: rows[cid]))
    wz, wzb = load_w(K, C, win, 0, KD, 1536, 528)
    for sub in range(TS // 128):
        zt, zb = A["tok"].next()
        ps, pb = K.psum()
        ps2, pb2 = K.psum()
        for k in range(KD):
            lhsT = hn[:, k * TS + sub * 128:k * TS + (sub + 1) * 128]
            K.mm(ps[:, 0:512], lhsT, wz[:, k, 0:512], k == 0, k == KD - 1, [hnb, wzb], [pb])
        for k in range(KD):
            lhsT = hn[:, k * TS + sub * 128:k * TS + (sub + 1) * 128]
            K.mm(ps2[:, 0:16], lhsT, wz[:, k, 512:528], k == 0, k == KD - 1, [hnb, wzb], [pb2])
        K.copy(zt[:, 0:512], ps[:, 0:512], [pb], [zb], eng="act")
        K.copy(zt[:, 512:528], ps2[:, 0:16], [pb2], [zb], eng="dve")
        tt0 = t0 + sub * 128
        K.dma_out(S["zt"].ap[tt0:tt0 + 128, :], zt[:, 0:512], zb, wr=S["zt"].bufs(tt0, tt0 + 128))
        K.dma_out(S["abt"].ap[tt0:tt0 + 128, :], zt[:, 512:528], zb, wr=S["abt"].bufs(tt0, tt0 + 128))


def even_consts(K, C, A, evp_ap):
    lc, lcb = A["lc"]
    K.dma_in(lc[:, 0:4096], evp_ap, lcb)
    dv, dvb = A["dv"]

    def half_one(src, np_, o):
        w = src.shape[1]
        K.ts(dv[0:np_, o:o + w], src[0:np_, :], 0.5, ALU.mult, [lcb], [dvb])
        K.ts(dv[0:np_, o + w:o + 2 * w], src[0:np_, :], -1.0, ALU.mult, [lcb], [dvb], s2=1.0, op1=ALU.add)

    half_one(ev(lc, "mu_rkv"), 128, 0)
    half_one(ev(lc, "mu_wd"), 64, 24)
    half_one(ev(lc, "mu_ad"), 64, 26)
    half_one(ev(lc, "mu_gd"), 96, 28)
    K.ts(dv[:, 30:34], ev(lc, "k_a"), -1.0, ALU.mult, [lcb], [dvb], s2=1.0, op1=ALU.add)
    K.ts(dv[:, 34:38], ev(lc, "r_k"), 0.5, ALU.mult, [lcb], [dvb])
    K.act(dv[:, 38:46], ev(lc, "alog"), AF.Exp, [lcb], [dvb])
    K.ts(dv[:, 38:46], dv[:, 38:46], -1.0, ALU.mult, [dvb], [dvb])


def prep_A(K, C, A, G, S, t0, T):
    lc, lcb = A["lc"]
    dv, dvb = A["dv"]
    cw = ev(lc, "cw")
    xin, xb = G["xin"]
    WD = TS + 4
    x3 = xin[:, 0:12 * WD].rearrange("p (c t) -> p c t", c=12)
    lo = max(t0 - 2, 0)
    hi = min(t0 + TS + 2, T)
    if lo > t0 - 2:
        K.memset(x3[:, :, 0:2], 0.0, [xb])
    if hi < t0 + TS + 2:
        K.memset(x3[:, :, TS + 2:TS + 4], 0.0, [xb])
    src = S["rawA"].ap[:, lo:hi].rearrange("(c p) t -> p c t", p=128)
    K.dma_in(x3[:, :, lo - (t0 - 2):hi - (t0 - 2)], src, xb, rd=S["rawA"].bufs(lo, hi))
    kfm, kfb = G["kfm"]
    vfm, vfb = G["vfm"]
    for c in range(12):
        acc, accb = G["acc"].next()
        K.ts(acc[:, 0:TS], x3[:, c, 0:TS], cw[:, c * 5:c * 5 + 1], ALU.mult, [xb, lcb], [accb])
        for k in range(1, 5):
            K.stt(acc[:, 0:TS], x3[:, c, k:k + TS], cw[:, c * 5 + k:c * 5 + k + 1], acc[:, 0:TS], ALU.mult, ALU.add, [xb, lcb, accb], [accb])
        if c >= 8:
            K.act(vfm[:, (c - 8) * TS:(c - 7) * TS], acc[:, 0:TS], AF.Silu, [accb], [vfb])
            continue
        sl, slb = G["acc"].next()
        K.act(sl[:, 0:TS], acc[:, 0:TS], AF.Silu, [accb], [slb])
        sq, sqb = G["sqr"].next()
        K.act(sq[:, 0:TS], sl[:, 0:TS], AF.Square, [slb], [sqb])
        ps, pb = K.psum()
        K.mm(ps[:, 0:TS], C["ones_r"], sq[:, 0:TS], True, True, [sqb, C["buf"]], [pb])
        rs, rb = C["rstd"].next()
        K.act(rs[:, 0:TS], ps[:, 0:TS], AF.Sqrt, [pb, C["buf"]], [rb], bias=C["eps6"], scale=1.0)
        K.recip(rs[:, 0:TS], rs[:, 0:TS], [rb], [rb])
        if c < 4:
            qo, qob = G["acc"].next()
            K.stt(qo[:, 0:TS], sl[:, 0:TS], 128.0 ** -0.5, rs[:, 0:TS], ALU.mult, ALU.mult, [slb, rb], [qob])
            K.dma_out(S["qT"].ap[c * 128:(c + 1) * 128, t0:t0 + TS], qo[:, 0:TS], qob, wr=S["qT"].bufs(t0, t0 + TS))
        else:
            hh = c - 4
            K.tt(kfm[:, hh * TS:(hh + 1) * TS], sl[:, 0:TS], rs[:, 0:TS], ALU.mult, [slb, rb], [kfb])
    K.dma_out(S["kT"].ap[:, t0:t0 + TS].rearrange("(c p) t -> p c t", p=128), kfm[:, 0:4 * TS].rearrange("p (c t) -> p c t", c=4), kfb,
              wr=S["kT"].bufs(t0, t0 + TS))
    for sub in range(TS // 128):
        tt0 = t0 + sub * 128
        for (src_t, src_b, dst) in ((kfm, kfb, S["ktok"]), (vfm, vfb, S["vtok"])):
            ps, pb = K.psum()
            for hh in range(4):
                K.mm(ps[:, hh * 128:(hh + 1) * 128], src_t[:, hh * TS + sub * 128:hh * TS + (sub + 1) * 128], C["ident"], True, True,
                     [src_b, C["buf"]], [pb])
            tk, tkb = G["tok"].next()
            K.copy(tk[:, 0:512], ps[:, 0:512], [pb], [tkb], eng="act")
            K.dma_out(dst.ap[tt0:tt0 + 128, :], tk[:, 0:512], tkb, wr=dst.bufs(tt0, tt0 + 128))
        ab, abb = G["gt"].next()
        K.dma_in(ab[:, 0:16], S["abt"].ap[tt0:tt0 + 128, :], abb, rd=S["abt"].bufs(tt0, tt0 + 128))
        g, gb_ = G["gt"].next()
        w_ = g[:, 16:64]
        K.tt(w_[:, 0:8], ab[:, 0:8], ev(lc, "dtb"), ALU.add, [abb, lcb], [gb_])
        K.ts(w_[:, 8:16], w_[:, 0:8], 0.0, ALU.max, [gb_], [gb_])
        K.act(w_[:, 16:24], w_[:, 0:8], AF.Abs, [gb_], [gb_])
        K.act(w_[:, 16:24], w_[:, 16:24], AF.Exp, [gb_], [gb_], scale=-1.0)
        K.ts(w_[:, 24:32], w_[:, 16:24], 2.0, ALU.add, [gb_], [gb_])
        K.recip(w_[:, 24:32], w_[:, 24:32], [gb_], [gb_])
        K.tt(w_[:, 24:32], w_[:, 24:32], w_[:, 16:24], ALU.mult, [gb_], [gb_])
        K.tt(w_[:, 32:40], w_[:, 24:32], w_[:, 24:32], ALU.mult, [gb_], [gb_])
        K.ts(w_[:, 40:48], w_[:, 32:40], 1.0 / 9, ALU.mult, [gb_], [gb_], s2=1.0 / 7, op1=ALU.add)
        for coef in (1.0 / 5, 1.0 / 3, 1.0):
            K.tt(w_[:, 40:48], w_[:, 40:48], w_[:, 32:40], ALU.mult, [gb_], [gb_])
            K.ts(w_[:, 40:48], w_[:, 40:48], coef, ALU.add, [gb_], [gb_])
        K.tt(w_[:, 40:48], w_[:, 40:48], w_[:, 24:32], ALU.mult, [gb_], [gb_])
        K.stt(w_[:, 40:48], w_[:, 40:48], 2.0, w_[:, 8:16], ALU.mult, ALU.add, [gb_], [gb_])
        K.tt(g[:, 0:8], w_[:, 40:48], dv[:, 38:46], ALU.mult, [gb_, dvb], [gb_])
        K.act(g[:, 8:16], ab[:, 8:16], AF.Sigmoid, [abb], [gb_])
        K.dma_out(S["gbt"].ap[tt0:tt0 + 128, :], g[:, 0:16], gb_, wr=S["gbt"].bufs(tt0, tt0 + 128))


def alloc_prepA(K):
    G = {}
    G["xin"] = (K.sb("pa_xin", [128, 12 * (TS + 4)]), Buf("pa_xin"))
    G["kfm"] = (K.sb("pa_kfm", [128, 4 * TS]), Buf("pa_kfm"))
    G["vfm"] = (K.sb("pa_vfm", [128, 4 * TS]), Buf("pa_vfm"))
    G["acc"] = Ring(K, "pa_acc", [128, TS], F32, 4)
    G["sqr"] = Ring(K, "pa_sq", [128, TS], F32R, 2)
    G["tok"] = Ring(K, "pa_tok", [128, 512], F32, 3)
    G["gt"] = Ring(K, "pa_gt", [128, 64], F32, 4)
    return G


def alloc_scanA(K):
    G = {}
    for nm in ("qT", "kT", "ktok", "vtok", "of", "zt"):
        G[nm] = Ring(K, "sa_" + nm, [128, 512], F32, 2)
    G["gb"] = Ring(K, "sa_gb", [128, 16], F32, 2)
    G["sm"] = Ring(K, "sa_sm", [128, 64], F32, 2)
    for nm in ("GM", "Dm", "egr", "QT", "tmpD", "vb", "kbg", "us", "wTs", "qkm", "qkTs", "qdT", "vnew", "S", "ot", "sqt", "yts", "kd0", "kd1"):
        G[nm] = (K.sb("sa_" + nm, [128, 512]), Buf("sa_" + nm))
    G["QX"] = (K.sb("sa_QX", [128, 1024]), Buf("sa_QX"))
    return G


def v3(ap, n=4):
    return ap.rearrange("p (h t) -> p h t", h=n)


def bc_last(ap, n=128):
    return ap.unsqueeze(2).to_broadcast([ap.shape[0], ap.shape[1], n])


def bc_mid(ap, k=4):
    return ap.unsqueeze(1).to_broadcast([ap.shape[0], k, ap.shape[1]])


def scan_A(K, C, A, G, S, d, T):
    lc, lcb = A["lc"]
    NT = T // 128
    cb = C["buf"]
    Sst, Sb = G["S"]
    K.memset(Sst[:, 0:512], 0.0, [Sb])
    INCL, AFTER, NEGINCL = C["INCL%d" % d], C["AFTER%d" % d], C["NEGINCL%d" % d]
    order = range(NT) if d == 0 else range(NT - 1, -1, -1)
    corder = (0, 1) if d == 0 else (1, 0)
    for ti in order:
        t0 = ti * 128
        qT, qTb = G["qT"].next()
        kT, kTb = G["kT"].next()
        ktok, ktb = G["ktok"].next()
        vtok, vtb = G["vtok"].next()
        gb, gbb = G["gb"].next()
        K.dma_in(v3(qT[:, 0:512]), S["qT"].ap[:, t0:t0 + 128].rearrange("(h p) t -> p h t", p=128), qTb, rd=S["qT"].bufs(t0, t0 + 128))
        K.dma_in(v3(kT[:, 0:512]), S["kT"].ap[:, t0:t0 + 128].rearrange("(h p) t -> p h t", p=128), kTb, rd=S["kT"].bufs(t0, t0 + 128))
        K.dma_in(ktok[:, 0:512], S["ktok"].ap[t0:t0 + 128, :], ktb, rd=S["ktok"].bufs(t0, t0 + 128))
        K.dma_in(vtok[:, 0:512], S["vtok"].ap[t0:t0 + 128, :], vtb, rd=S["vtok"].bufs(t0, t0 + 128))
        K.dma_in(gb[:, 0:16], S["gbt"].ap[t0:t0 + 128, :], gbb, rd=S["gbt"].bufs(t0, t0 + 128))
        g = gb[:, d * 4:d * 4 + 4]
        beta = gb[:, 8 + d * 4:12 + d * 4]
        sm, smb = G["sm"].next()
        ps1, pb1 = K.psum()
        K.mm(ps1[:, 0:4], INCL, g, True, True, [cb, gbb], [pb1])
        K.mm(ps1[:, 4:8], AFTER, g, True, True, [cb, gbb], [pb1])
        K.mm(ps1[:, 8:12], C["CMROW0"], g, True, True, [cb, gbb], [pb1])
        K.mm(ps1[:, 12:16], C["CMROW1"], g, True, True, [cb, gbb], [pb1])
        K.act(sm[:, 0:16], ps1[:, 0:16], AF.Exp, [pb1], [smb])
        egc, egrev = sm[:, 0:4], sm[:, 4:8]
        K.tt(sm[:, 16:20], beta, egc, ALU.mult, [gbb, smb], [smb])
        K.ts(sm[:, 20:24], beta, -1.0, ALU.mult, [gbb], [smb])
        K.tt(sm[:, 24:32].rearrange("p (c h) -> p c h", c=2), bc_mid(egrev, 2), bc_last(C["CHI"], 4), ALU.mult, [smb, cb], [smb])
        GM, GMb = G["GM"]
        K.tt(v3(GM[:, 0:512]), bc_mid(INCL), bc_last(g), ALU.mult, [cb, gbb], [GMb])
        pd, pdb = K.psum()
        pg, pgb = K.psum()
        for h in range(4):
            hs = slice(h * 128, (h + 1) * 128)
            K.mm(pd[:, hs], GM[:, hs], C["ones"], True, False, [GMb, cb], [pdb])
            K.mm(pd[:, hs], C["nones"], GM[:, hs], False, False, [GMb, cb], [pdb])
            K.mm(pd[:, hs], C["ident"], NEGINCL, False, True, [cb], [pdb])
            K.mm(pg[:, hs], C["ones"], GM[:, hs], True, True, [GMb, cb], [pgb])
        Dm, Dmb = G["Dm"]
        egr, egrb = G["egr"]
        K.act(Dm[:, 0:512], pd[:, 0:512], AF.Exp, [pdb], [Dmb])
        K.act(egr[:, 0:512], pg[:, 0:512], AF.Exp, [pgb], [egrb])
        pk, pkb = K.psum()
        for h in range(4):
            hs = slice(h * 128, (h + 1) * 128)
            K.mm(pk[:, hs], kT[:, hs], kT[:, hs], True, True, [kTb], [pkb])
        tmpD, tDb = G["tmpD"]
        K.tt(v3(tmpD[:, 0:512]), v3(Dm[:, 0:512]), bc_mid(C["OFFD"]), ALU.mult, [Dmb, cb], [tDb])
        K.tt(v3(tmpD[:, 0:512]), v3(tmpD[:, 0:512]), bc_last(sm[:, 20:24]), ALU.mult, [tDb, smb], [tDb])
        QT, QTb = G["QT"]
        QX, QXb = G["QX"]
        qx3 = QX[:, 0:1024].rearrange("p (h t) -> p h t", h=4)
        K.tt(QT[:, 0:512], pk[:, 0:512], tmpD[:, 0:512], ALU.mult, [pkb, tDb], [QTb])
        pq0, pq0b = K.psum()
        for h in range(4):
            hs = slice(h * 128, (h + 1) * 128)
            K.mm(pq0[:, hs], QT[:, hs], C["ident"], True, True, [QTb, cb], [pq0b])
        K.copy(qx3[:, :, 0:128], v3(pq0[:, 0:512]), [pq0b], [QXb], eng="act")
        K.copy(qx3[:, :, 128:256], bc_mid(C["ident"]), [cb], [QXb], eng="pool")
        for k in range(6):
            lastk = k == 5
            pf = [K.psum(), K.psum()]
            for h in range(4):
                pft, pfb = pf[h // 2]
                o0 = (h % 2) * 256
                if lastk:
                    K.mm(pft[:, o0 + 128:o0 + 256], QT[:, h * 128:(h + 1) * 128], qx3[:, h, 128:256], True, True, [QTb, QXb], [pfb])
                else:
                    K.mm(pft[:, o0:o0 + 256], QT[:, h * 128:(h + 1) * 128], qx3[:, h, :], True, True, [QTb, QXb], [pfb])
            if not lastk:
                ptq, ptqb = K.psum()
                for h in range(4):
                    hs = slice(h * 128, (h + 1) * 128)
                    K.mm(ptq[:, hs], qx3[:, h, 0:128], QT[:, hs], True, True, [QTb, QXb], [ptqb])
            for hp in range(2):
                pft, pfb = pf[hp]
                pf3 = pft[:, 0:512].rearrange("p (h t) -> p h t", h=2)
                if not lastk:
                    K.copy(qx3[:, 2 * hp:2 * hp + 2, 0:128], pf3[:, :, 0:128], [pfb], [QXb], eng="act")
                K.tt(qx3[:, 2 * hp:2 * hp + 2, 128:256], qx3[:, 2 * hp:2 * hp + 2, 128:256], pf3[:, :, 128:256], ALU.add, [pfb, QXb], [QXb])
            if not lastk:
                K.copy(QT[:, 0:512], ptq[:, 0:512], [ptqb], [QTb], eng="act")
        vb, vbb = G["vb"]
        kbg, kbgb = G["kbg"]
        K.tt(v3(vb[:, 0:512]), v3(vtok[:, 0:512]), bc_last(beta), ALU.mult, [vtb, gbb], [vbb])
        K.tt(v3(kbg[:, 0:512]), v3(ktok[:, 0:512]), bc_last(sm[:, 16:20]), ALU.mult, [ktb, smb], [kbgb])
        pu, pub = K.psum()
        pw, pwb = K.psum()
        pqk, pqkb = K.psum()
        for h in range(4):
            hs = slice(h * 128, (h + 1) * 128)
            K.mm(pu[:, hs], qx3[:, h, 128:256], vb[:, hs], True, True, [QXb, vbb], [pub])
            K.mm(pw[:, hs], kbg[:, hs], qx3[:, h, 128:256], True, True, [QXb, kbgb], [pwb])
            K.mm(pqk[:, hs], qT[:, hs], kT[:, hs], True, True, [qTb, kTb], [pqkb])
        us, usb = G["us"]
        wTs, wTb = G["wTs"]
        qkm, qkmb = G["qkm"]
        K.copy(us[:, 0:512], pu[:, 0:512], [pub], [usb], eng="act")
        K.copy(wTs[:, 0:512], pw[:, 0:512], [pwb], [wTb], eng="act")
        K.tt(qkm[:, 0:512], pqk[:, 0:512], Dm[:, 0:512], ALU.mult, [pqkb, Dmb], [qkmb])
        pqt, pqtb = K.psum()
        for h in range(4):
            hs = slice(h * 128, (h + 1) * 128)
            K.mm(pqt[:, hs], qkm[:, hs], C["ident"], True, True, [qkmb, cb], [pqtb])
        qkTs, qkTb = G["qkTs"]
        K.copy(qkTs[:, 0:512], pqt[:, 0:512], [pqtb], [qkTb], eng="act")
        qdT, qdTb = G["qdT"]
        K.tt(qdT[:, 0:512], qT[:, 0:512], egr[:, 0:512], ALU.mult, [qTb, egrb], [qdTb], eng="pool")
        kd = [G["kd0"], G["kd1"]]
        for cc in range(2):
            kdt, kdb = kd[cc]
            K.tt(v3(kdt[:, 0:512]), v3(ktok[:, 0:512]), bc_last(sm[:, 24 + cc * 4:28 + cc * 4]), ALU.mult, [ktb, smb], [kdb], eng="pool")
        vnew, vnb = G["vnew"]
        ot, otb = G["ot"]
        for cc in corder:
            kdt, kdb = kd[cc]
            pws, pwsb = K.psum()
            for h in range(4):
                hs = slice(h * 128, (h + 1) * 128)
                K.mm(pws[:, hs], wTs[:, hs], Sst[:, hs], True, True, [wTb, Sb], [pwsb])
            K.tt(vnew[:, 0:512], us[:, 0:512], pws[:, 0:512], ALU.subtract, [usb, pwsb], [vnb])
            po, pob = K.psum()
            pss, pssb = K.psum()
            for h in range(4):
                hs = slice(h * 128, (h + 1) * 128)
                K.mm(po[:, hs], qdT[:, hs], Sst[:, hs], True, False, [qdTb, Sb], [pob])
                K.mm(po[:, hs], qkTs[:, hs], vnew[:, hs], False, True, [qkTb, vnb], [pob])
                K.mm(pss[:, hs], kdt[:, hs], vnew[:, hs], True, True, [kdb, vnb], [pssb])
            rows = slice(cc * 64, (cc + 1) * 64)
            K.copy(ot[rows, 0:512], po[rows, 0:512], [pob], [otb], eng="act")
            K.tt(v3(Sst[:, 0:512]), v3(Sst[:, 0:512]), bc_last(sm[:, 8 + cc * 4:12 + cc * 4]), ALU.mult, [Sb, smb], [Sb])
            K.tt(Sst[:, 0:512], Sst[:, 0:512], pss[:, 0:512], ALU.add, [Sb, pssb], [Sb])
        if d == 0:
            K.dma_out(S["oAf"].ap[t0:t0 + 128, :], ot[:, 0:512], otb, wr=S["oAf"].bufs(t0, t0 + 128))
        else:
            of, ofb = G["of"].next()
            zt, ztb = G["zt"].next()
            K.dma_in(of[:, 0:512], S["oAf"].ap[t0:t0 + 128, :], ofb, rd=S["oAf"].bufs(t0, t0 + 128))
            K.dma_in(zt[:, 0:512], S["zt"].ap[t0:t0 + 128, :], ztb, rd=S["zt"].bufs(t0, t0 + 128))
            K.tt(ot[:, 0:512], ot[:, 0:512], of[:, 0:512], ALU.add, [otb, ofb], [otb])
            sqt, sqtb = G["sqt"]
            K.act(sqt[:, 0:512], ot[:, 0:512], AF.Square, [otb], [sqtb])
            sm2, sm2b = G["sm"].next()
            K.P.add("dve", lambda e, o_=sm2[:, 0:4], i_=v3(sqt[:, 0:512]): e.tensor_reduce(out=o_, in_=i_, axis=AX.X, op=ALU.add),
                    reads=[sqtb], writes=[sm2b])
            K.act(sm2[:, 0:4], sm2[:, 0:4], AF.Sqrt, [sm2b, cb], [sm2b], bias=C["eps6"], scale=1.0 / 128)
            K.recip(sm2[:, 0:4], sm2[:, 0:4], [sm2b], [sm2b])
            K.tt(v3(ot[:, 0:512]), v3(ot[:, 0:512]), bc_last(sm2[:, 0:4]), ALU.mult, [otb, sm2b], [otb])
            K.tt(ot[:, 0:512], ot[:, 0:512], ev(lc, "anorm"), ALU.mult, [otb, lcb], [otb])
            K.act(sqt[:, 0:512], zt[:, 0:512], AF.Silu, [ztb], [sqtb])
            K.tt(ot[:, 0:512], ot[:, 0:512], sqt[:, 0:512], ALU.mult, [otb, sqtb], [otb])
            py, pyb = K.psum()
            for h in range(4):
                hs = slice(h * 128, (h + 1) * 128)
                K.mm(py[:, hs], ot[:, hs], C["ident"], True, True, [otb, cb], [pyb])
            yts, ytsb = G["yts"]
            K.copy(yts[:, 0:512], py[:, 0:512], [pyb], [ytsb], eng="act")
            K.dma_out(S["yT"].ap[0:512, t0:t0 + 128].rearrange("(h p) t -> p h t", p=128), v3(yts[:, 0:512]), ytsb,
                      wr=S["yT"].bufs(t0, t0 + 128))


SCALE_W = 0.606531


def alloc_prepB(K):
    G = {}
    G["xin"] = (K.sb("pb_xin", [128, 12 * (TS + 2)]), Buf("pb_xin"))
    G["xs"] = (K.sb("pb_xs", [128, 3 * (TS + 2)]), Buf("pb_xs"))
    G["rkv"] = (K.sb("pb_rkv", [128, 12 * TS]), Buf("pb_rkv"))
    G["sml"] = (K.sb("pb_sml", [128, 3 * TS]), Buf("pb_sml"))
    G["kk"] = (K.sb("pb_kk", [128, 4 * TS]), Buf("pb_kk"))
    for d in range(2):
        G["bT%d" % d] = (K.sb("pb_bT%d" % d, [128, 4 * TS]), Buf("pb_bT%d" % d))
        G["kdT%d" % d] = (K.sb("pb_kdT%d" % d, [128, 4 * TS]), Buf("pb_kdT%d" % d))
    G["prod"] = (K.sb("pb_prod", [128, 4 * TS]), Buf("pb_prod"))
    G["tmp"] = Ring(K, "pb_tmp", [128, TS], F32, 4)
    G["sqr"] = Ring(K, "pb_sq", [128, TS], F32R, 2)
    G["tok"] = Ring(K, "pb_tok", [128, 512], F32, 4)
    G["t8"] = Ring(K, "pb_t8", [128, 8], F32, 2)
    return G


def prep_B(K, C, A, G, S, t0, T):
    lc, lcb = A["lc"]
    dv, dvb = A["dv"]
    cb = C["buf"]
    WD = TS + 2
    xin, xb = G["xin"]
    xs, xsb = G["xs"]
    x3 = xin[:, 0:12 * WD].rearrange("p (c t) -> p c t", c=12)
    s3 = xs[:, 0:3 * WD].rearrange("p (c t) -> p c t", c=3)
    lo = max(t0 - 1, 0)
    hi = min(t0 + TS + 1, T)
    if lo > t0 - 1:
        K.memset(x3[:, :, 0:1], 0.0, [xb])
        K.memset(s3[:, :, 0:1], 0.0, [xsb])
    if hi < t0 + TS + 1:
        K.memset(x3[:, :, TS + 1:TS + 2], 0.0, [xb])
        K.memset(s3[:, :, TS + 1:TS + 2], 0.0, [xsb])
    o0, o1 = lo - (t0 - 1), hi - (t0 - 1)
    rd = S["rawB"].bufs(lo, hi)
    K.dma_in(x3[:, :, o0:o1], S["rawB"].ap[0:1536, lo:hi].rearrange("(c p) t -> p c t", p=128), xb, rd=rd)
    K.dma_in(s3[0:64, 0, o0:o1], S["rawB"].ap[1536:1600, lo:hi], xsb, rd=rd)
    K.dma_in(s3[0:64, 1, o0:o1], S["rawB"].ap[1600:1664, lo:hi], xsb, rd=rd)
    K.dma_in(s3[0:96, 2, o0:o1], S["rawB"].ap[1664:1760, lo:hi], xsb, rd=rd)
    rkv, rkvb = G["rkv"]
    sml, smlb = G["sml"]

    def lerp(dst, dstb, src3, c, np_, hmu, omu, srcb):
        tmp, tb = G["tmp"].next()
        K.tt(tmp[0:np_, 0:TS], src3[0:np_, c, 0:TS], src3[0:np_, c, 2:TS + 2], ALU.add, [srcb], [tb], eng="pool")
        K.ts(tmp[0:np_, 0:TS], tmp[0:np_, 0:TS], hmu, ALU.mult, [tb, dvb], [tb])
        K.stt(dst, src3[0:np_, c, 1:TS + 1], omu, tmp[0:np_, 0:TS], ALU.mult, ALU.add, [srcb, tb, dvb], [dstb])

    for c in range(12):
        lerp(rkv[:, c * TS:(c + 1) * TS], rkvb, x3, c, 128, dv[:, c:c + 1], dv[:, 12 + c:13 + c], xb)
    lerp(sml[0:64, 0:TS], smlb, s3, 0, 64, dv[0:64, 24:25], dv[0:64, 25:26], xsb)
    lerp(sml[0:64, TS:2 * TS], smlb, s3, 1, 64, dv[0:64, 26:27], dv[0:64, 27:28], xsb)
    lerp(sml[0:96, 2 * TS:3 * TS], smlb, s3, 2, 96, dv[0:96, 28:29], dv[0:96, 29:30], xsb)
    K.act(sml[0:64, 0:TS], sml[0:64, 0:TS], AF.Tanh, [smlb], [smlb])
    K.act(sml[0:96, 2 * TS:3 * TS], sml[0:96, 2 * TS:3 * TS], AF.Sigmoid, [smlb], [smlb])
    twd = sml[:, 0:TS]
    adl = sml[:, TS:2 * TS]
    sgd = sml[:, 2 * TS:3 * TS]
    rT = rkv[:, 0:4 * TS]
    kT = rkv[:, 4 * TS:8 * TS]
    vT = rkv[:, 8 * TS:12 * TS]
    K.dma_out(S["rTB"].ap[:, t0:t0 + TS].rearrange("(c p) t -> p c t", p=128), rT.rearrange("p (c t) -> p c t", c=4), rkvb,
              wr=S["rTB"].bufs(t0, t0 + TS))
    kk, kkb = G["kk"]
    for c in range(4):
        t1, t1b = G["tmp"].next()
        K.ts(t1[:, 0:TS], kT[:, c * TS:(c + 1) * TS], ev(lc, "k_k")[:, c:c + 1], ALU.mult, [rkvb, lcb], [t1b])
        sq, sqb = G["sqr"].next()
        K.act(sq[:, 0:TS], t1[:, 0:TS], AF.Square, [t1b], [sqb])
        ps, pb = K.psum()
        K.mm(ps[:, 0:TS], C["blk_r"], sq[:, 0:TS], True, True, [sqb, cb], [pb])
        rs, rb = C["rstd"].next()
        K.act(rs[:, 0:TS], ps[:, 0:TS], AF.Sqrt, [pb, cb], [rb], bias=C["eps6"], scale=1.0)
        K.recip(rs[:, 0:TS], rs[:, 0:TS], [rb], [rb])
        K.tt(kk[:, c * TS:(c + 1) * TS], t1[:, 0:TS], rs[:, 0:TS], ALU.mult, [t1b, rb], [kkb])
    K.dma_out(S["kkT"].ap[:, t0:t0 + TS].rearrange("(c p) t -> p c t", p=128), kk[:, 0:4 * TS].rearrange("p (c t) -> p c t", c=4), kkb,
              wr=S["kkT"].bufs(t0, t0 + TS))
    a2s, a0, k_a = ev(lc, "a2s"), ev(lc, "a0"), ev(lc, "k_a")
    for d in range(2):
        bT, bTb = G["bT%d" % d]
        kdT, kdTb = G["kdT%d" % d]
        pr = slice(32 * d, 32 * d + 32)
        for c in range(4):
            ps, pb = K.psum()
            K.mm(ps[:, 0:TS], a2s[pr, c * 128:(c + 1) * 128], adl[pr, 0:TS], True, True, [lcb, smlb], [pb])
            al, alb = G["tmp"].next()
            K.act(al[:, 0:TS], ps[:, 0:TS], AF.Sigmoid, [pb, lcb], [alb], bias=a0[:, d * 4 + c:d * 4 + c + 1], scale=1.0)
            K.tt(bT[:, c * TS:(c + 1) * TS], kk[:, c * TS:(c + 1) * TS], al[:, 0:TS], ALU.mult, [kkb, alb], [bTb])
            K.ts(al[:, 0:TS], al[:, 0:TS], k_a[:, c:c + 1], ALU.mult, [alb, lcb, dvb], [alb], s2=dv[:, 30 + c:31 + c], op1=ALU.add)
            K.tt(kdT[:, c * TS:(c + 1) * TS], kT[:, c * TS:(c + 1) * TS], al[:, 0:TS], ALU.mult, [rkvb, alb], [kdTb])
        K.dma_out(S["bT%d" % d].ap[:, t0:t0 + TS].rearrange("(c p) t -> p c t", p=128), bT[:, 0:4 * TS].rearrange("p (c t) -> p c t", c=4), bTb,
                  wr=S["bT%d" % d].bufs(t0, t0 + TS))
        K.dma_out(S["kdT%d" % d].ap[:, t0:t0 + TS].rearrange("(c p) t -> p c t", p=128), kdT[:, 0:4 * TS].rearrange("p (c t) -> p c t", c=4), kdTb,
                  wr=S["kdT%d" % d].bufs(t0, t0 + TS))
    prod, prb = G["prod"]
    kd0, kd0b = G["kdT0"]
    kd1, kd1b = G["kdT1"]
    for c in range(4):
        cs = slice(c * TS, (c + 1) * TS)
        K.tt(prod[:, cs], kd0[:, cs], kd1[:, cs], ALU.add, [kd0b, kd1b], [prb], eng="pool")
        K.stt(prod[:, cs], prod[:, cs], dv[:, 34 + c:35 + c], rT[:, cs], ALU.mult, ALU.mult, [prb, dvb, rkvb], [prb])
    w2s, w0s, g2 = ev(lc, "w2s"), ev(lc, "w0s"), ev(lc, "g2")
    for sub in range(TS // 128):
        tt0 = t0 + sub * 128
        cols = slice(sub * 128, (sub + 1) * 128)
        trans = [(vT, rkvb, S["vtokB"])]
        for d in range(2):
            trans.append((G["bT%d" % d][0], G["bT%d" % d][1], S["btok%d" % d]))
            trans.append((G["kdT%d" % d][0], G["kdT%d" % d][1], S["kdtok%d" % d]))
        for i_, (src_t, src_b, dst) in enumerate(trans):
            ps, pb = K.psum()
            for c in range(4):
                K.mm(ps[:, c * 128:(c + 1) * 128], src_t[:, c * TS + sub * 128:c * TS + (sub + 1) * 128], C["ident"], True, True, [src_b, cb], [pb])
            tk, tkb = G["tok"].next()
            K.copy(tk[:, 0:512], ps[:, 0:512], [pb], [tkb], eng="act" if i_ % 2 else "dve")
            K.dma_out(dst.ap[tt0:tt0 + 128, :], tk[:, 0:512], tkb, wr=dst.bufs(tt0, tt0 + 128))
        for d in range(2):
            pr = slice(32 * d, 32 * d + 32)
            ps, pb = K.psum()
            K.mm(ps[:, 0:512], twd[pr, cols], w2s[pr, :], True, False, [smlb, lcb], [pb])
            K.mm(ps[:, 0:512], C["ones"][32 * d:32 * d + 1, :], w0s[32 * d:32 * d + 1, :], False, True, [cb, lcb], [pb])
            tk, tkb = G["tok"].next()
            K.act(tk[:, 0:512], ps[:, 0:512], AF.Sigmoid, [pb], [tkb])
            K.dma_out(S["sg%d" % d].ap[tt0:tt0 + 128, :], tk[:, 0:512], tkb, wr=S["sg%d" % d].bufs(tt0, tt0 + 128))
        ps, pb = K.psum()
        K.mm(ps[:, 0:512], sgd[0:96, cols], g2[0:96, :], True, True, [smlb, lcb], [pb])
        tk, tkb = G["tok"].next()
        K.copy(tk[:, 0:512], ps[:, 0:512], [pb], [tkb], eng="act")
        K.dma_out(S["gtok"].ap[tt0:tt0 + 128, :], tk[:, 0:512], tkb, wr=S["gtok"].bufs(tt0, tt0 + 128))
        ps, pb = K.psum()
        for c in range(4):
            K.mm(ps[:, 2 * c:2 * c + 2], prod[:, c * TS + sub * 128:c * TS + (sub + 1) * 128], C["HSEL"], True, True, [prb, cb], [pb])
        t8, t8b = G["t8"].next()
        K.copy(t8[:, 0:8], ps[:, 0:8], [pb], [t8b], eng="dve")
        K.dma_out(S["bons"].ap[tt0:tt0 + 128, :], t8[:, 0:8], t8b, wr=S["bons"].bufs(tt0, tt0 + 128))

import os as _os
DBG_LEVEL = int(_os.environ.get('SCANB_LEVEL', '9'))


def alloc_scanB(K):
    G = {}
    for nm in ("sg", "btok", "kdtok", "vtok", "yf", "gt"):
        G[nm] = Ring(K, "sb_" + nm, [128, 512], F32, 2)
    for nm in ("rT", "kkT", "bT", "kdT"):
        G[nm] = Ring(K, "sb_" + nm, [128, 1024], F32, 2)
    for nm in ("Ei", "Ev", "Ex", "rt", "at", "bt", "kt"):
        G[nm] = (K.sb("sb_" + nm, [128, 1024]), Buf("sb_" + nm))
    G["bon"] = Ring(K, "sb_bon", [128, 8], F32, 2)
    G["sm"] = Ring(K, "sb_sm", [128, 64], F32, 2)
    for nm in ("Er", "bg", "kg", "bgm", "kgm", "Xs", "Us", "S", "yt", "sqt", "yts",
               "QTa", "QTb", "AakA", "AakB", "ArbA", "ArbB", "ArkA", "ArkB"):
        G[nm] = (K.sb("sb_" + nm, [128, 512]), Buf("sb_" + nm))
    G["QXa"] = (K.sb("sb_QXa", [128, 1024]), Buf("sb_QXa"))
    G["QXb"] = (K.sb("sb_QXb", [128, 1024]), Buf("sb_QXb"))
    return G


def invert4(K, C, QT, QTb, QX, QXb):
    cb = C["buf"]
    qx3 = QX[:, 0:1024].rearrange("p (h t) -> p h t", h=4)
    pq0, pq0b = K.psum()
    for h in range(4):
        hs = slice(h * 128, (h + 1) * 128)
        K.mm(pq0[:, hs], QT[:, hs], C["ident"], True, True, [QTb, cb], [pq0b])
    K.copy(qx3[:, :, 0:128], v3(pq0[:, 0:512]), [pq0b], [QXb], eng="act")
    K.copy(qx3[:, :, 128:256], bc_mid(C["ident"]), [cb], [QXb], eng="pool")
    for k in range(6):
        lastk = k == 5
        pf = [K.psum(), K.psum()]
        for h in range(4):
            pft, pfb = pf[h // 2]
            o0 = (h % 2) * 256
            if lastk:
                K.mm(pft[:, o0 + 128:o0 + 256], QT[:, h * 128:(h + 1) * 128], qx3[:, h, 128:256], True, True, [QTb, QXb], [pfb])
            else:
                K.mm(pft[:, o0:o0 + 256], QT[:, h * 128:(h + 1) * 128], qx3[:, h, :], True, True, [QTb, QXb], [pfb])
        if not lastk:
            ptq, ptqb = K.psum()
            for h in range(4):
                hs = slice(h * 128, (h + 1) * 128)
                K.mm(ptq[:, hs], qx3[:, h, 0:128], QT[:, hs], True, True, [QTb, QXb], [ptqb])
        for hp in range(2):
            pft, pfb = pf[hp]
            pf3 = pft[:, 0:512].rearrange("p (h t) -> p h t", h=2)
            if not lastk:
                K.copy(qx3[:, 2 * hp:2 * hp + 2, 0:128], pf3[:, :, 0:128], [pfb], [QXb], eng="act")
            K.tt(qx3[:, 2 * hp:2 * hp + 2, 128:256], qx3[:, 2 * hp:2 * hp + 2, 128:256], pf3[:, :, 128:256], ALU.add, [pfb, QXb], [QXb])
        if not lastk:
            K.copy(QT[:, 0:512], ptq[:, 0:512], [ptqb], [QTb], eng="act")
    return qx3


def scan_B(K, C, A, G, S, d, T):
    lc, lcb = A["lc"]
    NT = T // 128
    cb = C["buf"]
    Sst, Sb = G["S"]
    K.memset(Sst[:, 0:512], 0.0, [Sb])
    INCL, STRICT, AFTER = C["INCL%d" % d], C["STRICT%d" % d], C["AFTER%d" % d]
    order = range(NT) if d == 0 else range(NT - 1, -1, -1)
    corder = (0, 1) if d == 0 else (1, 0)
    for ti in order:
        t0 = ti * 128
        ld = {}
        for nm, dt_, fm in (("sg", S["sg%d" % d], False), ("rT", S["rTB"], True), ("kkT", S["kkT"], True), ("bT", S["bT%d" % d], True),
                            ("kdT", S["kdT%d" % d], True), ("btok", S["btok%d" % d], False), ("kdtok", S["kdtok%d" % d], False),
                            ("vtok", S["vtokB"], False)):
            tl, tb = G[nm].next()
            if fm:
                K.dma_in(tl[0:64, 0:1024].rearrange("p (h t) -> p h t", h=8), dt_.ap[:, t0:t0 + 128].rearrange("(h p) t -> p h t", p=64), tb,
                         rd=dt_.bufs(t0, t0 + 128))
            else:
                K.dma_in(tl[:, 0:512], dt_.ap[t0:t0 + 128, :], tb, rd=dt_.bufs(t0, t0 + 128))
            ld[nm] = (tl, tb)
        sg, sgb = ld["sg"]
        rT, rTb = ld["rT"]
        kkT, kkTb = ld["kkT"]
        bT, bTb = ld["bT"]
        kdT, kdTb = ld["kdT"]
        btok, btokb = ld["btok"]
        kdtok, kdtokb = ld["kdtok"]
        vtok, vtb = ld["vtok"]
        Ei, Eib = G["Ei"]
        Ev, Evb = G["Ev"]
        Ex, Exb = G["Ex"]
        for q4 in range(4):
            pa_, pab = K.psum()
            for h2 in range(2):
                h = q4 * 2 + h2
                K.mm(pa_[0:64, h2 * 256:h2 * 256 + 128], sg[:, h * 64:(h + 1) * 64], INCL, True, True, [sgb, cb], [pab])
                K.mm(pa_[0:64, h2 * 256 + 128:h2 * 256 + 256], sg[:, h * 64:(h + 1) * 64], STRICT, True, True, [sgb, cb], [pab])
            pa3 = pa_[0:64, 0:512].rearrange("p (c t) -> p c t", c=2)
            hs = slice(q4 * 256, (q4 + 1) * 256)
            K.act(Ei[0:64, hs].rearrange("p (c t) -> p c t", c=2), pa3[:, :, 0:128], AF.Exp, [pab], [Eib], scale=-SCALE_W)
            K.act(Ev[0:64, hs].rearrange("p (c t) -> p c t", c=2), pa3[:, :, 0:128], AF.Exp, [pab], [Evb], scale=SCALE_W)
            K.act(Ex[0:64, hs].rearrange("p (c t) -> p c t", c=2), pa3[:, :, 128:256], AF.Exp, [pab], [Exb], scale=-SCALE_W)
        sm, smb = G["sm"].next()
        pc, pcb = K.psum()
        for h in range(8):
            K.mm(pc[0:64, 2 * h:2 * h + 2], sg[:, h * 64:(h + 1) * 64], C["CHI"], True, True, [sgb, cb], [pcb])
        K.act(sm[0:64, 0:16], pc[0:64, 0:16], AF.Exp, [pcb], [smb], scale=-SCALE_W)
        prv, prvb = K.psum()
        K.mm(prv[:, 0:512], AFTER, sg[:, 0:512], True, True, [sgb, cb], [prvb])
        Er, Erb = G["Er"]
        K.act(Er[:, 0:512], prv[:, 0:512], AF.Exp, [prvb], [Erb], scale=-SCALE_W)
        rt, rtb = G["rt"]
        at, atb = G["at"]
        bt, btb = G["bt"]
        kt, ktb_ = G["kt"]
        K.tt(rt[0:64, 0:1024], rT[0:64, 0:1024], Ei[0:64, 0:1024], ALU.mult, [rTb, Eib], [rtb])
        K.stt(at[0:64, 0:1024], kkT[0:64, 0:1024], -1.0, Ex[0:64, 0:1024], ALU.mult, ALU.mult, [kkTb, Exb], [atb])
        K.tt(bt[0:64, 0:1024], bT[0:64, 0:1024], Ev[0:64, 0:1024], ALU.mult, [bTb, Evb], [btb], eng="pool")
        K.tt(kt[0:64, 0:1024], kdT[0:64, 0:1024], Ev[0:64, 0:1024], ALU.mult, [kdTb, Evb], [ktb_], eng="pool")
        bg, bgb = G["bg"]
        kg, kgb = G["kg"]
        K.tt(bg[:, 0:512], btok[:, 0:512], Er[:, 0:512], ALU.mult, [btokb, Erb], [bgb])
        K.tt(kg[:, 0:512], kdtok[:, 0:512], Er[:, 0:512], ALU.mult, [kdtokb, Erb], [kgb], eng="pool")
        if DBG_LEVEL < 2:
            continue
        def hsl(t_, h):
            return t_[0:64, h * 128:(h + 1) * 128]

        QTs = [G["QTa"], G["QTb"]]
        Aak = [G["AakA"], G["AakB"]]
        Arb = [G["ArbA"], G["ArbB"]]
        Ark = [G["ArkA"], G["ArkB"]]
        for grp in range(2):
            for (dst, lh, lhb, rh, rhb, mask) in ((QTs[grp], at, atb, bt, btb, AFTER), (Aak[grp], kt, ktb_, at, atb, STRICT),
                                                  (Arb[grp], bt, btb, rt, rtb, INCL), (Ark[grp], kt, ktb_, rt, rtb, INCL)):
                ps, pb = K.psum()
                for h4 in range(4):
                    h = grp * 4 + h4
                    K.mm(ps[:, h4 * 128:(h4 + 1) * 128], hsl(lh, h), hsl(rh, h), True, True, [lhb, rhb], [pb])
                K.tt(v3(dst[0][:, 0:512]), v3(ps[:, 0:512]), bc_mid(mask), ALU.mult, [pb, cb], [dst[1]])
        if DBG_LEVEL < 3:
            continue
        QX = [G["QXa"], G["QXb"]]
        X3 = []
        for grp in range(2):
            X3.append(invert4(K, C, QTs[grp][0], QTs[grp][1], QX[grp][0], QX[grp][1]))
        if DBG_LEVEL < 4:
            continue
        Xs, Xsb = G["Xs"]
        Us, Usb = G["Us"]
        yt, ytb = G["yt"]
        bgm, bgmb = G["bgm"]
        kgm, kgmb = G["kgm"]
        for cc in corder:
            K.ts(bgm[:, 0:512], bg[:, 0:512], C["CHI"][:, cc:cc + 1], ALU.mult, [bgb, cb], [bgmb])
            K.ts(kgm[:, 0:512], kg[:, 0:512], C["CHI"][:, cc:cc + 1], ALU.mult, [kgb, cb], [kgmb])
            px, pxb = K.psum()
            for h in range(8):
                vs = slice(h * 64, (h + 1) * 64)
                K.mm(px[:, vs], hsl(at, h), Sst[0:64, vs], True, False, [atb, Sb], [pxb])
                K.mm(px[:, vs], Aak[h // 4][0][:, (h % 4) * 128:(h % 4 + 1) * 128], vtok[:, vs], False, True, [Aak[h // 4][1], vtb], [pxb])
            K.copy(Xs[:, 0:512], px[:, 0:512], [pxb], [Xsb], eng="act")
            pu, pub = K.psum()
            for h in range(8):
                vs = slice(h * 64, (h + 1) * 64)
                K.mm(pu[:, vs], X3[h // 4][:, h % 4, 128:256], Xs[:, vs], True, True, [QX[h // 4][1], Xsb], [pub])
            K.copy(Us[:, 0:512], pu[:, 0:512], [pub], [Usb], eng="act")
            py, pyb = K.psum()
            for h in range(8):
                vs = slice(h * 64, (h + 1) * 64)
                hs = slice((h % 4) * 128, (h % 4 + 1) * 128)
                K.mm(py[:, vs], hsl(rt, h), Sst[0:64, vs], True, False, [rtb, Sb], [pyb])
                K.mm(py[:, vs], Arb[h // 4][0][:, hs], Us[:, vs], False, False, [Arb[h // 4][1], Usb], [pyb])
                K.mm(py[:, vs], Ark[h // 4][0][:, hs], vtok[:, vs], False, True, [Ark[h // 4][1], vtb], [pyb])
            rows = slice(cc * 64, (cc + 1) * 64)
            K.copy(yt[rows, 0:512], py[rows, 0:512], [pyb], [ytb], eng="act")
            pS, pSb = K.psum()
            for h in range(8):
                vs = slice(h * 64, (h + 1) * 64)
                K.mm(pS[0:64, vs], bgm[:, vs], Us[:, vs], True, False, [bgmb, Usb], [pSb])
                K.mm(pS[0:64, vs], kgm[:, vs], vtok[:, vs], False, True, [kgmb, vtb], [pSb])
            egl = sm[0:64, 0:16].rearrange("p (h e) -> p h e", h=8)[:, :, cc:cc + 1].to_broadcast([64, 8, 64])
            s3 = Sst[0:64, 0:512].rearrange("p (h v) -> p h v", h=8)
            K.tt(s3, s3, egl, ALU.mult, [Sb, smb], [Sb])
            K.tt(Sst[0:64, 0:512], Sst[0:64, 0:512], pS[0:64, 0:512], ALU.add, [Sb, pSb], [Sb])
        if DBG_LEVEL < 5:
            continue
        if d == 0:
            K.dma_out(S["yBf"].ap[t0:t0 + 128, :], yt[:, 0:512], ytb, wr=S["yBf"].bufs(t0, t0 + 128))
        else:
            yf, yfb = G["yf"].next()
            gt, gtb = G["gt"].next()
            bon, bonb = G["bon"].next()
            K.dma_in(yf[:, 0:512], S["yBf"].ap[t0:t0 + 128, :], yfb, rd=S["yBf"].bufs(t0, t0 + 128))
            K.dma_in(gt[:, 0:512], S["gtok"].ap[t0:t0 + 128, :], gtb, rd=S["gtok"].bufs(t0, t0 + 128))
            K.dma_in(bon[:, 0:8], S["bons"].ap[t0:t0 + 128, :], bonb, rd=S["bons"].bufs(t0, t0 + 128))
            K.tt(yt[:, 0:512], yt[:, 0:512], yf[:, 0:512], ALU.add, [ytb, yfb], [ytb])
            y8 = yt[:, 0:512].rearrange("p (h v) -> p h v", h=8)
            sqt, sqtb = G["sqt"]
            sm2, sm2b = G["sm"].next()
            K.P.add("dve", lambda e, o_=sm2[:, 0:8], i_=y8: e.tensor_reduce(out=o_, in_=i_, axis=AX.X, op=ALU.add), reads=[ytb], writes=[sm2b])
            K.act(sqt[:, 0:512], yt[:, 0:512], AF.Square, [ytb], [sqtb])
            K.P.add("dve", lambda e, o_=sm2[:, 8:16], i_=sqt[:, 0:512].rearrange("p (h v) -> p h v", h=8): e.tensor_reduce(out=o_, in_=i_, axis=AX.X, op=ALU.add),
                    reads=[sqtb], writes=[sm2b])
            K.ts(sm2[:, 16:24], sm2[:, 0:8], 1.0 / 64, ALU.mult, [sm2b], [sm2b])
            K.tt(sm2[:, 24:32], sm2[:, 16:24], sm2[:, 16:24], ALU.mult, [sm2b], [sm2b])
            K.stt(sm2[:, 32:40], sm2[:, 8:16], 1.0 / 64, sm2[:, 24:32], ALU.mult, ALU.subtract, [sm2b], [sm2b])
            K.act(sm2[:, 40:48], sm2[:, 32:40], AF.Sqrt, [sm2b, cb], [sm2b], bias=C["epsgn"], scale=1.0)
            K.recip(sm2[:, 40:48], sm2[:, 40:48], [sm2b], [sm2b])
            K.tt(y8, y8, bc_last(sm2[:, 16:24], 64), ALU.subtract, [ytb, sm2b], [ytb])
            K.tt(y8, y8, bc_last(sm2[:, 40:48], 64), ALU.mult, [ytb, sm2b], [ytb])
            K.tt(yt[:, 0:512], yt[:, 0:512], ev(lc, "lnw"), ALU.mult, [ytb, lcb], [ytb])
            K.tt(yt[:, 0:512], yt[:, 0:512], ev(lc, "lnb"), ALU.add, [ytb, lcb], [ytb])
            K.tt(sqt[:, 0:512].rearrange("p (h v) -> p h v", h=8), vtok[:, 0:512].rearrange("p (h v) -> p h v", h=8), bc_last(bon[:, 0:8], 64),
                 ALU.mult, [vtb, bonb], [sqtb], eng="pool")
            K.tt(yt[:, 0:512], yt[:, 0:512], sqt[:, 0:512], ALU.add, [ytb, sqtb], [ytb])
            K.tt(yt[:, 0:512], yt[:, 0:512], gt[:, 0:512], ALU.mult, [ytb, gtb], [ytb])
            pt_, ptb = K.psum()
            for c in range(4):
                cs = slice(c * 128, (c + 1) * 128)
                K.mm(pt_[:, cs], yt[:, cs], C["ident"], True, True, [ytb, cb], [ptb])
            yts, ytsb = G["yts"]
            K.copy(yts[:, 0:512], pt_[:, 0:512], [ptb], [ytsb], eng="act")
            K.dma_out(S["yT"].ap[512:1024, t0:t0 + 128].rearrange("(h p) t -> p h t", p=128), v3(yts[:, 0:512]), ytsb,
                      wr=S["yT"].bufs(t0, t0 + 128))
SCRATCH = {
    "hT": (D, None), "rawA": (1536, None), "rawB": (1760, None), "zt": (None, 512), "abt": (None, 16), "gbt": (None, 16),
    "qT": (512, None), "kT": (512, None), "ktok": (None, 512), "vtok": (None, 512), "oAf": (None, 512), "yT": (1024, None),
    "rTB": (512, None), "kkT": (512, None), "vtokB": (None, 512), "gtok": (None, 512), "bons": (None, 8), "yBf": (None, 512),
    "bT0": (512, None), "bT1": (512, None), "kdT0": (512, None), "kdT1": (512, None),
    "btok0": (None, 512), "btok1": (None, 512), "kdtok0": (None, 512), "kdtok1": (None, 512), "sg0": (None, 512), "sg1": (None, 512),
}


def even_out(K, C, A, W, S, j, t0):
    h, hb = A["h"]
    ua, uab = A["ua"]
    src = S["yT"].ap[:, t0:t0 + TS].rearrange("(c p) t -> p c t", p=128)
    K.dma_in(ua[:, 0:KD * TS].rearrange("p (c t) -> p c t", c=KD), src.bitcast(F32R), uab, rd=S["yT"].bufs(t0, t0 + TS), eng="pool")

    def cons_o(cid, M, ps, pb):
        K.tt(h[:, cid * TS:(cid + 1) * TS], h[:, cid * TS:(cid + 1) * TS], ps[:, 0:TS], ALU.add, [hb, pb], [hb])

    linear_fm(K, C, W["w_out_even"][j], KD, blocks_of(D, 512), ua, uab, cons_o)


def build_program(T, layers, pack, dbg=False, stop_after=None):
    import contextlib
    nc = bass.Bass("TRN2", target_bir_lowering=False)
    x_in = nc.dram_tensor("x", [T, D], F32, kind="ExternalInput").ap()
    p_in = nc.dram_tensor("p", [DEPTH, T, PLE], F32, kind="ExternalInput").ap()
    cp_in = nc.dram_tensor("cpack", [128, pack.n], F32, kind="ExternalInput").ap()
    op_in = nc.dram_tensor("oddpack", [2, 128, 4096], F32, kind="ExternalInput").ap()
    ep_in = nc.dram_tensor("evenpack", [2, 128, 4096], F32, kind="ExternalInput").ap()
    W = {}
    for nm, shp in WEIGHT_SHAPES.items():
        t = nc.dram_tensor(nm, list(shp), F32, kind="ExternalInput").ap()
        W[nm] = [t[i] for i in range(shp[0])]
    out = nc.dram_tensor("out", [T, D], F32, kind="ExternalOutput").ap()
    S = {}
    for nm, (r, c) in SCRATCH.items():
        shp = [r if r is not None else T, c if c is not None else T]
        S[nm] = DT(nc, nm, shp, kind="ExternalOutput" if dbg else "Internal")
    hT = S["hT"]
    NS = T // TS
    with contextlib.ExitStack() as st:
        K = KB(nc, st)
        rem = nc.sbuf_bytes_remaining
        K.init_arena(rem // 4 - 64)
        C = setup_consts(K, cp_in, pack)
        base = K.mark()
        first = True
        for li_pos, li in enumerate(layers):
            last = li_pos == len(layers) - 1
            j = li // 2
            if li % 2 == 1:
                A = alloc_tl(K, C)
                lc, lcb = A["lc"]
                K.dma_in(lc[:, 0:4096], op_in[j], lcb)
                for s_ in range(NS):
                    t0 = s_ * TS
                    if first:
                        load_h_from_x(K, C, A, x_in, t0)
                    else:
                        load_h(K, C, A, hT, t0)
                    odd_mixer(K, C, A, W, j, li)
                    ffn_ple(K, C, A, W, li, p_in[li], t0)
                    if last:
                        final_out(K, C, A, out, t0)
                    else:
                        store_h(K, C, A, hT, t0)
                K.P.barrier()
                K.release(base)
            else:
                A = alloc_tl(K, C)
                even_consts(K, C, A, ep_in[j])
                for s_ in range(NS):
                    t0 = s_ * TS
                    if first:
                        load_h_from_x(K, C, A, x_in, t0)
                        store_h(K, C, A, hT, t0)
                    else:
                        load_h(K, C, A, hT, t0)
                    even_in(K, C, A, W, S, j, li, t0)
                K.P.barrier()
                K.release(base)
                if stop_after == "E1":
                    break
                A = {"lc": C["lc"], "dv": C["dv"]}
                G = alloc_prepA(K)
                for s_ in range(NS):
                    prep_A(K, C, A, G, S, s_ * TS, T)
                K.P.barrier()
                K.release(base)
                if stop_after == "E2A":
                    break
                G = alloc_scanA(K)
                scan_A(K, C, A, G, S, 0, T)
                scan_A(K, C, A, G, S, 1, T)
                K.P.barrier()
                K.release(base)
                if stop_after == "E3A":
                    break
                G = alloc_prepB(K)
                for s_ in range(NS):
                    prep_B(K, C, A, G, S, s_ * TS, T)
                K.P.barrier()
                K.release(base)
                if stop_after == "E2B":
                    break
                G = alloc_scanB(K)
                scan_B(K, C, A, G, S, 0, T)
                scan_B(K, C, A, G, S, 1, T)
                K.P.barrier()
                K.release(base)
                if stop_after == "E3B":
                    break
                A = alloc_tl(K, C)
                for s_ in range(NS):
                    t0 = s_ * TS
                    load_h(K, C, A, hT, t0)
                    even_out(K, C, A, W, S, j, t0)
                    ffn_ple(K, C, A, W, li, p_in[li], t0)
                    if last:
                        final_out(K, C, A, out, t0)
                    else:
                        store_h(K, C, A, hT, t0)
                K.P.barrier()
                K.release(base)
            first = False
        K.P.finalize_and_emit()
    return nc


N_CORES = 8


def kernel(**inputs):
    inp = {k: np.asarray(v) for k, v in inputs.items()}
    B, T = inp["x"].shape[0], inp["x"].shape[1]
    pack = build_packs(inp)
    ep = np.stack([even_pack(inp, 0), even_pack(inp, 1)])
    op = np.stack([odd_pack(inp, 0), odd_pack(inp, 1)])
    cp = pack.array()
    nc = build_program(T, [0, 1, 2, 3], pack)
    wts = {nm: np.ascontiguousarray(inp[nm], dtype=np.float32) for nm in WEIGHT_SHAPES}
    in_maps = []
    for c in range(N_CORES):
        b = c % B
        m = {"x": np.ascontiguousarray(inp["x"][b], dtype=np.float32), "p": np.ascontiguousarray(inp["p"][:, b], dtype=np.float32),
             "cpack": cp, "oddpack": op, "evenpack": ep}
        m.update(wts)
        in_maps.append(m)
    res = run_bass_kernel_spmd(nc, in_maps, core_ids=list(range(N_CORES)))
    out = np.stack([np.asarray(res.results[b]["out"], dtype=np.float32) for b in range(B)])
    return out
```

```python
import bisect
import numpy as np
import concourse.bass as bass
import concourse.mybir as mybir
from concourse.bass_utils import run_bass_kernel_spmd

F32 = mybir.dt.float32
F32R = mybir.dt.float32r
AF = mybir.ActivationFunctionType
ALU = mybir.AluOpType
AX = mybir.AxisListType

GEN = 30000
DGEN = 1800


class Buf:
    __slots__ = ("name", "last_w", "readers", "lane_in", "lane_out")

    def __init__(self, name):
        self.name = name
        self.last_w = None
        self.readers = []
        self.lane_in = None
        self.lane_out = None


class Op:
    __slots__ = ("eng", "fn", "deps", "stream", "sidx", "signal", "val", "waits", "oid")


class Prog:
    CE = ("pe", "act", "dve", "pool")

    def __init__(self, nc):
        self.nc = nc
        self.ops = []
        self.streams = {}
        self.pending_bar = {}

    def _stream(self, name):
        return self.streams.setdefault(name, [])

    def add(self, eng, fn, reads=(), writes=(), dma=None):
        op = Op()
        op.oid = len(self.ops)
        op.eng = eng
        op.fn = fn
        deps = {}
        for b in reads:
            if b.last_w is not None:
                deps[b.last_w] = "raw"
        for b in writes:
            if b.last_w is not None:
                deps.setdefault(b.last_w, "waw")
            for r in b.readers:
                deps.setdefault(r, "war")
        if dma is not None:
            kind, lb = dma
            op.stream = "dma_%s_%s" % (kind, lb.name)
        else:
            op.stream = eng
        pruned = {}
        for d, k in deps.items():
            p = self.ops[d]
            if p.stream == op.stream:
                if dma is not None:
                    continue
                if eng == "pe":
                    continue
            pruned[d] = k
        if eng in self.pending_bar:
            for d in self.pending_bar.pop(eng):
                if self.ops[d].stream != op.stream or dma is not None:
                    pruned[d] = "bar"
        op.deps = pruned
        st = self._stream(op.stream)
        op.sidx = len(st)
        st.append(op.oid)
        op.signal = dma is not None
        self.ops.append(op)
        for b in reads:
            b.readers.append(op.oid)
        for b in writes:
            b.last_w = op.oid
            b.readers = []
        return op.oid

    def barrier(self):
        last = set(lst[-1] for lst in self.streams.values() if lst)
        for e in ("pe", "act", "dve", "pool", "sp"):
            self.pending_bar[e] = set(last) | self.pending_bar.get(e, set())

    def finalize_and_emit(self, final_waits=()):
        nc = self.nc
        ops = self.ops
        waited = {}
        for op in ops:
            need = {}
            for d in op.deps:
                p = ops[d]
                if p.stream.startswith("dma_"):
                    lane = self.streams[p.stream]
                    cnt = bisect.bisect_left(lane, op.oid)
                    need[p.stream] = max(need.get(p.stream, -1), cnt - 1)
                else:
                    need[p.stream] = max(need.get(p.stream, -1), p.sidx)
            op.waits = []
            for s, idx in need.items():
                key = (op.eng, s)
                if waited.get(key, -1) >= idx:
                    continue
                waited[key] = idx
                op.waits.append((s, idx))
                if not s.startswith("dma_"):
                    ops[self.streams[s][idx]].signal = True
        fin = []
        for s_, lst_ in self.streams.items():
            if s_.startswith("dma_") and lst_:
                fin.append((s_, len(lst_) - 1))
        sem_of = {}
        import contextlib
        stack = contextlib.ExitStack()
        with stack:
            valmap = {}
            for s, lst in self.streams.items():
                if s.startswith("dma_"):
                    sem = None
                    for i, o in enumerate(lst):
                        if i % DGEN == 0:
                            sem = stack.enter_context(nc.semaphore("s_%s_%d" % (s, i // DGEN)))
                        valmap[(s, i)] = (sem, 16 * (i % DGEN + 1))
                        ops[o].val = (sem, 16)
                else:
                    cnt = 0
                    gen = 0
                    sem = stack.enter_context(nc.semaphore("s_%s_%d" % (s, gen)))
                    last = None
                    for i, o in enumerate(lst):
                        if ops[o].signal:
                            if cnt >= GEN:
                                gen += 1
                                cnt = 0
                                sem = stack.enter_context(nc.semaphore("s_%s_%d" % (s, gen)))
                            cnt += 1
                            ops[o].val = (sem, 1)
                            valmap[(s, i)] = (sem, cnt)
            self.n_sems = sum(1 for _ in sem_of)
            per_eng = {e: [] for e in ("pe", "act", "dve", "pool", "sp")}
            for op in ops:
                per_eng[op.eng].append(op)

            def run_engine(eobj, lst, is_sp=False):
                for op in lst:
                    for (s, idx) in op.waits:
                        sem, v = valmap[(s, idx)]
                        eobj.wait_ge(sem, v)
                    ins = op.fn(eobj)
                    if op.signal:
                        sem, inc = op.val
                        ins.then_inc(sem, inc)
                if is_sp:
                    for (s, idx) in fin:
                        sem, v = valmap[(s, idx)]
                        eobj.wait_ge(sem, v)

            with nc.Block() as block:
                @block.tensor
                def _(e):
                    run_engine(e, per_eng["pe"])

                @block.scalar
                def _(e):
                    run_engine(e, per_eng["act"])

                @block.vector
                def _(e):
                    run_engine(e, per_eng["dve"])

                @block.gpsimd
                def _(e):
                    run_engine(e, per_eng["pool"])

                @block.sync
                def _(e):
                    run_engine(e, per_eng["sp"], is_sp=True)


D = 1024
KD = 8
DEPTH = 4
PLE = 256
DFF = 2816
NFF = 22
A_COLS = 2064
B_COLS = 1760
EVEN_IN = A_COLS + B_COLS
TS = 512
NEG = -30000.0


class Pack:
    def __init__(self):
        self.cols = []
        self.off = {}
        self.n = 0

    def add(self, name, arr):
        arr = np.ascontiguousarray(arr, dtype=np.float32)
        assert arr.ndim == 2 and arr.shape[0] <= 128, (name, arr.shape)
        if arr.shape[0] < 128:
            pad = np.zeros((128, arr.shape[1]), np.float32)
            pad[: arr.shape[0]] = arr
            arr = pad
        self.off[name] = (self.n, arr.shape[1])
        self.cols.append(arr)
        self.n += arr.shape[1]

    def array(self):
        return np.concatenate(self.cols, axis=1)


def chunked(v, nch=None):
    v = np.asarray(v, np.float32).reshape(-1, 128)
    return np.ascontiguousarray(v.T)


def rowrep(v):
    v = np.asarray(v, np.float32).reshape(1, -1)
    return np.ascontiguousarray(np.broadcast_to(v, (128, v.shape[1])))


class Ring:
    def __init__(self, K, name, shape, dtype, n):
        self.tiles = []
        for i in range(n):
            t = K.sb("%s%d" % (name, i), shape, dtype)
            self.tiles.append((t, Buf("%s%d" % (name, i))))
        self.i = 0

    def next(self):
        r = self.tiles[self.i % len(self.tiles)]
        self.i += 1
        return r


class KB:
    def __init__(self, nc, st):
        self.nc = nc
        self.st = st
        self.P = Prog(nc)
        self.ps_tiles = []
        for i in range(8):
            t = st.enter_context(nc.psum_tensor("ps%d" % i, [128, 512], F32))
            self.ps_tiles.append((t, Buf("ps%d" % i)))
        self.ps_i = 0
        self.ps_sel = None
        self.ps_sub = {"a": 0, "b": 0}
        self.dq = 0

    def init_arena(self, nfloats):
        self.arena_t = self.st.enter_context(self.nc.sbuf_tensor("arena", [128, nfloats], F32))
        self.arena = self.arena_t
        self.arena_base = self.nc.lookup_mloc(self.arena_t).addr
        self.arena_n = nfloats
        self.arena_p = 0
        self.alias_n = 0

    def sb(self, name, shape, dtype=F32):
        n = 1
        for d in shape[1:]:
            n *= d
        off = (self.arena_p + 7) // 8 * 8
        assert off + n <= self.arena_n, ("arena overflow", name, off, n, self.arena_n)
        self.arena_p = off + n
        self.alias_n += 1
        t = self.nc.alloc_sbuf_tensor_at("%s_m%d" % (name, self.alias_n), [128, n], dtype, offset=self.arena_base + 4 * off)
        ap = t[0:shape[0], 0:n]
        if len(shape) == 3:
            ap = ap.rearrange("p (a b) -> p a b", a=shape[1])
        elif len(shape) == 4:
            ap = ap.rearrange("p (a b c) -> p a b c", a=shape[1], b=shape[2])
        return ap

    def mark(self):
        return self.arena_p

    def release(self, m):
        self.arena_p = m

    def psum(self):
        if self.ps_sel is not None:
            base = 0 if self.ps_sel == "a" else 4
            i = self.ps_sub[self.ps_sel]
            self.ps_sub[self.ps_sel] = i + 1
            return self.ps_tiles[base + i % 4]
        r = self.ps_tiles[self.ps_i % 8]
        self.ps_i += 1
        return r

    def dma_in(self, out_ap, in_ap, buf, rd=(), eng="sp"):
        self.P.add(eng, lambda e: e.dma_start(out=out_ap, in_=in_ap), reads=list(rd), writes=[buf], dma=("in", buf))

    def dma_out(self, out_ap, in_ap, buf, wr=(), eng="sp"):
        self.P.add(eng, lambda e: e.dma_start(out=out_ap, in_=in_ap), reads=[buf], writes=list(wr), dma=("out", buf))

    def mm(self, out_ap, lhsT, rhs, start, stop, rd, wr):
        self.P.add("pe", lambda e: e.matmul(out_ap, lhsT=lhsT, rhs=rhs, start=start, stop=stop), reads=rd, writes=wr)

    def act(self, out_ap, in_ap, func, rd, wr, bias=None, scale=None, accum_out=None):
        kw = {}
        if bias is not None:
            kw["bias"] = bias
        if scale is not None:
            kw["scale"] = scale
        if accum_out is not None:
            kw["accum_out"] = accum_out
        self.P.add("act", lambda e: e.activation(out=out_ap, in_=in_ap, func=func, **kw), reads=rd, writes=wr)

    def tt(self, out_ap, in0, in1, op, rd, wr, eng="dve"):
        self.P.add(eng, lambda e: e.tensor_tensor(out=out_ap, in0=in0, in1=in1, op=op), reads=rd, writes=wr)

    def ts(self, out_ap, in0, s1, op0, rd, wr, s2=None, op1=None, eng="dve", accum_out=None):
        kw = {}
        if accum_out is not None:
            kw["accum_out"] = accum_out
        if op1 is None:
            self.P.add(eng, lambda e: e.tensor_scalar(out=out_ap, in0=in0, scalar1=s1, scalar2=None, op0=op0, **kw), reads=rd, writes=wr)
        else:
            self.P.add(eng, lambda e: e.tensor_scalar(out=out_ap, in0=in0, scalar1=s1, scalar2=s2, op0=op0, op1=op1, **kw), reads=rd, writes=wr)

    def stt(self, out_ap, in0, scalar, in1, op0, op1, rd, wr, eng="dve"):
        self.P.add(eng, lambda e: e.scalar_tensor_tensor(out=out_ap, in0=in0, scalar=scalar, in1=in1, op0=op0, op1=op1), reads=rd, writes=wr)

    def copy(self, out_ap, in_ap, rd, wr, eng="dve"):
        if eng == "act":
            self.P.add("act", lambda e: e.activation(out=out_ap, in_=in_ap, func=AF.Copy), reads=rd, writes=wr)
        else:
            self.P.add(eng, lambda e: e.tensor_copy(out=out_ap, in_=in_ap), reads=rd, writes=wr)

    def recip(self, out_ap, in_ap, rd, wr):
        self.P.add("dve", lambda e: e.reciprocal(out=out_ap, in_=in_ap), reads=rd, writes=wr)

    def memset(self, ap, val, wr, eng="pool"):
        self.P.add(eng, lambda e: e.memset(ap, val), reads=[], writes=wr)


class DT:
    def __init__(self, nc, name, shape, kind="Internal", dtype=F32):
        self.name = name
        self.t = nc.dram_tensor(name, list(shape), dtype, kind=kind)
        self.ap = self.t.ap()
        self._b = {}

    def bufs(self, t0, t1):
        t0 = max(t0, 0)
        return [self._b.setdefault(i, Buf("%s_b%d" % (self.name, i))) for i in range(t0 // 128, (t1 + 127) // 128)]


def rmsnorm_fm(K, C, h, hbuf, wcol, out, obuf, sq, sqbuf, nk=KD, n=TS, eps_name="eps6", dscale=1.0 / D):
    K.act(sq[:, 0:nk * n], h[:, 0:nk * n], AF.Square, [hbuf], [sqbuf])
    ps, pb = K.psum()
    for c in range(nk):
        K.mm(ps[:, 0:n], C["ones_r"], sq[:, c * n:(c + 1) * n], c == 0, c == nk - 1, [sqbuf, C["buf"]], [pb])
    rs, rb = C["rstd"].next()
    K.act(rs[:, 0:n], ps[:, 0:n], AF.Sqrt, [pb, C["buf"]], [rb], bias=C[eps_name], scale=dscale)
    K.recip(rs[:, 0:n], rs[:, 0:n], [rb], [rb])
    for c in range(nk):
        K.stt(out[:, c * n:(c + 1) * n], h[:, c * n:(c + 1) * n], wcol[:, c:c + 1], rs[:, 0:n], ALU.mult, ALU.mult,
              [hbuf, rb, C["buf"]], [obuf])


def load_w(K, C, wap, r0, nk, c0, ncols, eng="pool"):
    wt, wb = C["wring"].next()
    view = wt[:, 0:nk * ncols].rearrange("p (k c) -> p k c", k=nk)
    src = wap[r0:r0 + nk * 128, c0:c0 + ncols].rearrange("(k p) c -> p k c", p=128)
    K.dma_in(view, src.bitcast(F32R), wb, eng=eng)
    return view, wb


def linear_fm(K, C, wap, nk, col_blocks, xin, xbuf, consumer, n=TS, r0=0):
    for (c0, ncols, chunks) in col_blocks:
        wv, wb = load_w(K, C, wap, r0, nk, c0, ncols)
        for (cid, off, M) in chunks:
            ps, pb = K.psum()
            for k in range(nk):
                K.mm(ps[0:M, 0:n], wv[:, k, off:off + M], xin[:, k * n:(k + 1) * n], k == 0, k == nk - 1, [wb, xbuf], [pb])
            consumer(cid, M, ps, pb)


def blocks_of(total, bs, cbase=0, m=128):
    out = []
    c0 = 0
    cid = cbase
    while c0 < total:
        nc_ = min(bs, total - c0)
        chunks = []
        off = 0
        while off < nc_:
            mm_ = min(m, nc_ - off)
            chunks.append((cid, off, mm_))
            cid += 1
            off += mm_
        out.append((c0, nc_, chunks))
        c0 += nc_
    return out


def build_packs(inp):
    P = Pack()
    P.add("ident", np.eye(128, dtype=np.float32))
    P.add("ones", np.ones((128, 128), np.float32))
    P.add("eps6", np.full((128, 1), 1e-6, np.float32))
    P.add("eps5", np.full((128, 1), 1e-5, np.float32))
    P.add("epsgn", np.full((128, 1), 64e-5, np.float32))
    for i in range(DEPTH):
        P.add("nmix%d" % i, chunked(inp["norm_mix"][i]))
        P.add("nffn%d" % i, chunked(inp["norm_ffn"][i]))
        P.add("nple%d" % i, chunked(inp["norm_ple"][i]))
    P.add("nfin", chunked(inp["norm_final"]))
    mask_pack(P)
    return P


class Ctx(dict):
    pass


def setup_consts(K, cp_dram, pack):
    C = Ctx()
    n = pack.n
    cpt = K.sb("cpack", [128, n])
    cb = Buf("cpack")
    K.dma_in(cpt[:, 0:n], cp_dram, cb)
    C["buf"] = cb
    C["cp"] = cpt
    for name, (off, w) in pack.off.items():
        C[name] = cpt[:, off:off + w]
    ones_r = K.sb("ones_r", [128, 128], F32R)
    K.P.add("pool", lambda e: e.tensor_copy(out=ones_r, in_=C["ones"]), reads=[cb], writes=[cb])
    C["ones_r"] = ones_r
    blk_r = K.sb("blk_r", [128, 128], F32R)
    K.P.add("pool", lambda e: e.tensor_copy(out=blk_r, in_=C["BLK"]), reads=[cb], writes=[cb])
    C["blk_r"] = blk_r
    C["rstd"] = Ring(K, "rstd", [128, TS], F32, 2)
    C["lc"] = (K.sb("lc", [128, 4096], F32), Buf("lc"))
    C["dv"] = (K.sb("dv", [128, 64], F32), Buf("dv"))
    return C


class Ring:
    def __init__(self, K, name, shape, dtype, n):
        self.tiles = []
        for i in range(n):
            t = K.sb("%s%d" % (name, i), shape, dtype)
            self.tiles.append((t, Buf("%s%d" % (name, i))))
        self.i = 0

    def next(self):
        r = self.tiles[self.i % len(self.tiles)]
        self.i += 1
        return r


def alloc_tl(K, C):
    A = {}
    for nm in ("h", "hn", "sq"):
        dt = F32 if nm in ("h",) else F32R
        A[nm] = (K.sb("tl_" + nm, [128, KD * TS], dt), Buf("tl_" + nm))
    A["ua"] = A["sq"]
    A["lc"] = C["lc"]
    A["dv"] = C["dv"]
    C["wring"] = Ring(K, "wring", [128, 4224], F32R, 2)
    A["act"] = (K.sb("tl_act", [128, NFF * TS], F32R), Buf("tl_act"))
    A["tmp"] = Ring(K, "tl_tmp", [128, TS], F32, 3)
    A["tok"] = Ring(K, "tl_tok", [128, 1024], F32, 2)
    A["tok2"] = Ring(K, "tl_tok2", [128, 1024], F32, 2)
    A["st"] = Ring(K, "tl_st", [128, 8], F32, 4)
    A["pt"] = (K.sb("tl_pt", [128, 2 * TS], F32R), Buf("tl_pt"))
    return A


def load_h_from_x(K, C, A, x_ap, t0):
    h, hb = A["h"]
    for sub in range(TS // 128):
        xt, xb = A["tok"].next()
        K.dma_in(xt[:, 0:1024], x_ap[t0 + sub * 128:t0 + (sub + 1) * 128, :], xb)
        for half in range(2):
            ps, pb = K.psum()
            for c4 in range(4):
                c = half * 4 + c4
                K.mm(ps[:, c4 * 128:(c4 + 1) * 128], xt[:, c * 128:(c + 1) * 128], C["ident"], True, True, [xb, C["buf"]], [pb])
            dst = h[:, half * 4 * TS:(half + 1) * 4 * TS].rearrange("p (c t) -> p c t", c=4)[:, :, sub * 128:(sub + 1) * 128]
            src = ps[:, 0:512].rearrange("p (c t) -> p c t", c=4)
            K.copy(dst, src, [pb], [hb], eng="act" if half else "dve")


def load_h(K, C, A, hT, t0):
    h, hb = A["h"]
    src = hT.ap[:, t0:t0 + TS].rearrange("(c p) t -> p c t", p=128)
    K.dma_in(h[:, 0:KD * TS].rearrange("p (c t) -> p c t", c=KD), src, hb, rd=hT.bufs(t0, t0 + TS))


def store_h(K, C, A, hT, t0):
    h, hb = A["h"]
    dst = hT.ap[:, t0:t0 + TS].rearrange("(c p) t -> p c t", p=128)
    K.dma_out(dst, h[:, 0:KD * TS].rearrange("p (c t) -> p c t", c=KD), hb, wr=hT.bufs(t0, t0 + TS))


def ffn_ple(K, C, A, W, li, p_ap, t0):
    h, hb = A["h"]
    hn, hnb = A["hn"]
    sq, sqb = A["sq"]
    act, ab = A["act"]
    rmsnorm_fm(K, C, h, hb, C["nffn%d" % li], hn, hnb, sq, sqb)
    gate_ps = {}

    def cons_gate(cid, M, ps, pb):
        gate_ps[cid] = (ps, pb)

    def cons_up(cid, M, ps, pb):
        gps, gpb = gate_ps.pop(cid)
        tmp, tb = A["tmp"].next()
        K.act(tmp[:, 0:TS], gps[:, 0:TS], AF.Silu, [gpb], [tb])
        K.tt(act[:, cid * TS:(cid + 1) * TS], tmp[:, 0:TS], ps[:, 0:TS], ALU.mult, [tb, pb], [ab])

    for (c0, ncols, chunks) in blocks_of(DFF, 384):
        linear_fm(K, C, W["w_gate"][li], KD, [(c0, ncols, chunks)], hn, hnb, cons_gate)
        linear_fm(K, C, W["w_up"][li], KD, [(c0, ncols, chunks)], hn, hnb, cons_up)

    def cons_down(cid, M, ps, pb):
        K.tt(h[:, cid * TS:(cid + 1) * TS], h[:, cid * TS:(cid + 1) * TS], ps[:, 0:TS], ALU.add, [hb, pb], [hb])

    linear_fm(K, C, W["w_down"][li], NFF, blocks_of(D, 128), act, ab, cons_down)
    rmsnorm_fm(K, C, h, hb, C["nple%d" % li], hn, hnb, sq, sqb)
    pt, ptb = A["pt"]
    for sub in range(TS // 128):
        xt, xb = A["tok"].next()
        K.dma_in(xt[:, 0:PLE], p_ap[t0 + sub * 128:t0 + (sub + 1) * 128, :], xb)
        ps, pb = K.psum()
        for c in range(2):
            K.mm(ps[:, c * 128:(c + 1) * 128], xt[:, c * 128:(c + 1) * 128], C["ident"], True, True, [xb, C["buf"]], [pb])
        dst = pt[:, 0:2 * TS].rearrange("p (c t) -> p c t", c=2)[:, :, sub * 128:(sub + 1) * 128]
        K.copy(dst, ps[:, 0:256].rearrange("p (c t) -> p c t", c=2), [pb], [ptb], eng="act")
    e_ps = {}

    def cons_e(cid, M, ps, pb):
        e_ps[cid] = (ps, pb)

    def cons_g(cid, M, ps, pb):
        eps_, epb = e_ps.pop(cid)
        tmp, tb = A["tmp"].next()
        K.act(tmp[:, 0:TS], ps[:, 0:TS], AF.Sigmoid, [pb], [tb])
        K.tt(tmp[:, 0:TS], tmp[:, 0:TS], eps_[:, 0:TS], ALU.mult, [tb, epb], [tb])
        K.tt(h[:, cid * TS:(cid + 1) * TS], h[:, cid * TS:(cid + 1) * TS], tmp[:, 0:TS], ALU.add, [hb, tb], [hb])

    for (c0, ncols, chunks) in blocks_of(D, 256):
        linear_fm(K, C, W["w_ple"][li], 2, [(c0, ncols, chunks)], pt, ptb, cons_e)
        linear_fm(K, C, W["w_ple_gate"][li], KD, [(c0, ncols, chunks)], hn, hnb, cons_g)


def final_out(K, C, A, out_ap, t0):
    h, hb = A["h"]
    hn, hnb = A["hn"]
    sq, sqb = A["sq"]
    rmsnorm_fm(K, C, h, hb, C["nfin"], hn, hnb, sq, sqb)
    hn32 = hn.bitcast(F32)
    for sub in range(TS // 128):
        ot, ob = A["tok"].next()
        for half in range(2):
            ps, pb = K.psum()
            for c4 in range(4):
                c = half * 4 + c4
                K.mm(ps[:, c4 * 128:(c4 + 1) * 128], hn32[:, c * TS + sub * 128:c * TS + (sub + 1) * 128], C["ident"], True, True,
                     [hnb, C["buf"]], [pb])
            K.copy(ot[:, half * 512:(half + 1) * 512], ps[:, 0:512], [pb], [ob], eng="act" if half else "dve")
        K.dma_out(out_ap[t0 + sub * 128:t0 + (sub + 1) * 128, :], ot[:, 0:1024], ob)


def odd_mixer(K, C, A, W, j, li):
    h, hb = A["h"]
    hn, hnb = A["hn"]
    sq, sqb = A["sq"]
    ua, uab = A["ua"]
    rmsnorm_fm(K, C, h, hb, C["nmix%d" % li], hn, hnb, sq, sqb)
    win = W["w_in_odd"][j]
    lc, lcb = A["lc"]
    c_lnw, c_lnb, c_wsT, c_bs = lc[:, 0:1024], lc[:, 1024:2048], lc[:, 2048:3072], lc[:, 3072:4096]

    def cons_u(cid, M, ps, pb):
        K.act(ua[:, cid * TS:(cid + 1) * TS], ps[:, 0:TS], AF.Gelu_apprx_tanh, [pb], [uab])

    linear_fm(K, C, win, KD, blocks_of(1024, 512), hn, hnb, cons_u)
    wv = []
    for half in range(2):
        wv.append(load_w(K, C, win, 0, KD, 1024 + half * 512, 512))
    for sub in range(TS // 128):
        vt, vb = A["tok"].next()
        for half in range(2):
            wview, wb = wv[half]
            ps, pb = K.psum()
            for k in range(KD):
                K.mm(ps[:, 0:512], hn[:, k * TS + sub * 128:k * TS + (sub + 1) * 128], wview[:, k, :], k == 0, k == KD - 1, [hnb, wb], [pb])
            K.act(vt[:, half * 512:(half + 1) * 512], ps[:, 0:512], AF.Gelu_apprx_tanh, [pb], [vb])
        st, sb_ = A["st"].next()
        v2, v2b = A["tok2"].next()
        K.P.add("dve", lambda e, st=st, vt=vt: e.reduce_sum(out=st[:, 0:1], in_=vt[:, 0:1024], axis=AX.X), reads=[vb], writes=[sb_])
        K.act(v2[:, 0:1024], vt[:, 0:1024], AF.Square, [vb], [v2b])
        K.P.add("dve", lambda e, st=st, v2=v2: e.reduce_sum(out=st[:, 1:2], in_=v2[:, 0:1024], axis=AX.X), reads=[v2b], writes=[sb_])
        K.ts(st[:, 2:3], st[:, 0:1], 1.0 / 1024, ALU.mult, [sb_], [sb_])
        K.tt(st[:, 3:4], st[:, 2:3], st[:, 2:3], ALU.mult, [sb_], [sb_])
        K.stt(st[:, 4:5], st[:, 1:2], 1.0 / 1024, st[:, 3:4], ALU.mult, ALU.subtract, [sb_], [sb_])
        K.act(st[:, 5:6], st[:, 4:5], AF.Sqrt, [sb_, C["buf"]], [sb_], bias=C["eps5"], scale=1.0)
        K.recip(st[:, 5:6], st[:, 5:6], [sb_], [sb_])
        K.ts(v2[:, 0:1024], vt[:, 0:1024], st[:, 2:3], ALU.subtract, [vb, sb_, v2b], [v2b], s2=st[:, 5:6], op1=ALU.mult)
        K.tt(v2[:, 0:1024], v2[:, 0:1024], c_lnw, ALU.mult, [v2b, lcb], [v2b])
        K.tt(v2[:, 0:1024], v2[:, 0:1024], c_lnb, ALU.add, [v2b, lcb], [v2b])
        for g2 in range(2):
            ps, pb = K.psum()
            for g4 in range(4):
                g = g2 * 4 + g4
                K.mm(ps[:, g4 * 128:(g4 + 1) * 128], v2[:, g * 128:(g + 1) * 128], c_wsT[:, g * 128:(g + 1) * 128], True, False,
                     [v2b, lcb], [pb])
                K.mm(ps[:, g4 * 128:(g4 + 1) * 128], C["ones"][0:1, :], c_bs[0:1, g * 128:(g + 1) * 128], False, True,
                     [C["buf"], lcb], [pb])
            dst = ua[:, g2 * 4 * TS:(g2 + 1) * 4 * TS].rearrange("p (c t) -> p c t", c=4)[:, :, sub * 128:(sub + 1) * 128]
            K.tt(dst, dst.bitcast(F32), ps[:, 0:512].rearrange("p (c t) -> p c t", c=4), ALU.mult, [uab, pb], [uab])

    def cons_o(cid, M, ps, pb):
        K.tt(h[:, cid * TS:(cid + 1) * TS], h[:, cid * TS:(cid + 1) * TS], ps[:, 0:TS], ALU.add, [hb, pb], [hb])

    linear_fm(K, C, W["w_out_odd"][j], KD, blocks_of(D, 512), ua, uab, cons_o)


def odd_pack(inp, j):
    out = np.zeros((128, 4096), np.float32)
    out[:, 0:1024] = rowrep(inp["c_ln_w"][j])
    out[:, 1024:2048] = rowrep(inp["c_ln_b"][j])
    out[:, 2048:3072] = np.transpose(np.asarray(inp["c_ws"][j], np.float32), (2, 0, 1)).reshape(128, 1024)
    out[0, 3072:4096] = np.asarray(inp["c_bs"][j], np.float32).reshape(1024)
    return out


WEIGHT_SHAPES = {
    "w_in_even": (2, D, EVEN_IN), "w_out_even": (2, D, D), "w_in_odd": (2, D, 2048), "w_out_odd": (2, D, D),
    "w_gate": (4, D, DFF), "w_up": (4, D, DFF), "w_down": (4, DFF, D), "w_ple": (4, PLE, D), "w_ple_gate": (4, D, D),
}


def mask_pack(P):
    idx = np.arange(128)
    same = (idx[:, None] // 64) == (idx[None, :] // 64)
    for d in range(2):
        before = (idx[:, None] < idx[None, :]) if d == 0 else (idx[:, None] > idx[None, :])
        strict = (same & before).astype(np.float32)
        incl = (same & (before | (idx[:, None] == idx[None, :]))).astype(np.float32)
        P.add("INCL%d" % d, incl)
        P.add("STRICT%d" % d, strict)
        P.add("AFTER%d" % d, strict.T.copy())
        P.add("NEGINCL%d" % d, np.where(incl.T > 0, 0.0, NEG).astype(np.float32))
    P.add("OFFD", 1.0 - np.eye(128, dtype=np.float32))
    P.add("nones", -np.ones((128, 128), np.float32))
    for cc in range(2):
        P.add("CMROW%d" % cc, np.broadcast_to(((idx // 64) == cc).astype(np.float32)[:, None], (128, 128)).copy())
    P.add("CHI", np.stack([(idx // 64) == 0, (idx // 64) == 1], 1).astype(np.float32))
    P.add("BLK", same.astype(np.float32))
    P.add("HSEL", np.stack([idx < 64, idx >= 64], 1).astype(np.float32))
    P.add("one1", np.ones((128, 1), np.float32))
    P.add("two1", np.full((128, 1), 2.0, np.float32))


EV = {}


def even_pack(inp, j):
    P = Pack()
    cw = np.asarray(inp["a_conv"][j], np.float32)
    P.add("cw", cw.reshape(5, 12, 128).transpose(2, 1, 0).reshape(128, 60))
    mu = np.asarray(inp["b_mu"][j], np.float32)
    P.add("mu_rkv", chunked(mu[0:1536]))
    P.add("mu_wd", mu[1536:1600].reshape(64, 1))
    P.add("mu_ad", mu[1600:1664].reshape(64, 1))
    P.add("mu_gd", mu[1664:1760].reshape(96, 1))
    P.add("k_k", chunked(inp["b_k_k"][j]))
    P.add("k_a", chunked(inp["b_k_a"][j]))
    P.add("r_k", chunked(np.asarray(inp["b_r_k"][j]).reshape(512)))
    P.add("a0", np.concatenate([chunked(inp["b_a0"][j][d]) for d in range(2)], 1))
    P.add("dtb", rowrep(np.asarray(inp["a_dt_bias"][j]).reshape(8)))
    P.add("alog", rowrep(np.asarray(inp["a_log"][j]).reshape(8)))
    P.add("anorm", rowrep(np.tile(np.asarray(inp["a_norm"][j], np.float32), 4)))
    P.add("lnw", rowrep(inp["b_ln_w"][j]))
    P.add("lnb", rowrep(inp["b_ln_b"][j]))
    P.add("w2s", np.asarray(inp["b_w2"][j], np.float32).reshape(64, 512))
    w0s = np.zeros((64, 512), np.float32)
    w0s[0] = inp["b_w0"][j][0]
    w0s[32] = inp["b_w0"][j][1]
    P.add("w0s", w0s)
    P.add("a2s", np.asarray(inp["b_a2"][j], np.float32).reshape(64, 512))
    P.add("g2", np.asarray(inp["b_g2"][j], np.float32))
    EV.update(P.off)
    arr = P.array()
    out = np.zeros((128, 4096), np.float32)
    out[:, : arr.shape[1]] = arr
    return out


def ev(lc, name):
    off, w = EV[name]
    return lc[:, off:off + w]


def even_in(K, C, A, W, S, j, li, t0):
    h, hb = A["h"]
    hn, hnb = A["hn"]
    sq, sqb = A["sq"]
    rmsnorm_fm(K, C, h, hb, C["nmix%d" % li], hn, hnb, sq, sqb)
    win = W["w_in_even"][j]
    cnt = [0]

    def store_to(dt, rowmap):
        def f(cid, M, ps, pb):
            tmp, tb = A["tmp"].next()
            cnt[0] += 1
            K.copy(tmp[0:M, 0:TS], ps[0:M, 0:TS], [pb], [tb], eng="act" if cnt[0] % 2 else "dve")
            r0 = rowmap(cid)
            K.dma_out(dt.ap[r0:r0 + M, t0:t0 + TS], tmp[0:M, 0:TS], tb, wr=dt.bufs(t0, t0 + TS))
        return f

    for b in range(3):
        chunks = [(b * 4 + c, c * 128, 128) for c in range(4)]
        linear_fm(K, C, win, KD, [(b * 512, 512, chunks)], hn, hnb, store_to(S["rawA"], lambda cid: cid * 128))
    for b in range(3):
        chunks = [(b * 4 + c, c * 128, 128) for c in range(4)]
        linear_fm(K, C, win, KD, [(A_COLS + b * 512, 512, chunks)], hn, hnb, store_to(S["rawB"], lambda cid: cid * 128))
    rows = {12: 1536, 13: 1600, 14: 1664}
    linear_fm(K, C, win, KD, [(A_COLS + 1536, 224, [(12, 0, 64), (13, 64, 64), (14, 128, 96)])], hn, hnb,
              store_to(S["rawB"], lambda cid: rows[cid]))
    wz, wzb = load_w(K, C, win, 0, KD, 1536, 528)
    for sub in range(TS // 128):
        zt, zb = A["tok"].next()
        ps, pb = K.psum()
        ps2, pb2 = K.psum()
        for k in range(KD):
            lhsT = hn[:, k * TS + sub * 128:k * TS + (sub + 1) * 128]
            K.mm(ps[:, 0:512], lhsT, wz[:, k, 0:512], k == 0, k == KD - 1, [hnb, wzb], [pb])
        for k in range(KD):
            lhsT = hn[:, k * TS + sub * 128:k * TS + (sub + 1) * 128]
            K.mm(ps2[:, 0:16], lhsT, wz[:, k, 512:528], k == 0, k == KD - 1, [hnb, wzb], [pb2])
        K.copy(zt[:, 0:512], ps[:, 0:512], [pb], [zb], eng="act")
        K.copy(zt[:, 512:528], ps2[:, 0:16], [pb2], [zb], eng="dve")
        tt0 = t0 + sub * 128
        K.dma_out(S["zt"].ap[tt0:tt0 + 128, :], zt[:, 0:512], zb, wr=S["zt"].bufs(tt0, tt0 + 128))
        K.dma_out(S["abt"].ap[tt0:tt0 + 128, :], zt[:, 512:528], zb, wr=S["abt"].bufs(tt0, tt0 + 128))


def even_consts(K, C, A, evp_ap):
    lc, lcb = A["lc"]
    K.dma_in(lc[:, 0:4096], evp_ap, lcb)
    dv, dvb = A["dv"]

    def half_one(src, np_, o):
        w = src.shape[1]
        K.ts(dv[0:np_, o:o + w], src[0:np_, :], 0.5, ALU.mult, [lcb], [dvb])
        K.ts(dv[0:np_, o + w:o + 2 * w], src[0:np_, :], -1.0, ALU.mult, [lcb], [dvb], s2=1.0, op1=ALU.add)

    half_one(ev(lc, "mu_rkv"), 128, 0)
    half_one(ev(lc, "mu_wd"), 64, 24)
    half_one(ev(lc, "mu_ad"), 64, 26)
    half_one(ev(lc, "mu_gd"), 96, 28)
    K.ts(dv[:, 30:34], ev(lc, "k_a"), -1.0, ALU.mult, [lcb], [dvb], s2=1.0, op1=ALU.add)
    K.ts(dv[:, 34:38], ev(lc, "r_k"), 0.5, ALU.mult, [lcb], [dvb])
    K.act(dv[:, 38:46], ev(lc, "alog"), AF.Exp, [lcb], [dvb])
    K.ts(dv[:, 38:46], dv[:, 38:46], -1.0, ALU.mult, [dvb], [dvb])


def prep_A(K, C, A, G, S, t0, T):
    lc, lcb = A["lc"]
    dv, dvb = A["dv"]
    cw = ev(lc, "cw")
    xin, xb = G["xin"]
    WD = TS + 4
    x3 = xin[:, 0:12 * WD].rearrange("p (c t) -> p c t", c=12)
    lo = max(t0 - 2, 0)
    hi = min(t0 + TS + 2, T)
    if lo > t0 - 2:
        K.memset(x3[:, :, 0:2], 0.0, [xb])
    if hi < t0 + TS + 2:
        K.memset(x3[:, :, TS + 2:TS + 4], 0.0, [xb])
    src = S["rawA"].ap[:, lo:hi].rearrange("(c p) t -> p c t", p=128)
    K.dma_in(x3[:, :, lo - (t0 - 2):hi - (t0 - 2)], src, xb, rd=S["rawA"].bufs(lo, hi))
    kfm, kfb = G["kfm"]
    vfm, vfb = G["vfm"]
    for c in range(12):
        acc, accb = G["acc"].next()
        K.ts(acc[:, 0:TS], x3[:, c, 0:TS], cw[:, c * 5:c * 5 + 1], ALU.mult, [xb, lcb], [accb])
        for k in range(1, 5):
            K.stt(acc[:, 0:TS], x3[:, c, k:k + TS], cw[:, c * 5 + k:c * 5 + k + 1], acc[:, 0:TS], ALU.mult, ALU.add, [xb, lcb, accb], [accb])
        if c >= 8:
            K.act(vfm[:, (c - 8) * TS:(c - 7) * TS], acc[:, 0:TS], AF.Silu, [accb], [vfb])
            continue
        sl, slb = G["acc"].next()
        K.act(sl[:, 0:TS], acc[:, 0:TS], AF.Silu, [accb], [slb])
        sq, sqb = G["sqr"].next()
        K.act(sq[:, 0:TS], sl[:, 0:TS], AF.Square, [slb], [sqb])
        ps, pb = K.psum()
        K.mm(ps[:, 0:TS], C["ones_r"], sq[:, 0:TS], True, True, [sqb, C["buf"]], [pb])
        rs, rb = C["rstd"].next()
        K.act(rs[:, 0:TS], ps[:, 0:TS], AF.Sqrt, [pb, C["buf"]], [rb], bias=C["eps6"], scale=1.0)
        K.recip(rs[:, 0:TS], rs[:, 0:TS], [rb], [rb])
        if c < 4:
            qo, qob = G["acc"].next()
            K.stt(qo[:, 0:TS], sl[:, 0:TS], 128.0 ** -0.5, rs[:, 0:TS], ALU.mult, ALU.mult, [slb, rb], [qob])
            K.dma_out(S["qT"].ap[c * 128:(c + 1) * 128, t0:t0 + TS], qo[:, 0:TS], qob, wr=S["qT"].bufs(t0, t0 + TS))
        else:
            hh = c - 4
            K.tt(kfm[:, hh * TS:(hh + 1) * TS], sl[:, 0:TS], rs[:, 0:TS], ALU.mult, [slb, rb], [kfb])
    K.dma_out(S["kT"].ap[:, t0:t0 + TS].rearrange("(c p) t -> p c t", p=128), kfm[:, 0:4 * TS].rearrange("p (c t) -> p c t", c=4), kfb,
              wr=S["kT"].bufs(t0, t0 + TS))
    for sub in range(TS // 128):
        tt0 = t0 + sub * 128
        for (src_t, src_b, dst) in ((kfm, kfb, S["ktok"]), (vfm, vfb, S["vtok"])):
            ps, pb = K.psum()
            for hh in range(4):
                K.mm(ps[:, hh * 128:(hh + 1) * 128], src_t[:, hh * TS + sub * 128:hh * TS + (sub + 1) * 128], C["ident"], True, True,
                     [src_b, C["buf"]], [pb])
            tk, tkb = G["tok"].next()
            K.copy(tk[:, 0:512], ps[:, 0:512], [pb], [tkb], eng="act")
            K.dma_out(dst.ap[tt0:tt0 + 128, :], tk[:, 0:512], tkb, wr=dst.bufs(tt0, tt0 + 128))
        ab, abb = G["gt"].next()
        K.dma_in(ab[:, 0:16], S["abt"].ap[tt0:tt0 + 128, :], abb, rd=S["abt"].bufs(tt0, tt0 + 128))
        g, gb_ = G["gt"].next()
        w_ = g[:, 16:64]
        K.tt(w_[:, 0:8], ab[:, 0:8], ev(lc, "dtb"), ALU.add, [abb, lcb], [gb_])
        K.ts(w_[:, 8:16], w_[:, 0:8], 0.0, ALU.max, [gb_], [gb_])
        K.act(w_[:, 16:24], w_[:, 0:8], AF.Abs, [gb_], [gb_])
        K.act(w_[:, 16:24], w_[:, 16:24], AF.Exp, [gb_], [gb_], scale=-1.0)
        K.ts(w_[:, 24:32], w_[:, 16:24], 2.0, ALU.add, [gb_], [gb_])
        K.recip(w_[:, 24:32], w_[:, 24:32], [gb_], [gb_])
        K.tt(w_[:, 24:32], w_[:, 24:32], w_[:, 16:24], ALU.mult, [gb_], [gb_])
        K.tt(w_[:, 32:40], w_[:, 24:32], w_[:, 24:32], ALU.mult, [gb_], [gb_])
        K.ts(w_[:, 40:48], w_[:, 32:40], 1.0 / 9, ALU.mult, [gb_], [gb_], s2=1.0 / 7, op1=ALU.add)
        for coef in (1.0 / 5, 1.0 / 3, 1.0):
            K.tt(w_[:, 40:48], w_[:, 40:48], w_[:, 32:40], ALU.mult, [gb_], [gb_])
            K.ts(w_[:, 40:48], w_[:, 40:48], coef, ALU.add, [gb_], [gb_])
        K.tt(w_[:, 40:48], w_[:, 40:48], w_[:, 24:32], ALU.mult, [gb_], [gb_])
        K.stt(w_[:, 40:48], w_[:, 40:48], 2.0, w_[:, 8:16], ALU.mult, ALU.add, [gb_], [gb_])
        K.tt(g[:, 0:8], w_[:, 40:48], dv[:, 38:46], ALU.mult, [gb_, dvb], [gb_])
        K.act(g[:, 8:16], ab[:, 8:16], AF.Sigmoid, [abb], [gb_])
        K.dma_out(S["gbt"].ap[tt0:tt0 + 128, :], g[:, 0:16], gb_, wr=S["gbt"].bufs(tt0, tt0 + 128))


def alloc_prepA(K):
    G = {}
    G["xin"] = (K.sb("pa_xin", [128, 12 * (TS + 4)]), Buf("pa_xin"))
    G["kfm"] = (K.sb("pa_kfm", [128, 4 * TS]), Buf("pa_kfm"))
    G["vfm"] = (K.sb("pa_vfm", [128, 4 * TS]), Buf("pa_vfm"))
    G["acc"] = Ring(K, "pa_acc", [128, TS], F32, 4)
    G["sqr"] = Ring(K, "pa_sq", [128, TS], F32R, 2)
    G["tok"] = Ring(K, "pa_tok", [128, 512], F32, 3)
    G["gt"] = Ring(K, "pa_gt", [128, 64], F32, 4)
    return G


def alloc_scanA(K):
    G = {}
    for nm in ("qT", "kT", "ktok", "vtok"):
        G[nm] = Ring(K, "sa_" + nm, [128, 512], F32, 2)
    for nm in ("of", "zt"):
        G[nm] = Ring(K, "sa_" + nm, [128, 512], F32, 1)
    G["gb"] = Ring(K, "sa_gb", [128, 16], F32, 2)
    G["sm"] = Ring(K, "sa_sm", [128, 64], F32, 2)
    for nm in ("GM", "Dm", "egr", "QT", "tmpD", "vb", "kbg", "us", "wTs", "qkm", "qkTs", "qdT", "vnew", "S", "ot", "sqt", "yts", "kd0", "kd1"):
        G[nm] = (K.sb("sa_" + nm, [128, 512]), Buf("sa_" + nm))
    G["QX"] = (K.sb("sa_QX", [128, 1024]), Buf("sa_QX"))
    return G


def v3(ap, n=4):
    return ap.rearrange("p (h t) -> p h t", h=n)


def bc_last(ap, n=128):
    return ap.unsqueeze(2).to_broadcast([ap.shape[0], ap.shape[1], n])


def bc_mid(ap, k=4):
    return ap.unsqueeze(1).to_broadcast([ap.shape[0], k, ap.shape[1]])


def scan_A(K, C, A, G, S, d, T):
    lc, lcb = A["lc"]
    NT = T // 128
    cb = C["buf"]
    Sst, Sb = G["S"]
    K.memset(Sst[:, 0:512], 0.0, [Sb])
    INCL, AFTER, NEGINCL = C["INCL%d" % d], C["AFTER%d" % d], C["NEGINCL%d" % d]
    order = range(NT) if d == 0 else range(NT - 1, -1, -1)
    corder = (0, 1) if d == 0 else (1, 0)
    for ti in order:
        t0 = ti * 128
        qT, qTb = G["qT"].next()
        kT, kTb = G["kT"].next()
        ktok, ktb = G["ktok"].next()
        vtok, vtb = G["vtok"].next()
        gb, gbb = G["gb"].next()
        K.dma_in(v3(qT[:, 0:512]), S["qT"].ap[:, t0:t0 + 128].rearrange("(h p) t -> p h t", p=128), qTb, rd=S["qT"].bufs(t0, t0 + 128))
        K.dma_in(v3(kT[:, 0:512]), S["kT"].ap[:, t0:t0 + 128].rearrange("(h p) t -> p h t", p=128), kTb, rd=S["kT"].bufs(t0, t0 + 128))
        K.dma_in(ktok[:, 0:512], S["ktok"].ap[t0:t0 + 128, :], ktb, rd=S["ktok"].bufs(t0, t0 + 128))
        K.dma_in(vtok[:, 0:512], S["vtok"].ap[t0:t0 + 128, :], vtb, rd=S["vtok"].bufs(t0, t0 + 128))
        K.dma_in(gb[:, 0:16], S["gbt"].ap[t0:t0 + 128, :], gbb, rd=S["gbt"].bufs(t0, t0 + 128))
        g = gb[:, d * 4:d * 4 + 4]
        beta = gb[:, 8 + d * 4:12 + d * 4]
        sm, smb = G["sm"].next()
        ps1, pb1 = K.psum()
        K.mm(ps1[:, 0:4], INCL, g, True, True, [cb, gbb], [pb1])
        K.mm(ps1[:, 4:8], AFTER, g, True, True, [cb, gbb], [pb1])
        K.mm(ps1[:, 8:12], C["CMROW0"], g, True, True, [cb, gbb], [pb1])
        K.mm(ps1[:, 12:16], C["CMROW1"], g, True, True, [cb, gbb], [pb1])
        K.act(sm[:, 0:16], ps1[:, 0:16], AF.Exp, [pb1], [smb])
        egc, egrev = sm[:, 0:4], sm[:, 4:8]
        K.tt(sm[:, 16:20], beta, egc, ALU.mult, [gbb, smb], [smb])
        K.ts(sm[:, 20:24], beta, -1.0, ALU.mult, [gbb], [smb])
        K.tt(sm[:, 24:32].rearrange("p (c h) -> p c h", c=2), bc_mid(egrev, 2), bc_last(C["CHI"], 4), ALU.mult, [smb, cb], [smb])
        yield
        GM, GMb = G["GM"]
        K.tt(v3(GM[:, 0:512]), bc_mid(INCL), bc_last(g), ALU.mult, [cb, gbb], [GMb])
        pd, pdb = K.psum()
        pg, pgb = K.psum()
        for h in range(4):
            hs = slice(h * 128, (h + 1) * 128)
            K.mm(pd[:, hs], GM[:, hs], C["ones"], True, False, [GMb, cb], [pdb])
            K.mm(pd[:, hs], C["nones"], GM[:, hs], False, False, [GMb, cb], [pdb])
            K.mm(pd[:, hs], C["ident"], NEGINCL, False, True, [cb], [pdb])
            K.mm(pg[:, hs], C["ones"], GM[:, hs], True, True, [GMb, cb], [pgb])
        Dm, Dmb = G["Dm"]
        egr, egrb = G["egr"]
        K.act(Dm[:, 0:512], pd[:, 0:512], AF.Exp, [pdb], [Dmb])
        K.act(egr[:, 0:512], pg[:, 0:512], AF.Exp, [pgb], [egrb])
        yield
        pk, pkb = K.psum()
        for h in range(4):
            hs = slice(h * 128, (h + 1) * 128)
            K.mm(pk[:, hs], kT[:, hs], kT[:, hs], True, True, [kTb], [pkb])
        tmpD, tDb = G["tmpD"]
        K.tt(v3(tmpD[:, 0:512]), v3(Dm[:, 0:512]), bc_mid(C["OFFD"]), ALU.mult, [Dmb, cb], [tDb])
        K.tt(v3(tmpD[:, 0:512]), v3(tmpD[:, 0:512]), bc_last(sm[:, 20:24]), ALU.mult, [tDb, smb], [tDb])
        QT, QTb = G["QT"]
        QX, QXb = G["QX"]
        qx3 = QX[:, 0:1024].rearrange("p (h t) -> p h t", h=4)
        K.tt(QT[:, 0:512], pk[:, 0:512], tmpD[:, 0:512], ALU.mult, [pkb, tDb], [QTb])
        pq0, pq0b = K.psum()
        for h in range(4):
            hs = slice(h * 128, (h + 1) * 128)
            K.mm(pq0[:, hs], QT[:, hs], C["ident"], True, True, [QTb, cb], [pq0b])
        K.copy(qx3[:, :, 0:128], v3(pq0[:, 0:512]), [pq0b], [QXb], eng="act")
        K.copy(qx3[:, :, 128:256], bc_mid(C["ident"]), [cb], [QXb], eng="pool")
        for k in range(6):
            yield
            lastk = k == 5
            pf = [K.psum(), K.psum()]
            for h in range(4):
                pft, pfb = pf[h // 2]
                o0 = (h % 2) * 256
                if lastk:
                    K.mm(pft[:, o0 + 128:o0 + 256], QT[:, h * 128:(h + 1) * 128], qx3[:, h, 128:256], True, True, [QTb, QXb], [pfb])
                else:
                    K.mm(pft[:, o0:o0 + 256], QT[:, h * 128:(h + 1) * 128], qx3[:, h, :], True, True, [QTb, QXb], [pfb])
            if not lastk:
                ptq, ptqb = K.psum()
                for h in range(4):
                    hs = slice(h * 128, (h + 1) * 128)
                    K.mm(ptq[:, hs], qx3[:, h, 0:128], QT[:, hs], True, True, [QTb, QXb], [ptqb])
            for hp in range(2):
                pft, pfb = pf[hp]
                pf3 = pft[:, 0:512].rearrange("p (h t) -> p h t", h=2)
                if not lastk:
                    K.copy(qx3[:, 2 * hp:2 * hp + 2, 0:128], pf3[:, :, 0:128], [pfb], [QXb], eng="act")
                K.tt(qx3[:, 2 * hp:2 * hp + 2, 128:256], qx3[:, 2 * hp:2 * hp + 2, 128:256], pf3[:, :, 128:256], ALU.add, [pfb, QXb], [QXb])
            if not lastk:
                K.copy(QT[:, 0:512], ptq[:, 0:512], [ptqb], [QTb], eng="act")
        yield
        vb, vbb = G["vb"]
        kbg, kbgb = G["kbg"]
        K.tt(v3(vb[:, 0:512]), v3(vtok[:, 0:512]), bc_last(beta), ALU.mult, [vtb, gbb], [vbb])
        K.tt(v3(kbg[:, 0:512]), v3(ktok[:, 0:512]), bc_last(sm[:, 16:20]), ALU.mult, [ktb, smb], [kbgb])
        pu, pub = K.psum()
        pw, pwb = K.psum()
        pqk, pqkb = K.psum()
        for h in range(4):
            hs = slice(h * 128, (h + 1) * 128)
            K.mm(pu[:, hs], qx3[:, h, 128:256], vb[:, hs], True, True, [QXb, vbb], [pub])
            K.mm(pw[:, hs], kbg[:, hs], qx3[:, h, 128:256], True, True, [QXb, kbgb], [pwb])
            K.mm(pqk[:, hs], qT[:, hs], kT[:, hs], True, True, [qTb, kTb], [pqkb])
        us, usb = G["us"]
        wTs, wTb = G["wTs"]
        qkm, qkmb = G["qkm"]
        K.copy(us[:, 0:512], pu[:, 0:512], [pub], [usb], eng="act")
        K.copy(wTs[:, 0:512], pw[:, 0:512], [pwb], [wTb], eng="act")
        K.tt(qkm[:, 0:512], pqk[:, 0:512], Dm[:, 0:512], ALU.mult, [pqkb, Dmb], [qkmb])
        yield
        pqt, pqtb = K.psum()
        for h in range(4):
            hs = slice(h * 128, (h + 1) * 128)
            K.mm(pqt[:, hs], qkm[:, hs], C["ident"], True, True, [qkmb, cb], [pqtb])
        qkTs, qkTb = G["qkTs"]
        K.copy(qkTs[:, 0:512], pqt[:, 0:512], [pqtb], [qkTb], eng="act")
        qdT, qdTb = G["qdT"]
        K.tt(qdT[:, 0:512], qT[:, 0:512], egr[:, 0:512], ALU.mult, [qTb, egrb], [qdTb], eng="pool")
        kd = [G["kd0"], G["kd1"]]
        for cc in range(2):
            kdt, kdb = kd[cc]
            K.tt(v3(kdt[:, 0:512]), v3(ktok[:, 0:512]), bc_last(sm[:, 24 + cc * 4:28 + cc * 4]), ALU.mult, [ktb, smb], [kdb], eng="pool")
        vnew, vnb = G["vnew"]
        ot, otb = G["ot"]
        for cc in corder:
            yield
            kdt, kdb = kd[cc]
            pws, pwsb = K.psum()
            for h in range(4):
                hs = slice(h * 128, (h + 1) * 128)
                K.mm(pws[:, hs], wTs[:, hs], Sst[:, hs], True, True, [wTb, Sb], [pwsb])
            K.tt(vnew[:, 0:512], us[:, 0:512], pws[:, 0:512], ALU.subtract, [usb, pwsb], [vnb])
            yield
            po, pob = K.psum()
            pss, pssb = K.psum()
            for h in range(4):
                hs = slice(h * 128, (h + 1) * 128)
                K.mm(po[:, hs], qdT[:, hs], Sst[:, hs], True, False, [qdTb, Sb], [pob])
                K.mm(po[:, hs], qkTs[:, hs], vnew[:, hs], False, True, [qkTb, vnb], [pob])
                K.mm(pss[:, hs], kdt[:, hs], vnew[:, hs], True, True, [kdb, vnb], [pssb])
            rows = slice(cc * 64, (cc + 1) * 64)
            K.copy(ot[rows, 0:512], po[rows, 0:512], [pob], [otb], eng="act")
            K.tt(v3(Sst[:, 0:512]), v3(Sst[:, 0:512]), bc_last(sm[:, 8 + cc * 4:12 + cc * 4]), ALU.mult, [Sb, smb], [Sb])
            K.tt(Sst[:, 0:512], Sst[:, 0:512], pss[:, 0:512], ALU.add, [Sb, pssb], [Sb])
        yield
        if d == 0:
            K.dma_out(S["oAf"].ap[t0:t0 + 128, :], ot[:, 0:512], otb, wr=S["oAf"].bufs(t0, t0 + 128))
        else:
            of, ofb = G["of"].next()
            zt, ztb = G["zt"].next()
            K.dma_in(of[:, 0:512], S["oAf"].ap[t0:t0 + 128, :], ofb, rd=S["oAf"].bufs(t0, t0 + 128))
            K.dma_in(zt[:, 0:512], S["zt"].ap[t0:t0 + 128, :], ztb, rd=S["zt"].bufs(t0, t0 + 128))
            K.tt(ot[:, 0:512], ot[:, 0:512], of[:, 0:512], ALU.add, [otb, ofb], [otb])
            sqt, sqtb = G["sqt"]
            K.act(sqt[:, 0:512], ot[:, 0:512], AF.Square, [otb], [sqtb])
            sm2, sm2b = G["sm"].next()
            K.P.add("dve", lambda e, o_=sm2[:, 0:4], i_=v3(sqt[:, 0:512]): e.tensor_reduce(out=o_, in_=i_, axis=AX.X, op=ALU.add),
                    reads=[sqtb], writes=[sm2b])
            K.act(sm2[:, 0:4], sm2[:, 0:4], AF.Sqrt, [sm2b, cb], [sm2b], bias=C["eps6"], scale=1.0 / 128)
            K.recip(sm2[:, 0:4], sm2[:, 0:4], [sm2b], [sm2b])
            K.tt(v3(ot[:, 0:512]), v3(ot[:, 0:512]), bc_last(sm2[:, 0:4]), ALU.mult, [otb, sm2b], [otb])
            K.tt(ot[:, 0:512], ot[:, 0:512], ev(lc, "anorm"), ALU.mult, [otb, lcb], [otb])
            K.act(sqt[:, 0:512], zt[:, 0:512], AF.Silu, [ztb], [sqtb])
            K.tt(ot[:, 0:512], ot[:, 0:512], sqt[:, 0:512], ALU.mult, [otb, sqtb], [otb])
            py, pyb = K.psum()
            for h in range(4):
                hs = slice(h * 128, (h + 1) * 128)
                K.mm(py[:, hs], ot[:, hs], C["ident"], True, True, [otb, cb], [pyb])
            yts, ytsb = G["yts"]
            K.copy(yts[:, 0:512], py[:, 0:512], [pyb], [ytsb], eng="act")
            K.dma_out(S["yT"].ap[0:512, t0:t0 + 128].rearrange("(h p) t -> p h t", p=128), v3(yts[:, 0:512]), ytsb,
                      wr=S["yT"].bufs(t0, t0 + 128))


SCALE_W = 0.606531


def alloc_prepB(K):
    G = {}
    G["xin"] = (K.sb("pb_xin", [128, 12 * (TS + 2)]), Buf("pb_xin"))
    G["xs"] = (K.sb("pb_xs", [128, 3 * (TS + 2)]), Buf("pb_xs"))
    G["rkv"] = (K.sb("pb_rkv", [128, 12 * TS]), Buf("pb_rkv"))
    G["sml"] = (K.sb("pb_sml", [128, 3 * TS]), Buf("pb_sml"))
    G["kk"] = (K.sb("pb_kk", [128, 4 * TS]), Buf("pb_kk"))
    for d in range(2):
        G["bT%d" % d] = (K.sb("pb_bT%d" % d, [128, 4 * TS]), Buf("pb_bT%d" % d))
        G["kdT%d" % d] = (K.sb("pb_kdT%d" % d, [128, 4 * TS]), Buf("pb_kdT%d" % d))
    G["prod"] = (K.sb("pb_prod", [128, 4 * TS]), Buf("pb_prod"))
    G["tmp"] = Ring(K, "pb_tmp", [128, TS], F32, 4)
    G["sqr"] = Ring(K, "pb_sq", [128, TS], F32R, 2)
    G["tok"] = Ring(K, "pb_tok", [128, 512], F32, 4)
    G["t8"] = Ring(K, "pb_t8", [128, 8], F32, 2)
    return G


def prep_B(K, C, A, G, S, t0, T):
    lc, lcb = A["lc"]
    dv, dvb = A["dv"]
    cb = C["buf"]
    WD = TS + 2
    xin, xb = G["xin"]
    xs, xsb = G["xs"]
    x3 = xin[:, 0:12 * WD].rearrange("p (c t) -> p c t", c=12)
    s3 = xs[:, 0:3 * WD].rearrange("p (c t) -> p c t", c=3)
    lo = max(t0 - 1, 0)
    hi = min(t0 + TS + 1, T)
    if lo > t0 - 1:
        K.memset(x3[:, :, 0:1], 0.0, [xb])
        K.memset(s3[:, :, 0:1], 0.0, [xsb])
    if hi < t0 + TS + 1:
        K.memset(x3[:, :, TS + 1:TS + 2], 0.0, [xb])
        K.memset(s3[:, :, TS + 1:TS + 2], 0.0, [xsb])
    o0, o1 = lo - (t0 - 1), hi - (t0 - 1)
    rd = S["rawB"].bufs(lo, hi)
    K.dma_in(x3[:, :, o0:o1], S["rawB"].ap[0:1536, lo:hi].rearrange("(c p) t -> p c t", p=128), xb, rd=rd)
    K.dma_in(s3[0:64, 0, o0:o1], S["rawB"].ap[1536:1600, lo:hi], xsb, rd=rd)
    K.dma_in(s3[0:64, 1, o0:o1], S["rawB"].ap[1600:1664, lo:hi], xsb, rd=rd)
    K.dma_in(s3[0:96, 2, o0:o1], S["rawB"].ap[1664:1760, lo:hi], xsb, rd=rd)
    rkv, rkvb = G["rkv"]
    sml, smlb = G["sml"]

    def lerp(dst, dstb, src3, c, np_, hmu, omu, srcb):
        tmp, tb = G["tmp"].next()
        K.tt(tmp[0:np_, 0:TS], src3[0:np_, c, 0:TS], src3[0:np_, c, 2:TS + 2], ALU.add, [srcb], [tb], eng="pool")
        K.ts(tmp[0:np_, 0:TS], tmp[0:np_, 0:TS], hmu, ALU.mult, [tb, dvb], [tb])
        K.stt(dst, src3[0:np_, c, 1:TS + 1], omu, tmp[0:np_, 0:TS], ALU.mult, ALU.add, [srcb, tb, dvb], [dstb])

    for c in range(12):
        lerp(rkv[:, c * TS:(c + 1) * TS], rkvb, x3, c, 128, dv[:, c:c + 1], dv[:, 12 + c:13 + c], xb)
    lerp(sml[0:64, 0:TS], smlb, s3, 0, 64, dv[0:64, 24:25], dv[0:64, 25:26], xsb)
    lerp(sml[0:64, TS:2 * TS], smlb, s3, 1, 64, dv[0:64, 26:27], dv[0:64, 27:28], xsb)
    lerp(sml[0:96, 2 * TS:3 * TS], smlb, s3, 2, 96, dv[0:96, 28:29], dv[0:96, 29:30], xsb)
    K.act(sml[0:64, 0:TS], sml[0:64, 0:TS], AF.Tanh, [smlb], [smlb])
    K.act(sml[0:96, 2 * TS:3 * TS], sml[0:96, 2 * TS:3 * TS], AF.Sigmoid, [smlb], [smlb])
    twd = sml[:, 0:TS]
    adl = sml[:, TS:2 * TS]
    sgd = sml[:, 2 * TS:3 * TS]
    rT = rkv[:, 0:4 * TS]
    kT = rkv[:, 4 * TS:8 * TS]
    vT = rkv[:, 8 * TS:12 * TS]
    K.dma_out(S["rTB"].ap[:, t0:t0 + TS].rearrange("(c p) t -> p c t", p=128), rT.rearrange("p (c t) -> p c t", c=4), rkvb,
              wr=S["rTB"].bufs(t0, t0 + TS))
    kk, kkb = G["kk"]
    for c in range(4):
        t1, t1b = G["tmp"].next()
        K.ts(t1[:, 0:TS], kT[:, c * TS:(c + 1) * TS], ev(lc, "k_k")[:, c:c + 1], ALU.mult, [rkvb, lcb], [t1b])
        sq, sqb = G["sqr"].next()
        K.act(sq[:, 0:TS], t1[:, 0:TS], AF.Square, [t1b], [sqb])
        ps, pb = K.psum()
        K.mm(ps[:, 0:TS], C["blk_r"], sq[:, 0:TS], True, True, [sqb, cb], [pb])
        rs, rb = C["rstd"].next()
        K.act(rs[:, 0:TS], ps[:, 0:TS], AF.Sqrt, [pb, cb], [rb], bias=C["eps6"], scale=1.0)
        K.recip(rs[:, 0:TS], rs[:, 0:TS], [rb], [rb])
        K.tt(kk[:, c * TS:(c + 1) * TS], t1[:, 0:TS], rs[:, 0:TS], ALU.mult, [t1b, rb], [kkb])
    K.dma_out(S["kkT"].ap[:, t0:t0 + TS].rearrange("(c p) t -> p c t", p=128), kk[:, 0:4 * TS].rearrange("p (c t) -> p c t", c=4), kkb,
              wr=S["kkT"].bufs(t0, t0 + TS))
    a2s, a0, k_a = ev(lc, "a2s"), ev(lc, "a0"), ev(lc, "k_a")
    for d in range(2):
        bT, bTb = G["bT%d" % d]
        kdT, kdTb = G["kdT%d" % d]
        pr = slice(32 * d, 32 * d + 32)
        for c in range(4):
            ps, pb = K.psum()
            K.mm(ps[:, 0:TS], a2s[pr, c * 128:(c + 1) * 128], adl[pr, 0:TS], True, True, [lcb, smlb], [pb])
            al, alb = G["tmp"].next()
            K.act(al[:, 0:TS], ps[:, 0:TS], AF.Sigmoid, [pb, lcb], [alb], bias=a0[:, d * 4 + c:d * 4 + c + 1], scale=1.0)
            K.tt(bT[:, c * TS:(c + 1) * TS], kk[:, c * TS:(c + 1) * TS], al[:, 0:TS], ALU.mult, [kkb, alb], [bTb])
            K.ts(al[:, 0:TS], al[:, 0:TS], k_a[:, c:c + 1], ALU.mult, [alb, lcb, dvb], [alb], s2=dv[:, 30 + c:31 + c], op1=ALU.add)
            K.tt(kdT[:, c * TS:(c + 1) * TS], kT[:, c * TS:(c + 1) * TS], al[:, 0:TS], ALU.mult, [rkvb, alb], [kdTb])
        K.dma_out(S["bT%d" % d].ap[:, t0:t0 + TS].rearrange("(c p) t -> p c t", p=128), bT[:, 0:4 * TS].rearrange("p (c t) -> p c t", c=4), bTb,
                  wr=S["bT%d" % d].bufs(t0, t0 + TS))
        K.dma_out(S["kdT%d" % d].ap[:, t0:t0 + TS].rearrange("(c p) t -> p c t", p=128), kdT[:, 0:4 * TS].rearrange("p (c t) -> p c t", c=4), kdTb,
                  wr=S["kdT%d" % d].bufs(t0, t0 + TS))
    prod, prb = G["prod"]
    kd0, kd0b = G["kdT0"]
    kd1, kd1b = G["kdT1"]
    for c in range(4):
        cs = slice(c * TS, (c + 1) * TS)
        K.tt(prod[:, cs], kd0[:, cs], kd1[:, cs], ALU.add, [kd0b, kd1b], [prb], eng="pool")
        K.stt(prod[:, cs], prod[:, cs], dv[:, 34 + c:35 + c], rT[:, cs], ALU.mult, ALU.mult, [prb, dvb, rkvb], [prb])
    w2s, w0s, g2 = ev(lc, "w2s"), ev(lc, "w0s"), ev(lc, "g2")
    for sub in range(TS // 128):
        tt0 = t0 + sub * 128
        cols = slice(sub * 128, (sub + 1) * 128)
        trans = [(vT, rkvb, S["vtokB"])]
        for d in range(2):
            trans.append((G["bT%d" % d][0], G["bT%d" % d][1], S["btok%d" % d]))
            trans.append((G["kdT%d" % d][0], G["kdT%d" % d][1], S["kdtok%d" % d]))
        for i_, (src_t, src_b, dst) in enumerate(trans):
            ps, pb = K.psum()
            for c in range(4):
                K.mm(ps[:, c * 128:(c + 1) * 128], src_t[:, c * TS + sub * 128:c * TS + (sub + 1) * 128], C["ident"], True, True, [src_b, cb], [pb])
            tk, tkb = G["tok"].next()
            K.copy(tk[:, 0:512], ps[:, 0:512], [pb], [tkb], eng="act" if i_ % 2 else "dve")
            K.dma_out(dst.ap[tt0:tt0 + 128, :], tk[:, 0:512], tkb, wr=dst.bufs(tt0, tt0 + 128))
        for d in range(2):
            pr = slice(32 * d, 32 * d + 32)
            ps, pb = K.psum()
            K.mm(ps[:, 0:512], twd[pr, cols], w2s[pr, :], True, False, [smlb, lcb], [pb])
            K.mm(ps[:, 0:512], C["ones"][32 * d:32 * d + 1, :], w0s[32 * d:32 * d + 1, :], False, True, [cb, lcb], [pb])
            tk, tkb = G["tok"].next()
            K.act(tk[:, 0:512], ps[:, 0:512], AF.Sigmoid, [pb], [tkb])
            K.dma_out(S["sg%d" % d].ap[tt0:tt0 + 128, :], tk[:, 0:512], tkb, wr=S["sg%d" % d].bufs(tt0, tt0 + 128))
        ps, pb = K.psum()
        K.mm(ps[:, 0:512], sgd[0:96, cols], g2[0:96, :], True, True, [smlb, lcb], [pb])
        tk, tkb = G["tok"].next()
        K.copy(tk[:, 0:512], ps[:, 0:512], [pb], [tkb], eng="act")
        K.dma_out(S["gtok"].ap[tt0:tt0 + 128, :], tk[:, 0:512], tkb, wr=S["gtok"].bufs(tt0, tt0 + 128))
        ps, pb = K.psum()
        for c in range(4):
            K.mm(ps[:, 2 * c:2 * c + 2], prod[:, c * TS + sub * 128:c * TS + (sub + 1) * 128], C["HSEL"], True, True, [prb, cb], [pb])
        t8, t8b = G["t8"].next()
        K.copy(t8[:, 0:8], ps[:, 0:8], [pb], [t8b], eng="dve")
        K.dma_out(S["bons"].ap[tt0:tt0 + 128, :], t8[:, 0:8], t8b, wr=S["bons"].bufs(tt0, tt0 + 128))


def alloc_scanB(K):
    G = {}
    for nm in ("sg", "btok", "kdtok", "vtok"):
        G[nm] = Ring(K, "sb_" + nm, [128, 512], F32, 2)
    for nm in ("yf", "gt"):
        G[nm] = Ring(K, "sb_" + nm, [128, 512], F32, 1)
    for nm in ("rT", "kkT", "bT", "kdT"):
        G[nm] = Ring(K, "sb_" + nm, [128, 1024], F32, 1)
    for nm in ("Ei", "Ev", "Ex", "rt", "at", "bt", "kt"):
        G[nm] = (K.sb("sb_" + nm, [128, 1024]), Buf("sb_" + nm))
    G["bon"] = Ring(K, "sb_bon", [128, 8], F32, 2)
    G["sm"] = Ring(K, "sb_sm", [128, 64], F32, 2)
    for nm in ("Er", "bg", "kg", "bgm", "kgm", "Xs", "Us", "S", "yt", "sqt", "yts",
               "QTa", "QTb", "AakA", "AakB", "ArbA", "ArbB", "ArkA", "ArkB"):
        G[nm] = (K.sb("sb_" + nm, [128, 512]), Buf("sb_" + nm))
    G["QXa"] = (K.sb("sb_QXa", [128, 1024]), Buf("sb_QXa"))
    G["QXb"] = (K.sb("sb_QXb", [128, 1024]), Buf("sb_QXb"))
    return G


def invert4(K, C, QT, QTb, QX, QXb):
    cb = C["buf"]
    qx3 = QX[:, 0:1024].rearrange("p (h t) -> p h t", h=4)
    pq0, pq0b = K.psum()
    for h in range(4):
        hs = slice(h * 128, (h + 1) * 128)
        K.mm(pq0[:, hs], QT[:, hs], C["ident"], True, True, [QTb, cb], [pq0b])
    K.copy(qx3[:, :, 0:128], v3(pq0[:, 0:512]), [pq0b], [QXb], eng="act")
    K.copy(qx3[:, :, 128:256], bc_mid(C["ident"]), [cb], [QXb], eng="pool")
    for k in range(6):
        yield
        lastk = k == 5
        pf = [K.psum(), K.psum()]
        for h in range(4):
            pft, pfb = pf[h // 2]
            o0 = (h % 2) * 256
            if lastk:
                K.mm(pft[:, o0 + 128:o0 + 256], QT[:, h * 128:(h + 1) * 128], qx3[:, h, 128:256], True, True, [QTb, QXb], [pfb])
            else:
                K.mm(pft[:, o0:o0 + 256], QT[:, h * 128:(h + 1) * 128], qx3[:, h, :], True, True, [QTb, QXb], [pfb])
        if not lastk:
            ptq, ptqb = K.psum()
            for h in range(4):
                hs = slice(h * 128, (h + 1) * 128)
                K.mm(ptq[:, hs], qx3[:, h, 0:128], QT[:, hs], True, True, [QTb, QXb], [ptqb])
        for hp in range(2):
            pft, pfb = pf[hp]
            pf3 = pft[:, 0:512].rearrange("p (h t) -> p h t", h=2)
            if not lastk:
                K.copy(qx3[:, 2 * hp:2 * hp + 2, 0:128], pf3[:, :, 0:128], [pfb], [QXb], eng="act")
            K.tt(qx3[:, 2 * hp:2 * hp + 2, 128:256], qx3[:, 2 * hp:2 * hp + 2, 128:256], pf3[:, :, 128:256], ALU.add, [pfb, QXb], [QXb])
        if not lastk:
            K.copy(QT[:, 0:512], ptq[:, 0:512], [ptqb], [QTb], eng="act")
    return qx3


def scan_B(K, C, A, G, S, d, T):
    lc, lcb = A["lc"]
    NT = T // 128
    cb = C["buf"]
    Sst, Sb = G["S"]
    K.memset(Sst[:, 0:512], 0.0, [Sb])
    INCL, STRICT, AFTER = C["INCL%d" % d], C["STRICT%d" % d], C["AFTER%d" % d]
    order = range(NT) if d == 0 else range(NT - 1, -1, -1)
    corder = (0, 1) if d == 0 else (1, 0)
    for ti in order:
        t0 = ti * 128
        ld = {}
        for nm, dt_, fm in (("sg", S["sg%d" % d], False), ("rT", S["rTB"], True), ("kkT", S["kkT"], True), ("bT", S["bT%d" % d], True),
                            ("kdT", S["kdT%d" % d], True), ("btok", S["btok%d" % d], False), ("kdtok", S["kdtok%d" % d], False),
                            ("vtok", S["vtokB"], False)):
            tl, tb = G[nm].next()
            if fm:
                K.dma_in(tl[0:64, 0:1024].rearrange("p (h t) -> p h t", h=8), dt_.ap[:, t0:t0 + 128].rearrange("(h p) t -> p h t", p=64), tb,
                         rd=dt_.bufs(t0, t0 + 128))
            else:
                K.dma_in(tl[:, 0:512], dt_.ap[t0:t0 + 128, :], tb, rd=dt_.bufs(t0, t0 + 128))
            ld[nm] = (tl, tb)
        sg, sgb = ld["sg"]
        rT, rTb = ld["rT"]
        kkT, kkTb = ld["kkT"]
        bT, bTb = ld["bT"]
        kdT, kdTb = ld["kdT"]
        btok, btokb = ld["btok"]
        kdtok, kdtokb = ld["kdtok"]
        vtok, vtb = ld["vtok"]
        Ei, Eib = G["Ei"]
        Ev, Evb = G["Ev"]
        Ex, Exb = G["Ex"]
        for q4 in range(4):
            pa_, pab = K.psum()
            for h2 in range(2):
                h = q4 * 2 + h2
                K.mm(pa_[0:64, h2 * 256:h2 * 256 + 128], sg[:, h * 64:(h + 1) * 64], INCL, True, True, [sgb, cb], [pab])
                K.mm(pa_[0:64, h2 * 256 + 128:h2 * 256 + 256], sg[:, h * 64:(h + 1) * 64], STRICT, True, True, [sgb, cb], [pab])
            pa3 = pa_[0:64, 0:512].rearrange("p (c t) -> p c t", c=2)
            hs = slice(q4 * 256, (q4 + 1) * 256)
            K.act(Ei[0:64, hs].rearrange("p (c t) -> p c t", c=2), pa3[:, :, 0:128], AF.Exp, [pab], [Eib], scale=-SCALE_W)
            K.act(Ev[0:64, hs].rearrange("p (c t) -> p c t", c=2), pa3[:, :, 0:128], AF.Exp, [pab], [Evb], scale=SCALE_W)
            K.act(Ex[0:64, hs].rearrange("p (c t) -> p c t", c=2), pa3[:, :, 128:256], AF.Exp, [pab], [Exb], scale=-SCALE_W)
        sm, smb = G["sm"].next()
        pc, pcb = K.psum()
        for h in range(8):
            K.mm(pc[0:64, 2 * h:2 * h + 2], sg[:, h * 64:(h + 1) * 64], C["CHI"], True, True, [sgb, cb], [pcb])
        K.act(sm[0:64, 0:16], pc[0:64, 0:16], AF.Exp, [pcb], [smb], scale=-SCALE_W)
        yield
        prv, prvb = K.psum()
        K.mm(prv[:, 0:512], AFTER, sg[:, 0:512], True, True, [sgb, cb], [prvb])
        Er, Erb = G["Er"]
        K.act(Er[:, 0:512], prv[:, 0:512], AF.Exp, [prvb], [Erb], scale=-SCALE_W)
        rt, rtb = G["rt"]
        at, atb = G["at"]
        bt, btb = G["bt"]
        kt, ktb_ = G["kt"]
        K.tt(rt[0:64, 0:1024], rT[0:64, 0:1024], Ei[0:64, 0:1024], ALU.mult, [rTb, Eib], [rtb])
        K.stt(at[0:64, 0:1024], kkT[0:64, 0:1024], -1.0, Ex[0:64, 0:1024], ALU.mult, ALU.mult, [kkTb, Exb], [atb])
        K.tt(bt[0:64, 0:1024], bT[0:64, 0:1024], Ev[0:64, 0:1024], ALU.mult, [bTb, Evb], [btb], eng="pool")
        K.tt(kt[0:64, 0:1024], kdT[0:64, 0:1024], Ev[0:64, 0:1024], ALU.mult, [kdTb, Evb], [ktb_], eng="pool")
        bg, bgb = G["bg"]
        kg, kgb = G["kg"]
        K.tt(bg[:, 0:512], btok[:, 0:512], Er[:, 0:512], ALU.mult, [btokb, Erb], [bgb])
        K.tt(kg[:, 0:512], kdtok[:, 0:512], Er[:, 0:512], ALU.mult, [kdtokb, Erb], [kgb], eng="pool")
        def hsl(t_, h):
            return t_[0:64, h * 128:(h + 1) * 128]

        QTs = [G["QTa"], G["QTb"]]
        Aak = [G["AakA"], G["AakB"]]
        Arb = [G["ArbA"], G["ArbB"]]
        Ark = [G["ArkA"], G["ArkB"]]
        for grp in range(2):
            yield
            for (dst, lh, lhb, rh, rhb, mask) in ((QTs[grp], at, atb, bt, btb, AFTER), (Aak[grp], kt, ktb_, at, atb, STRICT),
                                                  (Arb[grp], bt, btb, rt, rtb, INCL), (Ark[grp], kt, ktb_, rt, rtb, INCL)):
                ps, pb = K.psum()
                for h4 in range(4):
                    h = grp * 4 + h4
                    K.mm(ps[:, h4 * 128:(h4 + 1) * 128], hsl(lh, h), hsl(rh, h), True, True, [lhb, rhb], [pb])
                K.tt(v3(dst[0][:, 0:512]), v3(ps[:, 0:512]), bc_mid(mask), ALU.mult, [pb, cb], [dst[1]])
        QX = [G["QXa"], G["QXb"]]
        X3 = []
        for grp in range(2):
            x3_ = yield from invert4(K, C, QTs[grp][0], QTs[grp][1], QX[grp][0], QX[grp][1])
            X3.append(x3_)
        Xs, Xsb = G["Xs"]
        Us, Usb = G["Us"]
        yt, ytb = G["yt"]
        bgm, bgmb = G["bgm"]
        kgm, kgmb = G["kgm"]
        for cc in corder:
            yield
            K.ts(bgm[:, 0:512], bg[:, 0:512], C["CHI"][:, cc:cc + 1], ALU.mult, [bgb, cb], [bgmb])
            K.ts(kgm[:, 0:512], kg[:, 0:512], C["CHI"][:, cc:cc + 1], ALU.mult, [kgb, cb], [kgmb])
            px, pxb = K.psum()
            for h in range(8):
                vs = slice(h * 64, (h + 1) * 64)
                K.mm(px[:, vs], hsl(at, h), Sst[0:64, vs], True, False, [atb, Sb], [pxb])
                K.mm(px[:, vs], Aak[h // 4][0][:, (h % 4) * 128:(h % 4 + 1) * 128], vtok[:, vs], False, True, [Aak[h // 4][1], vtb], [pxb])
            K.copy(Xs[:, 0:512], px[:, 0:512], [pxb], [Xsb], eng="act")
            yield
            pu, pub = K.psum()
            for h in range(8):
                vs = slice(h * 64, (h + 1) * 64)
                K.mm(pu[:, vs], X3[h // 4][:, h % 4, 128:256], Xs[:, vs], True, True, [QX[h // 4][1], Xsb], [pub])
            K.copy(Us[:, 0:512], pu[:, 0:512], [pub], [Usb], eng="act")
            yield
            py, pyb = K.psum()
            for h in range(8):
                vs = slice(h * 64, (h + 1) * 64)
                hs = slice((h % 4) * 128, (h % 4 + 1) * 128)
                K.mm(py[:, vs], hsl(rt, h), Sst[0:64, vs], True, False, [rtb, Sb], [pyb])
                K.mm(py[:, vs], Arb[h // 4][0][:, hs], Us[:, vs], False, False, [Arb[h // 4][1], Usb], [pyb])
                K.mm(py[:, vs], Ark[h // 4][0][:, hs], vtok[:, vs], False, True, [Ark[h // 4][1], vtb], [pyb])
            rows = slice(cc * 64, (cc + 1) * 64)
            K.copy(yt[rows, 0:512], py[rows, 0:512], [pyb], [ytb], eng="act")
            yield
            pS, pSb = K.psum()
            for h in range(8):
                vs = slice(h * 64, (h + 1) * 64)
                K.mm(pS[0:64, vs], bgm[:, vs], Us[:, vs], True, False, [bgmb, Usb], [pSb])
                K.mm(pS[0:64, vs], kgm[:, vs], vtok[:, vs], False, True, [kgmb, vtb], [pSb])
            egl = sm[0:64, 0:16].rearrange("p (h e) -> p h e", h=8)[:, :, cc:cc + 1].to_broadcast([64, 8, 64])
            s3 = Sst[0:64, 0:512].rearrange("p (h v) -> p h v", h=8)
            K.tt(s3, s3, egl, ALU.mult, [Sb, smb], [Sb])
            K.tt(Sst[0:64, 0:512], Sst[0:64, 0:512], pS[0:64, 0:512], ALU.add, [Sb, pSb], [Sb])
        yield
        if d == 0:
            K.dma_out(S["yBf"].ap[t0:t0 + 128, :], yt[:, 0:512], ytb, wr=S["yBf"].bufs(t0, t0 + 128))
        else:
            yf, yfb = G["yf"].next()
            gt, gtb = G["gt"].next()
            bon, bonb = G["bon"].next()
            K.dma_in(yf[:, 0:512], S["yBf"].ap[t0:t0 + 128, :], yfb, rd=S["yBf"].bufs(t0, t0 + 128))
            K.dma_in(gt[:, 0:512], S["gtok"].ap[t0:t0 + 128, :], gtb, rd=S["gtok"].bufs(t0, t0 + 128))
            K.dma_in(bon[:, 0:8], S["bons"].ap[t0:t0 + 128, :], bonb, rd=S["bons"].bufs(t0, t0 + 128))
            K.tt(yt[:, 0:512], yt[:, 0:512], yf[:, 0:512], ALU.add, [ytb, yfb], [ytb])
            y8 = yt[:, 0:512].rearrange("p (h v) -> p h v", h=8)
            sqt, sqtb = G["sqt"]
            sm2, sm2b = G["sm"].next()
            K.P.add("dve", lambda e, o_=sm2[:, 0:8], i_=y8: e.tensor_reduce(out=o_, in_=i_, axis=AX.X, op=ALU.add), reads=[ytb], writes=[sm2b])
            K.act(sqt[:, 0:512], yt[:, 0:512], AF.Square, [ytb], [sqtb])
            K.P.add("dve", lambda e, o_=sm2[:, 8:16], i_=sqt[:, 0:512].rearrange("p (h v) -> p h v", h=8): e.tensor_reduce(out=o_, in_=i_, axis=AX.X, op=ALU.add),
                    reads=[sqtb], writes=[sm2b])
            K.ts(sm2[:, 16:24], sm2[:, 0:8], 1.0 / 64, ALU.mult, [sm2b], [sm2b])
            K.tt(sm2[:, 24:32], sm2[:, 16:24], sm2[:, 16:24], ALU.mult, [sm2b], [sm2b])
            K.stt(sm2[:, 32:40], sm2[:, 8:16], 1.0 / 64, sm2[:, 24:32], ALU.mult, ALU.subtract, [sm2b], [sm2b])
            K.act(sm2[:, 40:48], sm2[:, 32:40], AF.Sqrt, [sm2b, cb], [sm2b], bias=C["epsgn"], scale=1.0)
            K.recip(sm2[:, 40:48], sm2[:, 40:48], [sm2b], [sm2b])
            K.tt(y8, y8, bc_last(sm2[:, 16:24], 64), ALU.subtract, [ytb, sm2b], [ytb])
            K.tt(y8, y8, bc_last(sm2[:, 40:48], 64), ALU.mult, [ytb, sm2b], [ytb])
            K.tt(yt[:, 0:512], yt[:, 0:512], ev(lc, "lnw"), ALU.mult, [ytb, lcb], [ytb])
            K.tt(yt[:, 0:512], yt[:, 0:512], ev(lc, "lnb"), ALU.add, [ytb, lcb], [ytb])
            K.tt(sqt[:, 0:512].rearrange("p (h v) -> p h v", h=8), vtok[:, 0:512].rearrange("p (h v) -> p h v", h=8), bc_last(bon[:, 0:8], 64),
                 ALU.mult, [vtb, bonb], [sqtb], eng="pool")
            K.tt(yt[:, 0:512], yt[:, 0:512], sqt[:, 0:512], ALU.add, [ytb, sqtb], [ytb])
            K.tt(yt[:, 0:512], yt[:, 0:512], gt[:, 0:512], ALU.mult, [ytb, gtb], [ytb])
            pt_, ptb = K.psum()
            for c in range(4):
                cs = slice(c * 128, (c + 1) * 128)
                K.mm(pt_[:, cs], yt[:, cs], C["ident"], True, True, [ytb, cb], [ptb])
            yts, ytsb = G["yts"]
            K.copy(yts[:, 0:512], pt_[:, 0:512], [ptb], [ytsb], eng="act")
            K.dma_out(S["yT"].ap[512:1024, t0:t0 + 128].rearrange("(h p) t -> p h t", p=128), v3(yts[:, 0:512]), ytsb,
                      wr=S["yT"].bufs(t0, t0 + 128))
SCRATCH = {
    "hT": (D, None), "rawA": (1536, None), "rawB": (1760, None), "zt": (None, 512), "abt": (None, 16), "gbt": (None, 16),
    "qT": (512, None), "kT": (512, None), "ktok": (None, 512), "vtok": (None, 512), "oAf": (None, 512), "yT": (1024, None),
    "rTB": (512, None), "kkT": (512, None), "vtokB": (None, 512), "gtok": (None, 512), "bons": (None, 8), "yBf": (None, 512),
    "bT0": (512, None), "bT1": (512, None), "kdT0": (512, None), "kdT1": (512, None),
    "btok0": (None, 512), "btok1": (None, 512), "kdtok0": (None, 512), "kdtok1": (None, 512), "sg0": (None, 512), "sg1": (None, 512),
}


def run_interleaved(K, gens):
    active = list(gens)
    while active:
        nxt = []
        for g, sel in active:
            K.ps_sel = sel
            try:
                next(g)
                nxt.append((g, sel))
            except StopIteration:
                pass
        active = nxt
    K.ps_sel = None


def even_out(K, C, A, W, S, j, t0):
    h, hb = A["h"]
    ua, uab = A["ua"]
    src = S["yT"].ap[:, t0:t0 + TS].rearrange("(c p) t -> p c t", p=128)
    K.dma_in(ua[:, 0:KD * TS].rearrange("p (c t) -> p c t", c=KD), src.bitcast(F32R), uab, rd=S["yT"].bufs(t0, t0 + TS), eng="pool")

    def cons_o(cid, M, ps, pb):
        K.tt(h[:, cid * TS:(cid + 1) * TS], h[:, cid * TS:(cid + 1) * TS], ps[:, 0:TS], ALU.add, [hb, pb], [hb])

    linear_fm(K, C, W["w_out_even"][j], KD, blocks_of(D, 512), ua, uab, cons_o)


def build_program(T, layers, pack, dbg=False, stop_after=None):
    import contextlib
    nc = bass.Bass("TRN2", target_bir_lowering=False)
    x_in = nc.dram_tensor("x", [T, D], F32, kind="ExternalInput").ap()
    p_in = nc.dram_tensor("p", [DEPTH, T, PLE], F32, kind="ExternalInput").ap()
    cp_in = nc.dram_tensor("cpack", [128, pack.n], F32, kind="ExternalInput").ap()
    op_in = nc.dram_tensor("oddpack", [2, 128, 4096], F32, kind="ExternalInput").ap()
    ep_in = nc.dram_tensor("evenpack", [2, 128, 4096], F32, kind="ExternalInput").ap()
    W = {}
    for nm, shp in WEIGHT_SHAPES.items():
        t = nc.dram_tensor(nm, list(shp), F32, kind="ExternalInput").ap()
        W[nm] = [t[i] for i in range(shp[0])]
    out = nc.dram_tensor("out", [T, D], F32, kind="ExternalOutput").ap()
    S = {}
    for nm, (r, c) in SCRATCH.items():
        shp = [r if r is not None else T, c if c is not None else T]
        S[nm] = DT(nc, nm, shp, kind="ExternalOutput" if dbg else "Internal")
    hT = S["hT"]
    NS = T // TS
    with contextlib.ExitStack() as st:
        K = KB(nc, st)
        rem = nc.sbuf_bytes_remaining
        K.init_arena(rem // 4 - 64)
        C = setup_consts(K, cp_in, pack)
        base = K.mark()
        first = True
        for li_pos, li in enumerate(layers):
            last = li_pos == len(layers) - 1
            j = li // 2
            if li % 2 == 1:
                A = alloc_tl(K, C)
                lc, lcb = A["lc"]
                K.dma_in(lc[:, 0:4096], op_in[j], lcb)
                for s_ in range(NS):
                    t0 = s_ * TS
                    if first:
                        load_h_from_x(K, C, A, x_in, t0)
                    else:
                        load_h(K, C, A, hT, t0)
                    odd_mixer(K, C, A, W, j, li)
                    ffn_ple(K, C, A, W, li, p_in[li], t0)
                    if last:
                        final_out(K, C, A, out, t0)
                    else:
                        store_h(K, C, A, hT, t0)
                K.P.barrier()
                K.release(base)
            else:
                A = alloc_tl(K, C)
                even_consts(K, C, A, ep_in[j])
                for s_ in range(NS):
                    t0 = s_ * TS
                    if first:
                        load_h_from_x(K, C, A, x_in, t0)
                        store_h(K, C, A, hT, t0)
                    else:
                        load_h(K, C, A, hT, t0)
                    even_in(K, C, A, W, S, j, li, t0)
                K.P.barrier()
                K.release(base)
                if stop_after == "E1":
                    break
                A = {"lc": C["lc"], "dv": C["dv"]}
                G = alloc_prepA(K)
                for s_ in range(NS):
                    prep_A(K, C, A, G, S, s_ * TS, T)
                K.P.barrier()
                K.release(base)
                if stop_after == "E2A":
                    break
                G = alloc_prepB(K)
                for s_ in range(NS):
                    prep_B(K, C, A, G, S, s_ * TS, T)
                K.P.barrier()
                K.release(base)
                if stop_after == "E2B":
                    break
                GA = alloc_scanA(K)
                GB = alloc_scanB(K)
                for d_ in range(2):
                    run_interleaved(K, [(scan_A(K, C, A, GA, S, d_, T), "a"), (scan_B(K, C, A, GB, S, d_, T), "b")])
                K.P.barrier()
                K.release(base)
                if stop_after == "E3B":
                    break
                A = alloc_tl(K, C)
                for s_ in range(NS):
                    t0 = s_ * TS
                    load_h(K, C, A, hT, t0)
                    even_out(K, C, A, W, S, j, t0)
                    ffn_ple(K, C, A, W, li, p_in[li], t0)
                    if last:
                        final_out(K, C, A, out, t0)
                    else:
                        store_h(K, C, A, hT, t0)
                K.P.barrier()
                K.release(base)
            first = False
        K.P.finalize_and_emit()
    return nc


N_CORES = 8


def kernel(**inputs):
    inp = {k: np.asarray(v) for k, v in inputs.items()}
    B, T = inp["x"].shape[0], inp["x"].shape[1]
    pack = build_packs(inp)
    ep = np.stack([even_pack(inp, 0), even_pack(inp, 1)])
    op = np.stack([odd_pack(inp, 0), odd_pack(inp, 1)])
    cp = pack.array()
    nc = build_program(T, [0, 1, 2, 3], pack)
    wts = {nm: np.ascontiguousarray(inp[nm], dtype=np.float32) for nm in WEIGHT_SHAPES}
    in_maps = []
    for c in range(N_CORES):
        b = c % B
        m = {"x": np.ascontiguousarray(inp["x"][b], dtype=np.float32), "p": np.ascontiguousarray(inp["p"][:, b], dtype=np.float32),
             "cpack": cp, "oddpack": op, "evenpack": ep}
        m.update(wts)
        in_maps.append(m)
    res = run_bass_kernel_spmd(nc, in_maps, core_ids=list(range(N_CORES)))
    out = np.stack([np.asarray(res.results[b]["out"], dtype=np.float32) for b in range(B)])
    return out
```

```python
import bisect
import numpy as np
import concourse.bass as bass
import concourse.mybir as mybir
from concourse.bass_utils import run_bass_kernel_spmd

F32 = mybir.dt.float32
F32R = mybir.dt.float32r
AF = mybir.ActivationFunctionType
ALU = mybir.AluOpType
AX = mybir.AxisListType

HW_WLOAD = False
GEN = 30000
DGEN = 1800


class Buf:
    __slots__ = ("name", "last_w", "readers", "lane_in", "lane_out")

    def __init__(self, name):
        self.name = name
        self.last_w = None
        self.readers = []
        self.lane_in = None
        self.lane_out = None


class Op:
    __slots__ = ("eng", "fn", "deps", "stream", "sidx", "signal", "val", "waits", "oid")


class Prog:
    CE = ("pe", "act", "dve", "pool")

    def __init__(self, nc):
        self.nc = nc
        self.ops = []
        self.streams = {}
        self.pending_bar = {}

    def _stream(self, name):
        return self.streams.setdefault(name, [])

    def add(self, eng, fn, reads=(), writes=(), dma=None):
        op = Op()
        op.oid = len(self.ops)
        op.eng = eng
        op.fn = fn
        deps = {}
        for b in reads:
            if b.last_w is not None:
                deps[b.last_w] = "raw"
        for b in writes:
            if b.last_w is not None:
                deps.setdefault(b.last_w, "waw")
            for r in b.readers:
                deps.setdefault(r, "war")
        if dma is not None:
            kind, lb = dma
            op.stream = "dma_%s_%s" % (kind, lb.name)
        else:
            op.stream = eng
        pruned = {}
        for d, k in deps.items():
            p = self.ops[d]
            if p.stream == op.stream:
                if dma is not None:
                    continue
                if eng == "pe":
                    continue
            pruned[d] = k
        if eng in self.pending_bar:
            for d in self.pending_bar.pop(eng):
                if self.ops[d].stream != op.stream or dma is not None:
                    pruned[d] = "bar"
        op.deps = pruned
        st = self._stream(op.stream)
        op.sidx = len(st)
        st.append(op.oid)
        op.signal = dma is not None
        self.ops.append(op)
        for b in reads:
            b.readers.append(op.oid)
        for b in writes:
            b.last_w = op.oid
            b.readers = []
        return op.oid

    def barrier(self):
        last = set(lst[-1] for lst in self.streams.values() if lst)
        for e in ("pe", "act", "dve", "pool", "sp"):
            self.pending_bar[e] = set(last) | self.pending_bar.get(e, set())

    def finalize_and_emit(self, final_waits=()):
        nc = self.nc
        ops = self.ops
        waited = {}
        for op in ops:
            need = {}
            for d in op.deps:
                p = ops[d]
                if p.stream.startswith("dma_"):
                    lane = self.streams[p.stream]
                    cnt = bisect.bisect_left(lane, op.oid)
                    need[p.stream] = max(need.get(p.stream, -1), cnt - 1)
                else:
                    need[p.stream] = max(need.get(p.stream, -1), p.sidx)
            op.waits = []
            for s, idx in need.items():
                key = (op.eng, s)
                if waited.get(key, -1) >= idx:
                    continue
                waited[key] = idx
                op.waits.append((s, idx))
                if not s.startswith("dma_"):
                    ops[self.streams[s][idx]].signal = True
        fin = []
        for s_, lst_ in self.streams.items():
            if s_.startswith("dma_") and lst_:
                fin.append((s_, len(lst_) - 1))
        sem_of = {}
        import contextlib
        stack = contextlib.ExitStack()
        with stack:
            valmap = {}
            for s, lst in self.streams.items():
                if s.startswith("dma_"):
                    sem = None
                    for i, o in enumerate(lst):
                        if i % DGEN == 0:
                            sem = stack.enter_context(nc.semaphore("s_%s_%d" % (s, i // DGEN)))
                        valmap[(s, i)] = (sem, 16 * (i % DGEN + 1))
                        ops[o].val = (sem, 16)
                else:
                    cnt = 0
                    gen = 0
                    sem = stack.enter_context(nc.semaphore("s_%s_%d" % (s, gen)))
                    last = None
                    for i, o in enumerate(lst):
                        if ops[o].signal:
                            if cnt >= GEN:
                                gen += 1
                                cnt = 0
                                sem = stack.enter_context(nc.semaphore("s_%s_%d" % (s, gen)))
                            cnt += 1
                            ops[o].val = (sem, 1)
                            valmap[(s, i)] = (sem, cnt)
            self.n_sems = sum(1 for _ in sem_of)
            per_eng = {e: [] for e in ("pe", "act", "dve", "pool", "sp")}
            for op in ops:
                per_eng[op.eng].append(op)

            def run_engine(eobj, lst, is_sp=False):
                for op in lst:
                    for (s, idx) in op.waits:
                        sem, v = valmap[(s, idx)]
                        eobj.wait_ge(sem, v)
                    ins = op.fn(eobj)
                    if op.signal:
                        sem, inc = op.val
                        ins.then_inc(sem, inc)
                if is_sp:
                    for (s, idx) in fin:
                        sem, v = valmap[(s, idx)]
                        eobj.wait_ge(sem, v)

            with nc.Block() as block:
                @block.tensor
                def _(e):
                    run_engine(e, per_eng["pe"])

                @block.scalar
                def _(e):
                    run_engine(e, per_eng["act"])

                @block.vector
                def _(e):
                    run_engine(e, per_eng["dve"])

                @block.gpsimd
                def _(e):
                    run_engine(e, per_eng["pool"])

                @block.sync
                def _(e):
                    run_engine(e, per_eng["sp"], is_sp=True)


D = 1024
KD = 8
DEPTH = 4
PLE = 256
DFF = 2816
NFF = 22
A_COLS = 2064
B_COLS = 1760
EVEN_IN = A_COLS + B_COLS
TS = 512
NEG = -30000.0


class Pack:
    def __init__(self):
        self.cols = []
        self.off = {}
        self.n = 0

    def add(self, name, arr):
        arr = np.ascontiguousarray(arr, dtype=np.float32)
        assert arr.ndim == 2 and arr.shape[0] <= 128, (name, arr.shape)
        if arr.shape[0] < 128:
            pad = np.zeros((128, arr.shape[1]), np.float32)
            pad[: arr.shape[0]] = arr
            arr = pad
        self.off[name] = (self.n, arr.shape[1])
        self.cols.append(arr)
        self.n += arr.shape[1]

    def array(self):
        return np.concatenate(self.cols, axis=1)


def chunked(v, nch=None):
    v = np.asarray(v, np.float32).reshape(-1, 128)
    return np.ascontiguousarray(v.T)


def rowrep(v):
    v = np.asarray(v, np.float32).reshape(1, -1)
    return np.ascontiguousarray(np.broadcast_to(v, (128, v.shape[1])))


class Ring:
    def __init__(self, K, name, shape, dtype, n, alias=False):
        self.tiles = []
        self.alias = {}
        for i in range(n):
            t = K.sb("%s%d" % (name, i), shape, dtype)
            b = Buf("%s%d" % (name, i))
            self.tiles.append((t, b))
            if alias:
                self.alias[id(b)] = K.alias_f32("%s%d" % (name, i), shape[1])
        self.i = 0

    def next(self):
        r = self.tiles[self.i % len(self.tiles)]
        self.i += 1
        return r


class KB:
    def __init__(self, nc, st):
        self.nc = nc
        self.st = st
        self.P = Prog(nc)
        self.ps_tiles = []
        for i in range(8):
            t = st.enter_context(nc.psum_tensor("ps%d" % i, [128, 512], F32))
            self.ps_tiles.append((t, Buf("ps%d" % i)))
        self.ps_i = 0
        self.ps_sel = None
        self.ps_sub = {"a": 0, "b": 0}
        self.dq = 0

    def init_arena(self, nfloats):
        self.arena_t = self.st.enter_context(self.nc.sbuf_tensor("arena", [128, nfloats], F32))
        self.arena = self.arena_t
        self.arena_base = self.nc.lookup_mloc(self.arena_t).addr
        self.arena_n = nfloats
        self.arena_p = 0
        self.alias_n = 0

    def sb(self, name, shape, dtype=F32):
        n = 1
        for d in shape[1:]:
            n *= d
        off = (self.arena_p + 7) // 8 * 8
        assert off + n <= self.arena_n, ("arena overflow", name, off, n, self.arena_n)
        self.arena_p = off + n
        self.alias_n += 1
        self.last_off = off
        t = self.nc.alloc_sbuf_tensor_at("%s_m%d" % (name, self.alias_n), [128, n], dtype, offset=self.arena_base + 4 * off)
        ap = t[0:shape[0], 0:n]
        if len(shape) == 3:
            ap = ap.rearrange("p (a b) -> p a b", a=shape[1])
        elif len(shape) == 4:
            ap = ap.rearrange("p (a b c) -> p a b c", a=shape[1], b=shape[2])
        return ap

    def alias_f32(self, name, n):
        self.alias_n += 1
        t = self.nc.alloc_sbuf_tensor_at("%s_a%d" % (name, self.alias_n), [128, n], F32, offset=self.arena_base + 4 * self.last_off)
        return t[0:128, 0:n]

    def mark(self):
        return self.arena_p

    def release(self, m):
        self.arena_p = m

    def psum(self):
        if self.ps_sel is not None:
            base = 0 if self.ps_sel == "a" else 4
            i = self.ps_sub[self.ps_sel]
            self.ps_sub[self.ps_sel] = i + 1
            return self.ps_tiles[base + i % 4]
        r = self.ps_tiles[self.ps_i % 8]
        self.ps_i += 1
        return r

    def dma_in(self, out_ap, in_ap, buf, rd=(), eng="sp"):
        self.P.add(eng, lambda e: e.dma_start(out=out_ap, in_=in_ap), reads=list(rd), writes=[buf], dma=("in", buf))

    def dma_out(self, out_ap, in_ap, buf, wr=(), eng="sp"):
        self.P.add(eng, lambda e: e.dma_start(out=out_ap, in_=in_ap), reads=[buf], writes=list(wr), dma=("out", buf))

    def mm(self, out_ap, lhsT, rhs, start, stop, rd, wr):
        self.P.add("pe", lambda e: e.matmul(out_ap, lhsT=lhsT, rhs=rhs, start=start, stop=stop), reads=rd, writes=wr)

    def act(self, out_ap, in_ap, func, rd, wr, bias=None, scale=None, accum_out=None):
        kw = {}
        if bias is not None:
            kw["bias"] = bias
        if scale is not None:
            kw["scale"] = scale
        if accum_out is not None:
            kw["accum_out"] = accum_out
        self.P.add("act", lambda e: e.activation(out=out_ap, in_=in_ap, func=func, **kw), reads=rd, writes=wr)

    def tt(self, out_ap, in0, in1, op, rd, wr, eng="dve"):
        self.P.add(eng, lambda e: e.tensor_tensor(out=out_ap, in0=in0, in1=in1, op=op), reads=rd, writes=wr)

    def ts(self, out_ap, in0, s1, op0, rd, wr, s2=None, op1=None, eng="dve", accum_out=None):
        kw = {}
        if accum_out is not None:
            kw["accum_out"] = accum_out
        if op1 is None:
            self.P.add(eng, lambda e: e.tensor_scalar(out=out_ap, in0=in0, scalar1=s1, scalar2=None, op0=op0, **kw), reads=rd, writes=wr)
        else:
            self.P.add(eng, lambda e: e.tensor_scalar(out=out_ap, in0=in0, scalar1=s1, scalar2=s2, op0=op0, op1=op1, **kw), reads=rd, writes=wr)

    def stt(self, out_ap, in0, scalar, in1, op0, op1, rd, wr, eng="dve"):
        self.P.add(eng, lambda e: e.scalar_tensor_tensor(out=out_ap, in0=in0, scalar=scalar, in1=in1, op0=op0, op1=op1), reads=rd, writes=wr)

    def copy(self, out_ap, in_ap, rd, wr, eng="dve"):
        if eng == "act":
            self.P.add("act", lambda e: e.activation(out=out_ap, in_=in_ap, func=AF.Copy), reads=rd, writes=wr)
        else:
            self.P.add(eng, lambda e: e.tensor_copy(out=out_ap, in_=in_ap), reads=rd, writes=wr)

    def recip(self, out_ap, in_ap, rd, wr):
        self.P.add("dve", lambda e: e.reciprocal(out=out_ap, in_=in_ap), reads=rd, writes=wr)

    def memset(self, ap, val, wr, eng="pool"):
        self.P.add(eng, lambda e: e.memset(ap, val), reads=[], writes=wr)


class DT:
    def __init__(self, nc, name, shape, kind="Internal", dtype=F32):
        self.name = name
        self.t = nc.dram_tensor(name, list(shape), dtype, kind=kind)
        self.ap = self.t.ap()
        self._b = {}

    def bufs(self, t0, t1):
        t0 = max(t0, 0)
        return [self._b.setdefault(i, Buf("%s_b%d" % (self.name, i))) for i in range(t0 // 128, (t1 + 127) // 128)]


def rmsnorm_fm(K, C, h, hbuf, wcol, out, obuf, sq, sqbuf, nk=KD, n=TS, eps_name="eps6", dscale=1.0 / D):
    K.act(sq[:, 0:nk * n], h[:, 0:nk * n], AF.Square, [hbuf], [sqbuf])
    ps, pb = K.psum()
    for c in range(nk):
        K.mm(ps[:, 0:n], C["ones_r"], sq[:, c * n:(c + 1) * n], c == 0, c == nk - 1, [sqbuf, C["buf"]], [pb])
    rs, rb = C["rstd"].next()
    K.act(rs[:, 0:n], ps[:, 0:n], AF.Sqrt, [pb, C["buf"]], [rb], bias=C[eps_name], scale=dscale)
    K.recip(rs[:, 0:n], rs[:, 0:n], [rb], [rb])
    for c in range(nk):
        K.stt(out[:, c * n:(c + 1) * n], h[:, c * n:(c + 1) * n], wcol[:, c:c + 1], rs[:, 0:n], ALU.mult, ALU.mult,
              [hbuf, rb, C["buf"]], [obuf])


def load_w(K, C, wap, r0, nk, c0, ncols, eng="pool"):
    wt, wb = C["wring"].next()
    view = wt[:, 0:nk * ncols].rearrange("p (k c) -> p k c", k=nk)
    src = wap[r0:r0 + nk * 128, c0:c0 + ncols].rearrange("(k p) c -> p k c", p=128)
    if HW_WLOAD:
        wf = C["wring_f32"][id(wb)]
        K.dma_in(wf[:, 0:nk * ncols].rearrange("p (k c) -> p k c", k=nk), src, wb, eng="sp")
    else:
        K.dma_in(view, src.bitcast(F32R), wb, eng=eng)
    return view, wb


def linear_fm(K, C, wap, nk, col_blocks, xin, xbuf, consumer, n=TS, r0=0):
    for (c0, ncols, chunks) in col_blocks:
        wv, wb = load_w(K, C, wap, r0, nk, c0, ncols)
        for (cid, off, M) in chunks:
            ps, pb = K.psum()
            for k in range(nk):
                K.mm(ps[0:M, 0:n], wv[:, k, off:off + M], xin[:, k * n:(k + 1) * n], k == 0, k == nk - 1, [wb, xbuf], [pb])
            consumer(cid, M, ps, pb)


def blocks_of(total, bs, cbase=0, m=128):
    out = []
    c0 = 0
    cid = cbase
    while c0 < total:
        nc_ = min(bs, total - c0)
        chunks = []
        off = 0
        while off < nc_:
            mm_ = min(m, nc_ - off)
            chunks.append((cid, off, mm_))
            cid += 1
            off += mm_
        out.append((c0, nc_, chunks))
        c0 += nc_
    return out


def build_packs(inp):
    P = Pack()
    P.add("ident", np.eye(128, dtype=np.float32))
    P.add("ones", np.ones((128, 128), np.float32))
    P.add("eps6", np.full((128, 1), 1e-6, np.float32))
    P.add("eps5", np.full((128, 1), 1e-5, np.float32))
    P.add("epsgn", np.full((128, 1), 64e-5, np.float32))
    for i in range(DEPTH):
        P.add("nmix%d" % i, chunked(inp["norm_mix"][i]))
        P.add("nffn%d" % i, chunked(inp["norm_ffn"][i]))
        P.add("nple%d" % i, chunked(inp["norm_ple"][i]))
    P.add("nfin", chunked(inp["norm_final"]))
    mask_pack(P)
    return P


class Ctx(dict):
    pass


def setup_consts(K, cp_dram, pack):
    C = Ctx()
    n = pack.n
    cpt = K.sb("cpack", [128, n])
    cb = Buf("cpack")
    K.dma_in(cpt[:, 0:n], cp_dram, cb)
    C["buf"] = cb
    C["cp"] = cpt
    for name, (off, w) in pack.off.items():
        C[name] = cpt[:, off:off + w]
    ones_r = K.sb("ones_r", [128, 128], F32R)
    K.P.add("pool", lambda e: e.tensor_copy(out=ones_r, in_=C["ones"]), reads=[cb], writes=[cb])
    C["ones_r"] = ones_r
    blk_r = K.sb("blk_r", [128, 128], F32R)
    K.P.add("pool", lambda e: e.tensor_copy(out=blk_r, in_=C["BLK"]), reads=[cb], writes=[cb])
    C["blk_r"] = blk_r
    C["rstd"] = Ring(K, "rstd", [128, TS], F32, 2)
    C["lc"] = (K.sb("lc", [128, 4096], F32), Buf("lc"))
    C["dv"] = (K.sb("dv", [128, 64], F32), Buf("dv"))
    return C


class Ring:
    def __init__(self, K, name, shape, dtype, n, alias=False):
        self.tiles = []
        self.alias = {}
        for i in range(n):
            t = K.sb("%s%d" % (name, i), shape, dtype)
            b = Buf("%s%d" % (name, i))
            self.tiles.append((t, b))
            if alias:
                self.alias[id(b)] = K.alias_f32("%s%d" % (name, i), shape[1])
        self.i = 0

    def next(self):
        r = self.tiles[self.i % len(self.tiles)]
        self.i += 1
        return r


def alloc_tl(K, C):
    A = {}
    for nm in ("h", "hn", "sq"):
        dt = F32 if nm in ("h",) else F32R
        A[nm] = (K.sb("tl_" + nm, [128, KD * TS], dt), Buf("tl_" + nm))
    A["ua"] = A["sq"]
    A["lc"] = C["lc"]
    A["dv"] = C["dv"]
    C["wring"] = Ring(K, "wring", [128, 4224], F32R, 3, alias=True)
    C["wring_f32"] = C["wring"].alias
    A["act"] = (K.sb("tl_act", [128, NFF * TS], F32R), Buf("tl_act"))
    A["tmp"] = Ring(K, "tl_tmp", [128, TS], F32, 3)
    A["tok"] = Ring(K, "tl_tok", [128, 1024], F32, 2)
    A["tok2"] = Ring(K, "tl_tok2", [128, 1024], F32, 2)
    A["st"] = Ring(K, "tl_st", [128, 8], F32, 4)
    A["pt"] = (K.sb("tl_pt", [128, 2 * TS], F32R), Buf("tl_pt"))
    return A


def load_h_from_x(K, C, A, x_ap, t0):
    h, hb = A["h"]
    for sub in range(TS // 128):
        xt, xb = A["tok"].next()
        K.dma_in(xt[:, 0:1024], x_ap[t0 + sub * 128:t0 + (sub + 1) * 128, :], xb)
        for half in range(2):
            ps, pb = K.psum()
            for c4 in range(4):
                c = half * 4 + c4
                K.mm(ps[:, c4 * 128:(c4 + 1) * 128], xt[:, c * 128:(c + 1) * 128], C["ident"], True, True, [xb, C["buf"]], [pb])
            dst = h[:, half * 4 * TS:(half + 1) * 4 * TS].rearrange("p (c t) -> p c t", c=4)[:, :, sub * 128:(sub + 1) * 128]
            src = ps[:, 0:512].rearrange("p (c t) -> p c t", c=4)
            K.copy(dst, src, [pb], [hb], eng="act" if half else "dve")


def load_h(K, C, A, hT, t0):
    h, hb = A["h"]
    src = hT.ap[:, t0:t0 + TS].rearrange("(c p) t -> p c t", p=128)
    K.dma_in(h[:, 0:KD * TS].rearrange("p (c t) -> p c t", c=KD), src, hb, rd=hT.bufs(t0, t0 + TS))


def store_h(K, C, A, hT, t0):
    h, hb = A["h"]
    dst = hT.ap[:, t0:t0 + TS].rearrange("(c p) t -> p c t", p=128)
    K.dma_out(dst, h[:, 0:KD * TS].rearrange("p (c t) -> p c t", c=KD), hb, wr=hT.bufs(t0, t0 + TS))


def ffn_ple(K, C, A, W, li, p_ap, t0):
    h, hb = A["h"]
    hn, hnb = A["hn"]
    sq, sqb = A["sq"]
    act, ab = A["act"]
    rmsnorm_fm(K, C, h, hb, C["nffn%d" % li], hn, hnb, sq, sqb)
    gate_ps = {}

    def cons_gate(cid, M, ps, pb):
        gate_ps[cid] = (ps, pb)

    def cons_up(cid, M, ps, pb):
        gps, gpb = gate_ps.pop(cid)
        tmp, tb = A["tmp"].next()
        K.act(tmp[:, 0:TS], gps[:, 0:TS], AF.Silu, [gpb], [tb])
        K.tt(act[:, cid * TS:(cid + 1) * TS], tmp[:, 0:TS], ps[:, 0:TS], ALU.mult, [tb, pb], [ab])

    for (c0, ncols, chunks) in blocks_of(DFF, 384):
        linear_fm(K, C, W["w_gate"][li], KD, [(c0, ncols, chunks)], hn, hnb, cons_gate)
        linear_fm(K, C, W["w_up"][li], KD, [(c0, ncols, chunks)], hn, hnb, cons_up)

    def cons_down(cid, M, ps, pb):
        K.tt(h[:, cid * TS:(cid + 1) * TS], h[:, cid * TS:(cid + 1) * TS], ps[:, 0:TS], ALU.add, [hb, pb], [hb])

    linear_fm(K, C, W["w_down"][li], NFF, blocks_of(D, 128), act, ab, cons_down)
    rmsnorm_fm(K, C, h, hb, C["nple%d" % li], hn, hnb, sq, sqb)
    pt, ptb = A["pt"]
    for sub in range(TS // 128):
        xt, xb = A["tok"].next()
        K.dma_in(xt[:, 0:PLE], p_ap[t0 + sub * 128:t0 + (sub + 1) * 128, :], xb)
        ps, pb = K.psum()
        for c in range(2):
            K.mm(ps[:, c * 128:(c + 1) * 128], xt[:, c * 128:(c + 1) * 128], C["ident"], True, True, [xb, C["buf"]], [pb])
        dst = pt[:, 0:2 * TS].rearrange("p (c t) -> p c t", c=2)[:, :, sub * 128:(sub + 1) * 128]
        K.copy(dst, ps[:, 0:256].rearrange("p (c t) -> p c t", c=2), [pb], [ptb], eng="act")
    e_ps = {}

    def cons_e(cid, M, ps, pb):
        e_ps[cid] = (ps, pb)

    def cons_g(cid, M, ps, pb):
        eps_, epb = e_ps.pop(cid)
        tmp, tb = A["tmp"].next()
        K.act(tmp[:, 0:TS], ps[:, 0:TS], AF.Sigmoid, [pb], [tb])
        K.tt(tmp[:, 0:TS], tmp[:, 0:TS], eps_[:, 0:TS], ALU.mult, [tb, epb], [tb])
        K.tt(h[:, cid * TS:(cid + 1) * TS], h[:, cid * TS:(cid + 1) * TS], tmp[:, 0:TS], ALU.add, [hb, tb], [hb])

    for (c0, ncols, chunks) in blocks_of(D, 256):
        linear_fm(K, C, W["w_ple"][li], 2, [(c0, ncols, chunks)], pt, ptb, cons_e)
        linear_fm(K, C, W["w_ple_gate"][li], KD, [(c0, ncols, chunks)], hn, hnb, cons_g)


def final_out(K, C, A, out_ap, t0):
    h, hb = A["h"]
    hn, hnb = A["hn"]
    sq, sqb = A["sq"]
    rmsnorm_fm(K, C, h, hb, C["nfin"], hn, hnb, sq, sqb)
    hn32 = hn.bitcast(F32)
    for sub in range(TS // 128):
        ot, ob = A["tok"].next()
        for half in range(2):
            ps, pb = K.psum()
            for c4 in range(4):
                c = half * 4 + c4
                K.mm(ps[:, c4 * 128:(c4 + 1) * 128], hn32[:, c * TS + sub * 128:c * TS + (sub + 1) * 128], C["ident"], True, True,
                     [hnb, C["buf"]], [pb])
            K.copy(ot[:, half * 512:(half + 1) * 512], ps[:, 0:512], [pb], [ob], eng="act" if half else "dve")
        K.dma_out(out_ap[t0 + sub * 128:t0 + (sub + 1) * 128, :], ot[:, 0:1024], ob)


def odd_mixer(K, C, A, W, j, li):
    h, hb = A["h"]
    hn, hnb = A["hn"]
    sq, sqb = A["sq"]
    ua, uab = A["ua"]
    rmsnorm_fm(K, C, h, hb, C["nmix%d" % li], hn, hnb, sq, sqb)
    win = W["w_in_odd"][j]
    lc, lcb = A["lc"]
    c_lnw, c_lnb, c_wsT, c_bs = lc[:, 0:1024], lc[:, 1024:2048], lc[:, 2048:3072], lc[:, 3072:4096]

    def cons_u(cid, M, ps, pb):
        K.act(ua[:, cid * TS:(cid + 1) * TS], ps[:, 0:TS], AF.Gelu_apprx_tanh, [pb], [uab])

    linear_fm(K, C, win, KD, blocks_of(1024, 512), hn, hnb, cons_u)
    wv = []
    for half in range(2):
        wv.append(load_w(K, C, win, 0, KD, 1024 + half * 512, 512))
    for sub in range(TS // 128):
        vt, vb = A["tok"].next()
        for half in range(2):
            wview, wb = wv[half]
            ps, pb = K.psum()
            for k in range(KD):
                K.mm(ps[:, 0:512], hn[:, k * TS + sub * 128:k * TS + (sub + 1) * 128], wview[:, k, :], k == 0, k == KD - 1, [hnb, wb], [pb])
            K.act(vt[:, half * 512:(half + 1) * 512], ps[:, 0:512], AF.Gelu_apprx_tanh, [pb], [vb])
        st, sb_ = A["st"].next()
        v2, v2b = A["tok2"].next()
        K.P.add("dve", lambda e, st=st, vt=vt: e.reduce_sum(out=st[:, 0:1], in_=vt[:, 0:1024], axis=AX.X), reads=[vb], writes=[sb_])
        K.act(v2[:, 0:1024], vt[:, 0:1024], AF.Square, [vb], [v2b])
        K.P.add("dve", lambda e, st=st, v2=v2: e.reduce_sum(out=st[:, 1:2], in_=v2[:, 0:1024], axis=AX.X), reads=[v2b], writes=[sb_])
        K.ts(st[:, 2:3], st[:, 0:1], 1.0 / 1024, ALU.mult, [sb_], [sb_])
        K.tt(st[:, 3:4], st[:, 2:3], st[:, 2:3], ALU.mult, [sb_], [sb_])
        K.stt(st[:, 4:5], st[:, 1:2], 1.0 / 1024, st[:, 3:4], ALU.mult, ALU.subtract, [sb_], [sb_])
        K.act(st[:, 5:6], st[:, 4:5], AF.Sqrt, [sb_, C["buf"]], [sb_], bias=C["eps5"], scale=1.0)
        K.recip(st[:, 5:6], st[:, 5:6], [sb_], [sb_])
        K.ts(v2[:, 0:1024], vt[:, 0:1024], st[:, 2:3], ALU.subtract, [vb, sb_, v2b], [v2b], s2=st[:, 5:6], op1=ALU.mult)
        K.tt(v2[:, 0:1024], v2[:, 0:1024], c_lnw, ALU.mult, [v2b, lcb], [v2b])
        K.tt(v2[:, 0:1024], v2[:, 0:1024], c_lnb, ALU.add, [v2b, lcb], [v2b])
        for g2 in range(2):
            ps, pb = K.psum()
            for g4 in range(4):
                g = g2 * 4 + g4
                K.mm(ps[:, g4 * 128:(g4 + 1) * 128], v2[:, g * 128:(g + 1) * 128], c_wsT[:, g * 128:(g + 1) * 128], True, False,
                     [v2b, lcb], [pb])
                K.mm(ps[:, g4 * 128:(g4 + 1) * 128], C["ones"][0:1, :], c_bs[0:1, g * 128:(g + 1) * 128], False, True,
                     [C["buf"], lcb], [pb])
            dst = ua[:, g2 * 4 * TS:(g2 + 1) * 4 * TS].rearrange("p (c t) -> p c t", c=4)[:, :, sub * 128:(sub + 1) * 128]
            K.tt(dst, dst.bitcast(F32), ps[:, 0:512].rearrange("p (c t) -> p c t", c=4), ALU.mult, [uab, pb], [uab])

    def cons_o(cid, M, ps, pb):
        K.tt(h[:, cid * TS:(cid + 1) * TS], h[:, cid * TS:(cid + 1) * TS], ps[:, 0:TS], ALU.add, [hb, pb], [hb])

    linear_fm(K, C, W["w_out_odd"][j], KD, blocks_of(D, 512), ua, uab, cons_o)


def odd_pack(inp, j):
    out = np.zeros((128, 4096), np.float32)
    out[:, 0:1024] = rowrep(inp["c_ln_w"][j])
    out[:, 1024:2048] = rowrep(inp["c_ln_b"][j])
    out[:, 2048:3072] = np.transpose(np.asarray(inp["c_ws"][j], np.float32), (2, 0, 1)).reshape(128, 1024)
    out[0, 3072:4096] = np.asarray(inp["c_bs"][j], np.float32).reshape(1024)
    return out


WEIGHT_SHAPES = {
    "w_in_even": (2, D, EVEN_IN), "w_out_even": (2, D, D), "w_in_odd": (2, D, 2048), "w_out_odd": (2, D, D),
    "w_gate": (4, D, DFF), "w_up": (4, D, DFF), "w_down": (4, DFF, D), "w_ple": (4, PLE, D), "w_ple_gate": (4, D, D),
}


def mask_pack(P):
    idx = np.arange(128)
    same = (idx[:, None] // 64) == (idx[None, :] // 64)
    for d in range(2):
        before = (idx[:, None] < idx[None, :]) if d == 0 else (idx[:, None] > idx[None, :])
        strict = (same & before).astype(np.float32)
        incl = (same & (before | (idx[:, None] == idx[None, :]))).astype(np.float32)
        P.add("INCL%d" % d, incl)
        P.add("STRICT%d" % d, strict)
        P.add("AFTER%d" % d, strict.T.copy())
        P.add("NEGINCL%d" % d, np.where(incl.T > 0, 0.0, NEG).astype(np.float32))
    P.add("OFFD", 1.0 - np.eye(128, dtype=np.float32))
    P.add("nones", -np.ones((128, 128), np.float32))
    for cc in range(2):
        P.add("CMROW%d" % cc, np.broadcast_to(((idx // 64) == cc).astype(np.float32)[:, None], (128, 128)).copy())
    P.add("CHI", np.stack([(idx // 64) == 0, (idx // 64) == 1], 1).astype(np.float32))
    P.add("BLK", same.astype(np.float32))
    P.add("HSEL", np.stack([idx < 64, idx >= 64], 1).astype(np.float32))
    P.add("one1", np.ones((128, 1), np.float32))
    P.add("two1", np.full((128, 1), 2.0, np.float32))


EV = {}


def even_pack(inp, j):
    P = Pack()
    cw = np.asarray(inp["a_conv"][j], np.float32)
    P.add("cw", cw.reshape(5, 12, 128).transpose(2, 1, 0).reshape(128, 60))
    mu = np.asarray(inp["b_mu"][j], np.float32)
    P.add("mu_rkv", chunked(mu[0:1536]))
    P.add("mu_wd", mu[1536:1600].reshape(64, 1))
    P.add("mu_ad", mu[1600:1664].reshape(64, 1))
    P.add("mu_gd", mu[1664:1760].reshape(96, 1))
    P.add("k_k", chunked(inp["b_k_k"][j]))
    P.add("k_a", chunked(inp["b_k_a"][j]))
    P.add("r_k", chunked(np.asarray(inp["b_r_k"][j]).reshape(512)))
    P.add("a0", np.concatenate([chunked(inp["b_a0"][j][d]) for d in range(2)], 1))
    P.add("dtb", rowrep(np.asarray(inp["a_dt_bias"][j]).reshape(8)))
    P.add("alog", rowrep(np.asarray(inp["a_log"][j]).reshape(8)))
    P.add("anorm", rowrep(np.tile(np.asarray(inp["a_norm"][j], np.float32), 4)))
    P.add("lnw", rowrep(inp["b_ln_w"][j]))
    P.add("lnb", rowrep(inp["b_ln_b"][j]))
    P.add("w2s", np.asarray(inp["b_w2"][j], np.float32).reshape(64, 512))
    w0s = np.zeros((64, 512), np.float32)
    w0s[0] = inp["b_w0"][j][0]
    w0s[32] = inp["b_w0"][j][1]
    P.add("w0s", w0s)
    P.add("a2s", np.asarray(inp["b_a2"][j], np.float32).reshape(64, 512))
    P.add("g2", np.asarray(inp["b_g2"][j], np.float32))
    EV.update(P.off)
    arr = P.array()
    out = np.zeros((128, 4096), np.float32)
    out[:, : arr.shape[1]] = arr
    return out


def ev(lc, name):
    off, w = EV[name]
    return lc[:, off:off + w]


def even_in(K, C, A, W, S, j, li, t0):
    h, hb = A["h"]
    hn, hnb = A["hn"]
    sq, sqb = A["sq"]
    rmsnorm_fm(K, C, h, hb, C["nmix%d" % li], hn, hnb, sq, sqb)
    win = W["w_in_even"][j]
    cnt = [0]

    def store_to(dt, rowmap):
        def f(cid, M, ps, pb):
            tmp, tb = A["tmp"].next()
            cnt[0] += 1
            K.copy(tmp[0:M, 0:TS], ps[0:M, 0:TS], [pb], [tb], eng="act" if cnt[0] % 2 else "dve")
            r0 = rowmap(cid)
            K.dma_out(dt.ap[r0:r0 + M, t0:t0 + TS], tmp[0:M, 0:TS], tb, wr=dt.bufs(t0, t0 + TS))
        return f

    for b in range(3):
        chunks = [(b * 4 + c, c * 128, 128) for c in range(4)]
        linear_fm(K, C, win, KD, [(b * 512, 512, chunks)], hn, hnb, store_to(S["rawA"], lambda cid: cid * 128))
    for b in range(3):
        chunks = [(b * 4 + c, c * 128, 128) for c in range(4)]
        linear_fm(K, C, win, KD, [(A_COLS + b * 512, 512, chunks)], hn, hnb, store_to(S["rawB"], lambda cid: cid * 128))
    rows = {12: 1536, 13: 1600, 14: 1664}
    linear_fm(K, C, win, KD, [(A_COLS + 1536, 224, [(12, 0, 64), (13, 64, 64), (14, 128, 96)])], hn, hnb,
              store_to(S["rawB"], lambda cid: rows[cid]))
    wz, wzb = load_w(K, C, win, 0, KD, 1536, 528)
    for sub in range(TS // 128):
        zt, zb = A["tok"].next()
        ps, pb = K.psum()
        ps2, pb2 = K.psum()
        for k in range(KD):
            lhsT = hn[:, k * TS + sub * 128:k * TS + (sub + 1) * 128]
            K.mm(ps[:, 0:512], lhsT, wz[:, k, 0:512], k == 0, k == KD - 1, [hnb, wzb], [pb])
        for k in range(KD):
            lhsT = hn[:, k * TS + sub * 128:k * TS + (sub + 1) * 128]
            K.mm(ps2[:, 0:16], lhsT, wz[:, k, 512:528], k == 0, k == KD - 1, [hnb, wzb], [pb2])
        K.copy(zt[:, 0:512], ps[:, 0:512], [pb], [zb], eng="act")
        K.copy(zt[:, 512:528], ps2[:, 0:16], [pb2], [zb], eng="dve")
        tt0 = t0 + sub * 128
        K.dma_out(S["zt"].ap[tt0:tt0 + 128, :], zt[:, 0:512], zb, wr=S["zt"].bufs(tt0, tt0 + 128))
        K.dma_out(S["abt"].ap[tt0:tt0 + 128, :], zt[:, 512:528], zb, wr=S["abt"].bufs(tt0, tt0 + 128))


def even_consts(K, C, A, evp_ap):
    lc, lcb = A["lc"]
    K.dma_in(lc[:, 0:4096], evp_ap, lcb)
    dv, dvb = A["dv"]

    def half_one(src, np_, o):
        w = src.shape[1]
        K.ts(dv[0:np_, o:o + w], src[0:np_, :], 0.5, ALU.mult, [lcb], [dvb])
        K.ts(dv[0:np_, o + w:o + 2 * w], src[0:np_, :], -1.0, ALU.mult, [lcb], [dvb], s2=1.0, op1=ALU.add)

    half_one(ev(lc, "mu_rkv"), 128, 0)
    half_one(ev(lc, "mu_wd"), 64, 24)
    half_one(ev(lc, "mu_ad"), 64, 26)
    half_one(ev(lc, "mu_gd"), 96, 28)
    K.ts(dv[:, 30:34], ev(lc, "k_a"), -1.0, ALU.mult, [lcb], [dvb], s2=1.0, op1=ALU.add)
    K.ts(dv[:, 34:38], ev(lc, "r_k"), 0.5, ALU.mult, [lcb], [dvb])
    K.act(dv[:, 38:46], ev(lc, "alog"), AF.Exp, [lcb], [dvb])
    K.ts(dv[:, 38:46], dv[:, 38:46], -1.0, ALU.mult, [dvb], [dvb])


def prep_A(K, C, A, G, S, t0, T):
    lc, lcb = A["lc"]
    dv, dvb = A["dv"]
    cw = ev(lc, "cw")
    xin, xb = G["xin"]
    WD = TS + 4
    x3 = xin[:, 0:12 * WD].rearrange("p (c t) -> p c t", c=12)
    lo = max(t0 - 2, 0)
    hi = min(t0 + TS + 2, T)
    if lo > t0 - 2:
        K.memset(x3[:, :, 0:2], 0.0, [xb])
    if hi < t0 + TS + 2:
        K.memset(x3[:, :, TS + 2:TS + 4], 0.0, [xb])
    src = S["rawA"].ap[:, lo:hi].rearrange("(c p) t -> p c t", p=128)
    K.dma_in(x3[:, :, lo - (t0 - 2):hi - (t0 - 2)], src, xb, rd=S["rawA"].bufs(lo, hi))
    kfm, kfb = G["kfm"]
    vfm, vfb = G["vfm"]
    for c in range(12):
        acc, accb = G["acc"].next()
        K.ts(acc[:, 0:TS], x3[:, c, 0:TS], cw[:, c * 5:c * 5 + 1], ALU.mult, [xb, lcb], [accb])
        for k in range(1, 5):
            K.stt(acc[:, 0:TS], x3[:, c, k:k + TS], cw[:, c * 5 + k:c * 5 + k + 1], acc[:, 0:TS], ALU.mult, ALU.add, [xb, lcb, accb], [accb])
        if c >= 8:
            K.act(vfm[:, (c - 8) * TS:(c - 7) * TS], acc[:, 0:TS], AF.Silu, [accb], [vfb])
            continue
        sl, slb = G["acc"].next()
        K.act(sl[:, 0:TS], acc[:, 0:TS], AF.Silu, [accb], [slb])
        sq, sqb = G["sqr"].next()
        K.act(sq[:, 0:TS], sl[:, 0:TS], AF.Square, [slb], [sqb])
        ps, pb = K.psum()
        K.mm(ps[:, 0:TS], C["ones_r"], sq[:, 0:TS], True, True, [sqb, C["buf"]], [pb])
        rs, rb = C["rstd"].next()
        K.act(rs[:, 0:TS], ps[:, 0:TS], AF.Sqrt, [pb, C["buf"]], [rb], bias=C["eps6"], scale=1.0)
        K.recip(rs[:, 0:TS], rs[:, 0:TS], [rb], [rb])
        if c < 4:
            qo, qob = G["acc"].next()
            K.stt(qo[:, 0:TS], sl[:, 0:TS], 128.0 ** -0.5, rs[:, 0:TS], ALU.mult, ALU.mult, [slb, rb], [qob])
            K.dma_out(S["qT"].ap[c * 128:(c + 1) * 128, t0:t0 + TS], qo[:, 0:TS], qob, wr=S["qT"].bufs(t0, t0 + TS))
        else:
            hh = c - 4
            K.tt(kfm[:, hh * TS:(hh + 1) * TS], sl[:, 0:TS], rs[:, 0:TS], ALU.mult, [slb, rb], [kfb])
    K.dma_out(S["kT"].ap[:, t0:t0 + TS].rearrange("(c p) t -> p c t", p=128), kfm[:, 0:4 * TS].rearrange("p (c t) -> p c t", c=4), kfb,
              wr=S["kT"].bufs(t0, t0 + TS))
    for sub in range(TS // 128):
        tt0 = t0 + sub * 128
        for (src_t, src_b, dst) in ((kfm, kfb, S["ktok"]), (vfm, vfb, S["vtok"])):
            ps, pb = K.psum()
            for hh in range(4):
                K.mm(ps[:, hh * 128:(hh + 1) * 128], src_t[:, hh * TS + sub * 128:hh * TS + (sub + 1) * 128], C["ident"], True, True,
                     [src_b, C["buf"]], [pb])
            tk, tkb = G["tok"].next()
            K.copy(tk[:, 0:512], ps[:, 0:512], [pb], [tkb], eng="act")
            K.dma_out(dst.ap[tt0:tt0 + 128, :], tk[:, 0:512], tkb, wr=dst.bufs(tt0, tt0 + 128))
        ab, abb = G["gt"].next()
        K.dma_in(ab[:, 0:16], S["abt"].ap[tt0:tt0 + 128, :], abb, rd=S["abt"].bufs(tt0, tt0 + 128))
        g, gb_ = G["gt"].next()
        w_ = g[:, 16:64]
        K.tt(w_[:, 0:8], ab[:, 0:8], ev(lc, "dtb"), ALU.add, [abb, lcb], [gb_])
        K.ts(w_[:, 8:16], w_[:, 0:8], 0.0, ALU.max, [gb_], [gb_])
        K.act(w_[:, 16:24], w_[:, 0:8], AF.Abs, [gb_], [gb_])
        K.act(w_[:, 16:24], w_[:, 16:24], AF.Exp, [gb_], [gb_], scale=-1.0)
        K.ts(w_[:, 24:32], w_[:, 16:24], 2.0, ALU.add, [gb_], [gb_])
        K.recip(w_[:, 24:32], w_[:, 24:32], [gb_], [gb_])
        K.tt(w_[:, 24:32], w_[:, 24:32], w_[:, 16:24], ALU.mult, [gb_], [gb_])
        K.tt(w_[:, 32:40], w_[:, 24:32], w_[:, 24:32], ALU.mult, [gb_], [gb_])
        K.ts(w_[:, 40:48], w_[:, 32:40], 1.0 / 9, ALU.mult, [gb_], [gb_], s2=1.0 / 7, op1=ALU.add)
        for coef in (1.0 / 5, 1.0 / 3, 1.0):
            K.tt(w_[:, 40:48], w_[:, 40:48], w_[:, 32:40], ALU.mult, [gb_], [gb_])
            K.ts(w_[:, 40:48], w_[:, 40:48], coef, ALU.add, [gb_], [gb_])
        K.tt(w_[:, 40:48], w_[:, 40:48], w_[:, 24:32], ALU.mult, [gb_], [gb_])
        K.stt(w_[:, 40:48], w_[:, 40:48], 2.0, w_[:, 8:16], ALU.mult, ALU.add, [gb_], [gb_])
        K.tt(g[:, 0:8], w_[:, 40:48], dv[:, 38:46], ALU.mult, [gb_, dvb], [gb_])
        K.act(g[:, 8:16], ab[:, 8:16], AF.Sigmoid, [abb], [gb_])
        K.dma_out(S["gbt"].ap[tt0:tt0 + 128, :], g[:, 0:16], gb_, wr=S["gbt"].bufs(tt0, tt0 + 128))


def alloc_prepA(K):
    G = {}
    G["xin"] = (K.sb("pa_xin", [128, 12 * (TS + 4)]), Buf("pa_xin"))
    G["kfm"] = (K.sb("pa_kfm", [128, 4 * TS]), Buf("pa_kfm"))
    G["vfm"] = (K.sb("pa_vfm", [128, 4 * TS]), Buf("pa_vfm"))
    G["acc"] = Ring(K, "pa_acc", [128, TS], F32, 4)
    G["sqr"] = Ring(K, "pa_sq", [128, TS], F32R, 2)
    G["tok"] = Ring(K, "pa_tok", [128, 512], F32, 3)
    G["gt"] = Ring(K, "pa_gt", [128, 64], F32, 4)
    return G


def alloc_scanA(K):
    G = {}
    for nm in ("qT", "kT", "ktok", "vtok"):
        G[nm] = Ring(K, "sa_" + nm, [128, 512], F32, 2)
    for nm in ("of", "zt"):
        G[nm] = Ring(K, "sa_" + nm, [128, 512], F32, 1)
    G["gb"] = Ring(K, "sa_gb", [128, 16], F32, 2)
    G["sm"] = Ring(K, "sa_sm", [128, 64], F32, 2)
    for nm in ("GM", "Dm", "egr", "QT", "tmpD", "vb", "kbg", "us", "wTs", "qkm", "qkTs", "qdT", "vnew", "S", "ot", "sqt", "yts", "kd0", "kd1"):
        G[nm] = (K.sb("sa_" + nm, [128, 512]), Buf("sa_" + nm))
    G["QX"] = (K.sb("sa_QX", [128, 1024]), Buf("sa_QX"))
    return G


def v3(ap, n=4):
    return ap.rearrange("p (h t) -> p h t", h=n)


def bc_last(ap, n=128):
    return ap.unsqueeze(2).to_broadcast([ap.shape[0], ap.shape[1], n])


def bc_mid(ap, k=4):
    return ap.unsqueeze(1).to_broadcast([ap.shape[0], k, ap.shape[1]])


def scan_A(K, C, A, G, S, d, T):
    lc, lcb = A["lc"]
    NT = T // 128
    cb = C["buf"]
    Sst, Sb = G["S"]
    K.memset(Sst[:, 0:512], 0.0, [Sb])
    INCL, AFTER, NEGINCL = C["INCL%d" % d], C["AFTER%d" % d], C["NEGINCL%d" % d]
    order = range(NT) if d == 0 else range(NT - 1, -1, -1)
    corder = (0, 1) if d == 0 else (1, 0)
    for ti in order:
        t0 = ti * 128
        qT, qTb = G["qT"].next()
        kT, kTb = G["kT"].next()
        ktok, ktb = G["ktok"].next()
        vtok, vtb = G["vtok"].next()
        gb, gbb = G["gb"].next()
        K.dma_in(v3(qT[:, 0:512]), S["qT"].ap[:, t0:t0 + 128].rearrange("(h p) t -> p h t", p=128), qTb, rd=S["qT"].bufs(t0, t0 + 128))
        K.dma_in(v3(kT[:, 0:512]), S["kT"].ap[:, t0:t0 + 128].rearrange("(h p) t -> p h t", p=128), kTb, rd=S["kT"].bufs(t0, t0 + 128))
        K.dma_in(ktok[:, 0:512], S["ktok"].ap[t0:t0 + 128, :], ktb, rd=S["ktok"].bufs(t0, t0 + 128))
        K.dma_in(vtok[:, 0:512], S["vtok"].ap[t0:t0 + 128, :], vtb, rd=S["vtok"].bufs(t0, t0 + 128))
        K.dma_in(gb[:, 0:16], S["gbt"].ap[t0:t0 + 128, :], gbb, rd=S["gbt"].bufs(t0, t0 + 128))
        g = gb[:, d * 4:d * 4 + 4]
        beta = gb[:, 8 + d * 4:12 + d * 4]
        sm, smb = G["sm"].next()
        ps1, pb1 = K.psum()
        K.mm(ps1[:, 0:4], INCL, g, True, True, [cb, gbb], [pb1])
        K.mm(ps1[:, 4:8], AFTER, g, True, True, [cb, gbb], [pb1])
        K.mm(ps1[:, 8:12], C["CMROW0"], g, True, True, [cb, gbb], [pb1])
        K.mm(ps1[:, 12:16], C["CMROW1"], g, True, True, [cb, gbb], [pb1])
        K.act(sm[:, 0:16], ps1[:, 0:16], AF.Exp, [pb1], [smb])
        egc, egrev = sm[:, 0:4], sm[:, 4:8]
        K.tt(sm[:, 16:20], beta, egc, ALU.mult, [gbb, smb], [smb])
        K.ts(sm[:, 20:24], beta, -1.0, ALU.mult, [gbb], [smb])
        K.tt(sm[:, 24:32].rearrange("p (c h) -> p c h", c=2), bc_mid(egrev, 2), bc_last(C["CHI"], 4), ALU.mult, [smb, cb], [smb])
        yield
        GM, GMb = G["GM"]
        K.tt(v3(GM[:, 0:512]), bc_mid(INCL), bc_last(g), ALU.mult, [cb, gbb], [GMb])
        pd, pdb = K.psum()
        pg, pgb = K.psum()
        for h in range(4):
            hs = slice(h * 128, (h + 1) * 128)
            K.mm(pd[:, hs], GM[:, hs], C["ones"], True, False, [GMb, cb], [pdb])
            K.mm(pd[:, hs], C["nones"], GM[:, hs], False, False, [GMb, cb], [pdb])
            K.mm(pd[:, hs], C["ident"], NEGINCL, False, True, [cb], [pdb])
            K.mm(pg[:, hs], C["ones"], GM[:, hs], True, True, [GMb, cb], [pgb])
        Dm, Dmb = G["Dm"]
        egr, egrb = G["egr"]
        K.act(Dm[:, 0:512], pd[:, 0:512], AF.Exp, [pdb], [Dmb])
        K.act(egr[:, 0:512], pg[:, 0:512], AF.Exp, [pgb], [egrb])
        yield
        pk, pkb = K.psum()
        for h in range(4):
            hs = slice(h * 128, (h + 1) * 128)
            K.mm(pk[:, hs], kT[:, hs], kT[:, hs], True, True, [kTb], [pkb])
        tmpD, tDb = G["tmpD"]
        K.tt(v3(tmpD[:, 0:512]), v3(Dm[:, 0:512]), bc_mid(C["OFFD"]), ALU.mult, [Dmb, cb], [tDb])
        K.tt(v3(tmpD[:, 0:512]), v3(tmpD[:, 0:512]), bc_last(sm[:, 20:24]), ALU.mult, [tDb, smb], [tDb])
        QT, QTb = G["QT"]
        QX, QXb = G["QX"]
        qx3 = QX[:, 0:1024].rearrange("p (h t) -> p h t", h=4)
        K.tt(QT[:, 0:512], pk[:, 0:512], tmpD[:, 0:512], ALU.mult, [pkb, tDb], [QTb])
        pq0, pq0b = K.psum()
        for h in range(4):
            hs = slice(h * 128, (h + 1) * 128)
            K.mm(pq0[:, hs], QT[:, hs], C["ident"], True, True, [QTb, cb], [pq0b])
        K.copy(qx3[:, :, 0:128], v3(pq0[:, 0:512]), [pq0b], [QXb], eng="act")
        K.copy(qx3[:, :, 128:256], bc_mid(C["ident"]), [cb], [QXb], eng="pool")
        for k in range(6):
            yield
            lastk = k == 5
            pf = [K.psum(), K.psum()]
            for h in range(4):
                pft, pfb = pf[h // 2]
                o0 = (h % 2) * 256
                if lastk:
                    K.mm(pft[:, o0 + 128:o0 + 256], QT[:, h * 128:(h + 1) * 128], qx3[:, h, 128:256], True, True, [QTb, QXb], [pfb])
                else:
                    K.mm(pft[:, o0:o0 + 256], QT[:, h * 128:(h + 1) * 128], qx3[:, h, :], True, True, [QTb, QXb], [pfb])
            if not lastk:
                ptq, ptqb = K.psum()
                for h in range(4):
                    hs = slice(h * 128, (h + 1) * 128)
                    K.mm(ptq[:, hs], qx3[:, h, 0:128], QT[:, hs], True, True, [QTb, QXb], [ptqb])
            for hp in range(2):
                pft, pfb = pf[hp]
                pf3 = pft[:, 0:512].rearrange("p (h t) -> p h t", h=2)
                if not lastk:
                    K.copy(qx3[:, 2 * hp:2 * hp + 2, 0:128], pf3[:, :, 0:128], [pfb], [QXb], eng="act")
                K.tt(qx3[:, 2 * hp:2 * hp + 2, 128:256], qx3[:, 2 * hp:2 * hp + 2, 128:256], pf3[:, :, 128:256], ALU.add, [pfb, QXb], [QXb])
            if not lastk:
                K.copy(QT[:, 0:512], ptq[:, 0:512], [ptqb], [QTb], eng="act")
        yield
        vb, vbb = G["vb"]
        kbg, kbgb = G["kbg"]
        K.tt(v3(vb[:, 0:512]), v3(vtok[:, 0:512]), bc_last(beta), ALU.mult, [vtb, gbb], [vbb])
        K.tt(v3(kbg[:, 0:512]), v3(ktok[:, 0:512]), bc_last(sm[:, 16:20]), ALU.mult, [ktb, smb], [kbgb])
        pu, pub = K.psum()
        pw, pwb = K.psum()
        pqk, pqkb = K.psum()
        for h in range(4):
            hs = slice(h * 128, (h + 1) * 128)
            K.mm(pu[:, hs], qx3[:, h, 128:256], vb[:, hs], True, True, [QXb, vbb], [pub])
            K.mm(pw[:, hs], kbg[:, hs], qx3[:, h, 128:256], True, True, [QXb, kbgb], [pwb])
            K.mm(pqk[:, hs], qT[:, hs], kT[:, hs], True, True, [qTb, kTb], [pqkb])
        us, usb = G["us"]
        wTs, wTb = G["wTs"]
        qkm, qkmb = G["qkm"]
        K.copy(us[:, 0:512], pu[:, 0:512], [pub], [usb], eng="act")
        K.copy(wTs[:, 0:512], pw[:, 0:512], [pwb], [wTb], eng="act")
        K.tt(qkm[:, 0:512], pqk[:, 0:512], Dm[:, 0:512], ALU.mult, [pqkb, Dmb], [qkmb])
        yield
        pqt, pqtb = K.psum()
        for h in range(4):
            hs = slice(h * 128, (h + 1) * 128)
            K.mm(pqt[:, hs], qkm[:, hs], C["ident"], True, True, [qkmb, cb], [pqtb])
        qkTs, qkTb = G["qkTs"]
        K.copy(qkTs[:, 0:512], pqt[:, 0:512], [pqtb], [qkTb], eng="act")
        qdT, qdTb = G["qdT"]
        K.tt(qdT[:, 0:512], qT[:, 0:512], egr[:, 0:512], ALU.mult, [qTb, egrb], [qdTb], eng="pool")
        kd = [G["kd0"], G["kd1"]]
        for cc in range(2):
            kdt, kdb = kd[cc]
            K.tt(v3(kdt[:, 0:512]), v3(ktok[:, 0:512]), bc_last(sm[:, 24 + cc * 4:28 + cc * 4]), ALU.mult, [ktb, smb], [kdb], eng="pool")
        vnew, vnb = G["vnew"]
        ot, otb = G["ot"]
        for cc in corder:
            yield
            kdt, kdb = kd[cc]
            pws, pwsb = K.psum()
            for h in range(4):
                hs = slice(h * 128, (h + 1) * 128)
                K.mm(pws[:, hs], wTs[:, hs], Sst[:, hs], True, True, [wTb, Sb], [pwsb])
            K.tt(vnew[:, 0:512], us[:, 0:512], pws[:, 0:512], ALU.subtract, [usb, pwsb], [vnb])
            yield
            po, pob = K.psum()
            pss, pssb = K.psum()
            for h in range(4):
                hs = slice(h * 128, (h + 1) * 128)
                K.mm(po[:, hs], qdT[:, hs], Sst[:, hs], True, False, [qdTb, Sb], [pob])
                K.mm(po[:, hs], qkTs[:, hs], vnew[:, hs], False, True, [qkTb, vnb], [pob])
                K.mm(pss[:, hs], kdt[:, hs], vnew[:, hs], True, True, [kdb, vnb], [pssb])
            rows = slice(cc * 64, (cc + 1) * 64)
            K.copy(ot[rows, 0:512], po[rows, 0:512], [pob], [otb], eng="act")
            K.tt(v3(Sst[:, 0:512]), v3(Sst[:, 0:512]), bc_last(sm[:, 8 + cc * 4:12 + cc * 4]), ALU.mult, [Sb, smb], [Sb])
            K.tt(Sst[:, 0:512], Sst[:, 0:512], pss[:, 0:512], ALU.add, [Sb, pssb], [Sb])
        yield
        if d == 0:
            K.dma_out(S["oAf"].ap[t0:t0 + 128, :], ot[:, 0:512], otb, wr=S["oAf"].bufs(t0, t0 + 128))
        else:
            of, ofb = G["of"].next()
            zt, ztb = G["zt"].next()
            K.dma_in(of[:, 0:512], S["oAf"].ap[t0:t0 + 128, :], ofb, rd=S["oAf"].bufs(t0, t0 + 128))
            K.dma_in(zt[:, 0:512], S["zt"].ap[t0:t0 + 128, :], ztb, rd=S["zt"].bufs(t0, t0 + 128))
            K.tt(ot[:, 0:512], ot[:, 0:512], of[:, 0:512], ALU.add, [otb, ofb], [otb])
            sqt, sqtb = G["sqt"]
            K.act(sqt[:, 0:512], ot[:, 0:512], AF.Square, [otb], [sqtb])
            sm2, sm2b = G["sm"].next()
            K.P.add("dve", lambda e, o_=sm2[:, 0:4], i_=v3(sqt[:, 0:512]): e.tensor_reduce(out=o_, in_=i_, axis=AX.X, op=ALU.add),
                    reads=[sqtb], writes=[sm2b])
            K.act(sm2[:, 0:4], sm2[:, 0:4], AF.Sqrt, [sm2b, cb], [sm2b], bias=C["eps6"], scale=1.0 / 128)
            K.recip(sm2[:, 0:4], sm2[:, 0:4], [sm2b], [sm2b])
            K.tt(v3(ot[:, 0:512]), v3(ot[:, 0:512]), bc_last(sm2[:, 0:4]), ALU.mult, [otb, sm2b], [otb])
            K.tt(ot[:, 0:512], ot[:, 0:512], ev(lc, "anorm"), ALU.mult, [otb, lcb], [otb])
            K.act(sqt[:, 0:512], zt[:, 0:512], AF.Silu, [ztb], [sqtb])
            K.tt(ot[:, 0:512], ot[:, 0:512], sqt[:, 0:512], ALU.mult, [otb, sqtb], [otb])
            py, pyb = K.psum()
            for h in range(4):
                hs = slice(h * 128, (h + 1) * 128)
                K.mm(py[:, hs], ot[:, hs], C["ident"], True, True, [otb, cb], [pyb])
            yts, ytsb = G["yts"]
            K.copy(yts[:, 0:512], py[:, 0:512], [pyb], [ytsb], eng="act")
            K.dma_out(S["yT"].ap[0:512, t0:t0 + 128].rearrange("(h p) t -> p h t", p=128), v3(yts[:, 0:512]), ytsb,
                      wr=S["yT"].bufs(t0, t0 + 128))


SCALE_W = 0.606531


def alloc_prepB(K):
    G = {}
    G["xin"] = (K.sb("pb_xin", [128, 12 * (TS + 2)]), Buf("pb_xin"))
    G["xs"] = (K.sb("pb_xs", [128, 3 * (TS + 2)]), Buf("pb_xs"))
    G["rkv"] = (K.sb("pb_rkv", [128, 12 * TS]), Buf("pb_rkv"))
    G["sml"] = (K.sb("pb_sml", [128, 3 * TS]), Buf("pb_sml"))
    G["kk"] = (K.sb("pb_kk", [128, 4 * TS]), Buf("pb_kk"))
    for d in range(2):
        G["bT%d" % d] = (K.sb("pb_bT%d" % d, [128, 4 * TS]), Buf("pb_bT%d" % d))
        G["kdT%d" % d] = (K.sb("pb_kdT%d" % d, [128, 4 * TS]), Buf("pb_kdT%d" % d))
    G["prod"] = (K.sb("pb_prod", [128, 4 * TS]), Buf("pb_prod"))
    G["tmp"] = Ring(K, "pb_tmp", [128, TS], F32, 4)
    G["sqr"] = Ring(K, "pb_sq", [128, TS], F32R, 2)
    G["tok"] = Ring(K, "pb_tok", [128, 512], F32, 4)
    G["t8"] = Ring(K, "pb_t8", [128, 8], F32, 2)
    return G


def prep_B(K, C, A, G, S, t0, T):
    lc, lcb = A["lc"]
    dv, dvb = A["dv"]
    cb = C["buf"]
    WD = TS + 2
    xin, xb = G["xin"]
    xs, xsb = G["xs"]
    x3 = xin[:, 0:12 * WD].rearrange("p (c t) -> p c t", c=12)
    s3 = xs[:, 0:3 * WD].rearrange("p (c t) -> p c t", c=3)
    lo = max(t0 - 1, 0)
    hi = min(t0 + TS + 1, T)
    if lo > t0 - 1:
        K.memset(x3[:, :, 0:1], 0.0, [xb])
        K.memset(s3[:, :, 0:1], 0.0, [xsb])
    if hi < t0 + TS + 1:
        K.memset(x3[:, :, TS + 1:TS + 2], 0.0, [xb])
        K.memset(s3[:, :, TS + 1:TS + 2], 0.0, [xsb])
    o0, o1 = lo - (t0 - 1), hi - (t0 - 1)
    rd = S["rawB"].bufs(lo, hi)
    K.dma_in(x3[:, :, o0:o1], S["rawB"].ap[0:1536, lo:hi].rearrange("(c p) t -> p c t", p=128), xb, rd=rd)
    K.dma_in(s3[0:64, 0, o0:o1], S["rawB"].ap[1536:1600, lo:hi], xsb, rd=rd)
    K.dma_in(s3[0:64, 1, o0:o1], S["rawB"].ap[1600:1664, lo:hi], xsb, rd=rd)
    K.dma_in(s3[0:96, 2, o0:o1], S["rawB"].ap[1664:1760, lo:hi], xsb, rd=rd)
    rkv, rkvb = G["rkv"]
    sml, smlb = G["sml"]

    def lerp(dst, dstb, src3, c, np_, hmu, omu, srcb):
        tmp, tb = G["tmp"].next()
        K.tt(tmp[0:np_, 0:TS], src3[0:np_, c, 0:TS], src3[0:np_, c, 2:TS + 2], ALU.add, [srcb], [tb], eng="pool")
        K.ts(tmp[0:np_, 0:TS], tmp[0:np_, 0:TS], hmu, ALU.mult, [tb, dvb], [tb])
        K.stt(dst, src3[0:np_, c, 1:TS + 1], omu, tmp[0:np_, 0:TS], ALU.mult, ALU.add, [srcb, tb, dvb], [dstb])

    for c in range(12):
        lerp(rkv[:, c * TS:(c + 1) * TS], rkvb, x3, c, 128, dv[:, c:c + 1], dv[:, 12 + c:13 + c], xb)
    lerp(sml[0:64, 0:TS], smlb, s3, 0, 64, dv[0:64, 24:25], dv[0:64, 25:26], xsb)
    lerp(sml[0:64, TS:2 * TS], smlb, s3, 1, 64, dv[0:64, 26:27], dv[0:64, 27:28], xsb)
    lerp(sml[0:96, 2 * TS:3 * TS], smlb, s3, 2, 96, dv[0:96, 28:29], dv[0:96, 29:30], xsb)
    K.act(sml[0:64, 0:TS], sml[0:64, 0:TS], AF.Tanh, [smlb], [smlb])
    K.act(sml[0:96, 2 * TS:3 * TS], sml[0:96, 2 * TS:3 * TS], AF.Sigmoid, [smlb], [smlb])
    twd = sml[:, 0:TS]
    adl = sml[:, TS:2 * TS]
    sgd = sml[:, 2 * TS:3 * TS]
    rT = rkv[:, 0:4 * TS]
    kT = rkv[:, 4 * TS:8 * TS]
    vT = rkv[:, 8 * TS:12 * TS]
    K.dma_out(S["rTB"].ap[:, t0:t0 + TS].rearrange("(c p) t -> p c t", p=128), rT.rearrange("p (c t) -> p c t", c=4), rkvb,
              wr=S["rTB"].bufs(t0, t0 + TS))
    kk, kkb = G["kk"]
    for c in range(4):
        t1, t1b = G["tmp"].next()
        K.ts(t1[:, 0:TS], kT[:, c * TS:(c + 1) * TS], ev(lc, "k_k")[:, c:c + 1], ALU.mult, [rkvb, lcb], [t1b])
        sq, sqb = G["sqr"].next()
        K.act(sq[:, 0:TS], t1[:, 0:TS], AF.Square, [t1b], [sqb])
        ps, pb = K.psum()
        K.mm(ps[:, 0:TS], C["blk_r"], sq[:, 0:TS], True, True, [sqb, cb], [pb])
        rs, rb = C["rstd"].next()
        K.act(rs[:, 0:TS], ps[:, 0:TS], AF.Sqrt, [pb, cb], [rb], bias=C["eps6"], scale=1.0)
        K.recip(rs[:, 0:TS], rs[:, 0:TS], [rb], [rb])
        K.tt(kk[:, c * TS:(c + 1) * TS], t1[:, 0:TS], rs[:, 0:TS], ALU.mult, [t1b, rb], [kkb])
    K.dma_out(S["kkT"].ap[:, t0:t0 + TS].rearrange("(c p) t -> p c t", p=128), kk[:, 0:4 * TS].rearrange("p (c t) -> p c t", c=4), kkb,
              wr=S["kkT"].bufs(t0, t0 + TS))
    a2s, a0, k_a = ev(lc, "a2s"), ev(lc, "a0"), ev(lc, "k_a")
    for d in range(2):
        bT, bTb = G["bT%d" % d]
        kdT, kdTb = G["kdT%d" % d]
        pr = slice(32 * d, 32 * d + 32)
        for c in range(4):
            ps, pb = K.psum()
            K.mm(ps[:, 0:TS], a2s[pr, c * 128:(c + 1) * 128], adl[pr, 0:TS], True, True, [lcb, smlb], [pb])
            al, alb = G["tmp"].next()
            K.act(al[:, 0:TS], ps[:, 0:TS], AF.Sigmoid, [pb, lcb], [alb], bias=a0[:, d * 4 + c:d * 4 + c + 1], scale=1.0)
            K.tt(bT[:, c * TS:(c + 1) * TS], kk[:, c * TS:(c + 1) * TS], al[:, 0:TS], ALU.mult, [kkb, alb], [bTb])
            K.ts(al[:, 0:TS], al[:, 0:TS], k_a[:, c:c + 1], ALU.mult, [alb, lcb, dvb], [alb], s2=dv[:, 30 + c:31 + c], op1=ALU.add)
            K.tt(kdT[:, c * TS:(c + 1) * TS], kT[:, c * TS:(c + 1) * TS], al[:, 0:TS], ALU.mult, [rkvb, alb], [kdTb])
        K.dma_out(S["bT%d" % d].ap[:, t0:t0 + TS].rearrange("(c p) t -> p c t", p=128), bT[:, 0:4 * TS].rearrange("p (c t) -> p c t", c=4), bTb,
                  wr=S["bT%d" % d].bufs(t0, t0 + TS))
        K.dma_out(S["kdT%d" % d].ap[:, t0:t0 + TS].rearrange("(c p) t -> p c t", p=128), kdT[:, 0:4 * TS].rearrange("p (c t) -> p c t", c=4), kdTb,
                  wr=S["kdT%d" % d].bufs(t0, t0 + TS))
    prod, prb = G["prod"]
    kd0, kd0b = G["kdT0"]
    kd1, kd1b = G["kdT1"]
    for c in range(4):
        cs = slice(c * TS, (c + 1) * TS)
        K.tt(prod[:, cs], kd0[:, cs], kd1[:, cs], ALU.add, [kd0b, kd1b], [prb], eng="pool")
        K.stt(prod[:, cs], prod[:, cs], dv[:, 34 + c:35 + c], rT[:, cs], ALU.mult, ALU.mult, [prb, dvb, rkvb], [prb])
    w2s, w0s, g2 = ev(lc, "w2s"), ev(lc, "w0s"), ev(lc, "g2")
    for sub in range(TS // 128):
        tt0 = t0 + sub * 128
        cols = slice(sub * 128, (sub + 1) * 128)
        trans = [(vT, rkvb, S["vtokB"])]
        for d in range(2):
            trans.append((G["bT%d" % d][0], G["bT%d" % d][1], S["btok%d" % d]))
            trans.append((G["kdT%d" % d][0], G["kdT%d" % d][1], S["kdtok%d" % d]))
        for i_, (src_t, src_b, dst) in enumerate(trans):
            ps, pb = K.psum()
            for c in range(4):
                K.mm(ps[:, c * 128:(c + 1) * 128], src_t[:, c * TS + sub * 128:c * TS + (sub + 1) * 128], C["ident"], True, True, [src_b, cb], [pb])
            tk, tkb = G["tok"].next()
            K.copy(tk[:, 0:512], ps[:, 0:512], [pb], [tkb], eng="act" if i_ % 2 else "dve")
            K.dma_out(dst.ap[tt0:tt0 + 128, :], tk[:, 0:512], tkb, wr=dst.bufs(tt0, tt0 + 128))
        for d in range(2):
            pr = slice(32 * d, 32 * d + 32)
            ps, pb = K.psum()
            K.mm(ps[:, 0:512], twd[pr, cols], w2s[pr, :], True, False, [smlb, lcb], [pb])
            K.mm(ps[:, 0:512], C["ones"][32 * d:32 * d + 1, :], w0s[32 * d:32 * d + 1, :], False, True, [cb, lcb], [pb])
            tk, tkb = G["tok"].next()
            K.act(tk[:, 0:512], ps[:, 0:512], AF.Sigmoid, [pb], [tkb])
            K.dma_out(S["sg%d" % d].ap[tt0:tt0 + 128, :], tk[:, 0:512], tkb, wr=S["sg%d" % d].bufs(tt0, tt0 + 128))
        ps, pb = K.psum()
        K.mm(ps[:, 0:512], sgd[0:96, cols], g2[0:96, :], True, True, [smlb, lcb], [pb])
        tk, tkb = G["tok"].next()
        K.copy(tk[:, 0:512], ps[:, 0:512], [pb], [tkb], eng="act")
        K.dma_out(S["gtok"].ap[tt0:tt0 + 128, :], tk[:, 0:512], tkb, wr=S["gtok"].bufs(tt0, tt0 + 128))
        ps, pb = K.psum()
        for c in range(4):
            K.mm(ps[:, 2 * c:2 * c + 2], prod[:, c * TS + sub * 128:c * TS + (sub + 1) * 128], C["HSEL"], True, True, [prb, cb], [pb])
        t8, t8b = G["t8"].next()
        K.copy(t8[:, 0:8], ps[:, 0:8], [pb], [t8b], eng="dve")
        K.dma_out(S["bons"].ap[tt0:tt0 + 128, :], t8[:, 0:8], t8b, wr=S["bons"].bufs(tt0, tt0 + 128))


def alloc_scanB(K):
    G = {}
    for nm in ("sg", "btok", "kdtok", "vtok"):
        G[nm] = Ring(K, "sb_" + nm, [128, 512], F32, 2)
    for nm in ("yf", "gt"):
        G[nm] = Ring(K, "sb_" + nm, [128, 512], F32, 1)
    for nm in ("rT", "kkT", "bT", "kdT"):
        G[nm] = Ring(K, "sb_" + nm, [128, 1024], F32, 1)
    for nm in ("Ei", "Ev", "Ex", "rt", "at", "bt", "kt"):
        G[nm] = (K.sb("sb_" + nm, [128, 1024]), Buf("sb_" + nm))
    G["bon"] = Ring(K, "sb_bon", [128, 8], F32, 2)
    G["sm"] = Ring(K, "sb_sm", [128, 64], F32, 2)
    for nm in ("Er", "bg", "kg", "bgm", "kgm", "Xs", "Us", "S", "yt", "sqt", "yts",
               "QTa", "QTb", "AakA", "AakB", "ArbA", "ArbB", "ArkA", "ArkB"):
        G[nm] = (K.sb("sb_" + nm, [128, 512]), Buf("sb_" + nm))
    G["QXa"] = (K.sb("sb_QXa", [128, 1024]), Buf("sb_QXa"))
    G["QXb"] = (K.sb("sb_QXb", [128, 1024]), Buf("sb_QXb"))
    return G


def invert4(K, C, QT, QTb, QX, QXb):
    cb = C["buf"]
    qx3 = QX[:, 0:1024].rearrange("p (h t) -> p h t", h=4)
    pq0, pq0b = K.psum()
    for h in range(4):
        hs = slice(h * 128, (h + 1) * 128)
        K.mm(pq0[:, hs], QT[:, hs], C["ident"], True, True, [QTb, cb], [pq0b])
    K.copy(qx3[:, :, 0:128], v3(pq0[:, 0:512]), [pq0b], [QXb], eng="act")
    K.copy(qx3[:, :, 128:256], bc_mid(C["ident"]), [cb], [QXb], eng="pool")
    for k in range(6):
        yield
        lastk = k == 5
        pf = [K.psum(), K.psum()]
        for h in range(4):
            pft, pfb = pf[h // 2]
            o0 = (h % 2) * 256
            if lastk:
                K.mm(pft[:, o0 + 128:o0 + 256], QT[:, h * 128:(h + 1) * 128], qx3[:, h, 128:256], True, True, [QTb, QXb], [pfb])
            else:
                K.mm(pft[:, o0:o0 + 256], QT[:, h * 128:(h + 1) * 128], qx3[:, h, :], True, True, [QTb, QXb], [pfb])
        if not lastk:
            ptq, ptqb = K.psum()
            for h in range(4):
                hs = slice(h * 128, (h + 1) * 128)
                K.mm(ptq[:, hs], qx3[:, h, 0:128], QT[:, hs], True, True, [QTb, QXb], [ptqb])
        for hp in range(2):
            pft, pfb = pf[hp]
            pf3 = pft[:, 0:512].rearrange("p (h t) -> p h t", h=2)
            if not lastk:
                K.copy(qx3[:, 2 * hp:2 * hp + 2, 0:128], pf3[:, :, 0:128], [pfb], [QXb], eng="act")
            K.tt(qx3[:, 2 * hp:2 * hp + 2, 128:256], qx3[:, 2 * hp:2 * hp + 2, 128:256], pf3[:, :, 128:256], ALU.add, [pfb, QXb], [QXb])
        if not lastk:
            K.copy(QT[:, 0:512], ptq[:, 0:512], [ptqb], [QTb], eng="act")
    return qx3


def scan_B(K, C, A, G, S, d, T):
    lc, lcb = A["lc"]
    NT = T // 128
    cb = C["buf"]
    Sst, Sb = G["S"]
    K.memset(Sst[:, 0:512], 0.0, [Sb])
    INCL, STRICT, AFTER = C["INCL%d" % d], C["STRICT%d" % d], C["AFTER%d" % d]
    order = range(NT) if d == 0 else range(NT - 1, -1, -1)
    corder = (0, 1) if d == 0 else (1, 0)
    for ti in order:
        t0 = ti * 128
        ld = {}
        for nm, dt_, fm in (("sg", S["sg%d" % d], False), ("rT", S["rTB"], True), ("kkT", S["kkT"], True), ("bT", S["bT%d" % d], True),
                            ("kdT", S["kdT%d" % d], True), ("btok", S["btok%d" % d], False), ("kdtok", S["kdtok%d" % d], False),
                            ("vtok", S["vtokB"], False)):
            tl, tb = G[nm].next()
            if fm:
                K.dma_in(tl[0:64, 0:1024].rearrange("p (h t) -> p h t", h=8), dt_.ap[:, t0:t0 + 128].rearrange("(h p) t -> p h t", p=64), tb,
                         rd=dt_.bufs(t0, t0 + 128))
            else:
                K.dma_in(tl[:, 0:512], dt_.ap[t0:t0 + 128, :], tb, rd=dt_.bufs(t0, t0 + 128))
            ld[nm] = (tl, tb)
        sg, sgb = ld["sg"]
        rT, rTb = ld["rT"]
        kkT, kkTb = ld["kkT"]
        bT, bTb = ld["bT"]
        kdT, kdTb = ld["kdT"]
        btok, btokb = ld["btok"]
        kdtok, kdtokb = ld["kdtok"]
        vtok, vtb = ld["vtok"]
        Ei, Eib = G["Ei"]
        Ev, Evb = G["Ev"]
        Ex, Exb = G["Ex"]
        for q4 in range(4):
            pa_, pab = K.psum()
            for h2 in range(2):
                h = q4 * 2 + h2
                K.mm(pa_[0:64, h2 * 256:h2 * 256 + 128], sg[:, h * 64:(h + 1) * 64], INCL, True, True, [sgb, cb], [pab])
                K.mm(pa_[0:64, h2 * 256 + 128:h2 * 256 + 256], sg[:, h * 64:(h + 1) * 64], STRICT, True, True, [sgb, cb], [pab])
            pa3 = pa_[0:64, 0:512].rearrange("p (c t) -> p c t", c=2)
            hs = slice(q4 * 256, (q4 + 1) * 256)
            K.act(Ei[0:64, hs].rearrange("p (c t) -> p c t", c=2), pa3[:, :, 0:128], AF.Exp, [pab], [Eib], scale=-SCALE_W)
            K.act(Ev[0:64, hs].rearrange("p (c t) -> p c t", c=2), pa3[:, :, 0:128], AF.Exp, [pab], [Evb], scale=SCALE_W)
            K.act(Ex[0:64, hs].rearrange("p (c t) -> p c t", c=2), pa3[:, :, 128:256], AF.Exp, [pab], [Exb], scale=-SCALE_W)
        sm, smb = G["sm"].next()
        pc, pcb = K.psum()
        for h in range(8):
            K.mm(pc[0:64, 2 * h:2 * h + 2], sg[:, h * 64:(h + 1) * 64], C["CHI"], True, True, [sgb, cb], [pcb])
        K.act(sm[0:64, 0:16], pc[0:64, 0:16], AF.Exp, [pcb], [smb], scale=-SCALE_W)
        yield
        prv, prvb = K.psum()
        K.mm(prv[:, 0:512], AFTER, sg[:, 0:512], True, True, [sgb, cb], [prvb])
        Er, Erb = G["Er"]
        K.act(Er[:, 0:512], prv[:, 0:512], AF.Exp, [prvb], [Erb], scale=-SCALE_W)
        rt, rtb = G["rt"]
        at, atb = G["at"]
        bt, btb = G["bt"]
        kt, ktb_ = G["kt"]
        K.tt(rt[0:64, 0:1024], rT[0:64, 0:1024], Ei[0:64, 0:1024], ALU.mult, [rTb, Eib], [rtb])
        K.stt(at[0:64, 0:1024], kkT[0:64, 0:1024], -1.0, Ex[0:64, 0:1024], ALU.mult, ALU.mult, [kkTb, Exb], [atb])
        K.tt(bt[0:64, 0:1024], bT[0:64, 0:1024], Ev[0:64, 0:1024], ALU.mult, [bTb, Evb], [btb], eng="pool")
        K.tt(kt[0:64, 0:1024], kdT[0:64, 0:1024], Ev[0:64, 0:1024], ALU.mult, [kdTb, Evb], [ktb_], eng="pool")
        bg, bgb = G["bg"]
        kg, kgb = G["kg"]
        K.tt(bg[:, 0:512], btok[:, 0:512], Er[:, 0:512], ALU.mult, [btokb, Erb], [bgb])
        K.tt(kg[:, 0:512], kdtok[:, 0:512], Er[:, 0:512], ALU.mult, [kdtokb, Erb], [kgb], eng="pool")
        def hsl(t_, h):
            return t_[0:64, h * 128:(h + 1) * 128]

        QTs = [G["QTa"], G["QTb"]]
        Aak = [G["AakA"], G["AakB"]]
        Arb = [G["ArbA"], G["ArbB"]]
        Ark = [G["ArkA"], G["ArkB"]]
        for grp in range(2):
            yield
            for (dst, lh, lhb, rh, rhb, mask) in ((QTs[grp], at, atb, bt, btb, AFTER), (Aak[grp], kt, ktb_, at, atb, STRICT),
                                                  (Arb[grp], bt, btb, rt, rtb, INCL), (Ark[grp], kt, ktb_, rt, rtb, INCL)):
                ps, pb = K.psum()
                for h4 in range(4):
                    h = grp * 4 + h4
                    K.mm(ps[:, h4 * 128:(h4 + 1) * 128], hsl(lh, h), hsl(rh, h), True, True, [lhb, rhb], [pb])
                K.tt(v3(dst[0][:, 0:512]), v3(ps[:, 0:512]), bc_mid(mask), ALU.mult, [pb, cb], [dst[1]])
        QX = [G["QXa"], G["QXb"]]
        X3 = []
        for grp in range(2):
            x3_ = yield from invert4(K, C, QTs[grp][0], QTs[grp][1], QX[grp][0], QX[grp][1])
            X3.append(x3_)
        Xs, Xsb = G["Xs"]
        Us, Usb = G["Us"]
        yt, ytb = G["yt"]
        bgm, bgmb = G["bgm"]
        kgm, kgmb = G["kgm"]
        for cc in corder:
            yield
            K.ts(bgm[:, 0:512], bg[:, 0:512], C["CHI"][:, cc:cc + 1], ALU.mult, [bgb, cb], [bgmb])
            K.ts(kgm[:, 0:512], kg[:, 0:512], C["CHI"][:, cc:cc + 1], ALU.mult, [kgb, cb], [kgmb])
            px, pxb = K.psum()
            for h in range(8):
                vs = slice(h * 64, (h + 1) * 64)
                K.mm(px[:, vs], hsl(at, h), Sst[0:64, vs], True, False, [atb, Sb], [pxb])
                K.mm(px[:, vs], Aak[h // 4][0][:, (h % 4) * 128:(h % 4 + 1) * 128], vtok[:, vs], False, True, [Aak[h // 4][1], vtb], [pxb])
            K.copy(Xs[:, 0:512], px[:, 0:512], [pxb], [Xsb], eng="act")
            yield
            pu, pub = K.psum()
            for h in range(8):
                vs = slice(h * 64, (h + 1) * 64)
                K.mm(pu[:, vs], X3[h // 4][:, h % 4, 128:256], Xs[:, vs], True, True, [QX[h // 4][1], Xsb], [pub])
            K.copy(Us[:, 0:512], pu[:, 0:512], [pub], [Usb], eng="act")
            yield
            py, pyb = K.psum()
            for h in range(8):
                vs = slice(h * 64, (h + 1) * 64)
                hs = slice((h % 4) * 128, (h % 4 + 1) * 128)
                K.mm(py[:, vs], hsl(rt, h), Sst[0:64, vs], True, False, [rtb, Sb], [pyb])
                K.mm(py[:, vs], Arb[h // 4][0][:, hs], Us[:, vs], False, False, [Arb[h // 4][1], Usb], [pyb])
                K.mm(py[:, vs], Ark[h // 4][0][:, hs], vtok[:, vs], False, True, [Ark[h // 4][1], vtb], [pyb])
            rows = slice(cc * 64, (cc + 1) * 64)
            K.copy(yt[rows, 0:512], py[rows, 0:512], [pyb], [ytb], eng="act")
            yield
            pS, pSb = K.psum()
            for h in range(8):
                vs = slice(h * 64, (h + 1) * 64)
                K.mm(pS[0:64, vs], bgm[:, vs], Us[:, vs], True, False, [bgmb, Usb], [pSb])
                K.mm(pS[0:64, vs], kgm[:, vs], vtok[:, vs], False, True, [kgmb, vtb], [pSb])
            egl = sm[0:64, 0:16].rearrange("p (h e) -> p h e", h=8)[:, :, cc:cc + 1].to_broadcast([64, 8, 64])
            s3 = Sst[0:64, 0:512].rearrange("p (h v) -> p h v", h=8)
            K.tt(s3, s3, egl, ALU.mult, [Sb, smb], [Sb])
            K.tt(Sst[0:64, 0:512], Sst[0:64, 0:512], pS[0:64, 0:512], ALU.add, [Sb, pSb], [Sb])
        yield
        if d == 0:
            K.dma_out(S["yBf"].ap[t0:t0 + 128, :], yt[:, 0:512], ytb, wr=S["yBf"].bufs(t0, t0 + 128))
        else:
            yf, yfb = G["yf"].next()
            gt, gtb = G["gt"].next()
            bon, bonb = G["bon"].next()
            K.dma_in(yf[:, 0:512], S["yBf"].ap[t0:t0 + 128, :], yfb, rd=S["yBf"].bufs(t0, t0 + 128))
            K.dma_in(gt[:, 0:512], S["gtok"].ap[t0:t0 + 128, :], gtb, rd=S["gtok"].bufs(t0, t0 + 128))
            K.dma_in(bon[:, 0:8], S["bons"].ap[t0:t0 + 128, :], bonb, rd=S["bons"].bufs(t0, t0 + 128))
            K.tt(yt[:, 0:512], yt[:, 0:512], yf[:, 0:512], ALU.add, [ytb, yfb], [ytb])
            y8 = yt[:, 0:512].rearrange("p (h v) -> p h v", h=8)
            sqt, sqtb = G["sqt"]
            sm2, sm2b = G["sm"].next()
            K.P.add("dve", lambda e, o_=sm2[:, 0:8], i_=y8: e.tensor_reduce(out=o_, in_=i_, axis=AX.X, op=ALU.add), reads=[ytb], writes=[sm2b])
            K.act(sqt[:, 0:512], yt[:, 0:512], AF.Square, [ytb], [sqtb])
            K.P.add("dve", lambda e, o_=sm2[:, 8:16], i_=sqt[:, 0:512].rearrange("p (h v) -> p h v", h=8): e.tensor_reduce(out=o_, in_=i_, axis=AX.X, op=ALU.add),
                    reads=[sqtb], writes=[sm2b])
            K.ts(sm2[:, 16:24], sm2[:, 0:8], 1.0 / 64, ALU.mult, [sm2b], [sm2b])
            K.tt(sm2[:, 24:32], sm2[:, 16:24], sm2[:, 16:24], ALU.mult, [sm2b], [sm2b])
            K.stt(sm2[:, 32:40], sm2[:, 8:16], 1.0 / 64, sm2[:, 24:32], ALU.mult, ALU.subtract, [sm2b], [sm2b])
            K.act(sm2[:, 40:48], sm2[:, 32:40], AF.Sqrt, [sm2b, cb], [sm2b], bias=C["epsgn"], scale=1.0)
            K.recip(sm2[:, 40:48], sm2[:, 40:48], [sm2b], [sm2b])
            K.tt(y8, y8, bc_last(sm2[:, 16:24], 64), ALU.subtract, [ytb, sm2b], [ytb])
            K.tt(y8, y8, bc_last(sm2[:, 40:48], 64), ALU.mult, [ytb, sm2b], [ytb])
            K.tt(yt[:, 0:512], yt[:, 0:512], ev(lc, "lnw"), ALU.mult, [ytb, lcb], [ytb])
            K.tt(yt[:, 0:512], yt[:, 0:512], ev(lc, "lnb"), ALU.add, [ytb, lcb], [ytb])
            K.tt(sqt[:, 0:512].rearrange("p (h v) -> p h v", h=8), vtok[:, 0:512].rearrange("p (h v) -> p h v", h=8), bc_last(bon[:, 0:8], 64),
                 ALU.mult, [vtb, bonb], [sqtb], eng="pool")
            K.tt(yt[:, 0:512], yt[:, 0:512], sqt[:, 0:512], ALU.add, [ytb, sqtb], [ytb])
            K.tt(yt[:, 0:512], yt[:, 0:512], gt[:, 0:512], ALU.mult, [ytb, gtb], [ytb])
            pt_, ptb = K.psum()
            for c in range(4):
                cs = slice(c * 128, (c + 1) * 128)
                K.mm(pt_[:, cs], yt[:, cs], C["ident"], True, True, [ytb, cb], [ptb])
            yts, ytsb = G["yts"]
            K.copy(yts[:, 0:512], pt_[:, 0:512], [ptb], [ytsb], eng="act")
            K.dma_out(S["yT"].ap[512:1024, t0:t0 + 128].rearrange("(h p) t -> p h t", p=128), v3(yts[:, 0:512]), ytsb,
                      wr=S["yT"].bufs(t0, t0 + 128))
SCRATCH = {
    "hT": (D, None), "rawA": (1536, None), "rawB": (1760, None), "zt": (None, 512), "abt": (None, 16), "gbt": (None, 16),
    "qT": (512, None), "kT": (512, None), "ktok": (None, 512), "vtok": (None, 512), "oAf": (None, 512), "yT": (1024, None),
    "rTB": (512, None), "kkT": (512, None), "vtokB": (None, 512), "gtok": (None, 512), "bons": (None, 8), "yBf": (None, 512),
    "bT0": (512, None), "bT1": (512, None), "kdT0": (512, None), "kdT1": (512, None),
    "btok0": (None, 512), "btok1": (None, 512), "kdtok0": (None, 512), "kdtok1": (None, 512), "sg0": (None, 512), "sg1": (None, 512),
}


def run_interleaved(K, gens):
    active = list(gens)
    while active:
        nxt = []
        for g, sel in active:
            K.ps_sel = sel
            try:
                next(g)
                nxt.append((g, sel))
            except StopIteration:
                pass
        active = nxt
    K.ps_sel = None


def even_out(K, C, A, W, S, j, t0):
    h, hb = A["h"]
    ua, uab = A["ua"]
    src = S["yT"].ap[:, t0:t0 + TS].rearrange("(c p) t -> p c t", p=128)
    K.dma_in(ua[:, 0:KD * TS].rearrange("p (c t) -> p c t", c=KD), src.bitcast(F32R), uab, rd=S["yT"].bufs(t0, t0 + TS), eng="pool")

    def cons_o(cid, M, ps, pb):
        K.tt(h[:, cid * TS:(cid + 1) * TS], h[:, cid * TS:(cid + 1) * TS], ps[:, 0:TS], ALU.add, [hb, pb], [hb])

    linear_fm(K, C, W["w_out_even"][j], KD, blocks_of(D, 512), ua, uab, cons_o)


def build_program(T, layers, pack, dbg=False, stop_after=None):
    import contextlib
    nc = bass.Bass("TRN2", target_bir_lowering=False)
    x_in = nc.dram_tensor("x", [T, D], F32, kind="ExternalInput").ap()
    p_in = nc.dram_tensor("p", [DEPTH, T, PLE], F32, kind="ExternalInput").ap()
    cp_in = nc.dram_tensor("cpack", [128, pack.n], F32, kind="ExternalInput").ap()
    op_in = nc.dram_tensor("oddpack", [2, 128, 4096], F32, kind="ExternalInput").ap()
    ep_in = nc.dram_tensor("evenpack", [2, 128, 4096], F32, kind="ExternalInput").ap()
    W = {}
    for nm, shp in WEIGHT_SHAPES.items():
        t = nc.dram_tensor(nm, list(shp), F32, kind="ExternalInput").ap()
        W[nm] = [t[i] for i in range(shp[0])]
    out = nc.dram_tensor("out", [T, D], F32, kind="ExternalOutput").ap()
    S = {}
    for nm, (r, c) in SCRATCH.items():
        shp = [r if r is not None else T, c if c is not None else T]
        S[nm] = DT(nc, nm, shp, kind="ExternalOutput" if dbg else "Internal")
    hT = S["hT"]
    NS = T // TS
    with contextlib.ExitStack() as st:
        K = KB(nc, st)
        rem = nc.sbuf_bytes_remaining
        K.init_arena(rem // 4 - 64)
        C = setup_consts(K, cp_in, pack)
        base = K.mark()
        first = True
        for li_pos, li in enumerate(layers):
            last = li_pos == len(layers) - 1
            j = li // 2
            if li % 2 == 1:
                A = alloc_tl(K, C)
                lc, lcb = A["lc"]
                K.dma_in(lc[:, 0:4096], op_in[j], lcb)
                for s_ in range(NS):
                    t0 = s_ * TS
                    if first:
                        load_h_from_x(K, C, A, x_in, t0)
                    else:
                        load_h(K, C, A, hT, t0)
                    odd_mixer(K, C, A, W, j, li)
                    ffn_ple(K, C, A, W, li, p_in[li], t0)
                    if last:
                        final_out(K, C, A, out, t0)
                    else:
                        store_h(K, C, A, hT, t0)
                K.P.barrier()
                K.release(base)
            else:
                A = alloc_tl(K, C)
                even_consts(K, C, A, ep_in[j])
                for s_ in range(NS):
                    t0 = s_ * TS
                    if first:
                        load_h_from_x(K, C, A, x_in, t0)
                        store_h(K, C, A, hT, t0)
                    else:
                        load_h(K, C, A, hT, t0)
                    even_in(K, C, A, W, S, j, li, t0)
                K.P.barrier()
                K.release(base)
                if stop_after == "E1":
                    break
                A = {"lc": C["lc"], "dv": C["dv"]}
                G = alloc_prepA(K)
                for s_ in range(NS):
                    prep_A(K, C, A, G, S, s_ * TS, T)
                K.P.barrier()
                K.release(base)
                if stop_after == "E2A":
                    break
                G = alloc_prepB(K)
                for s_ in range(NS):
                    prep_B(K, C, A, G, S, s_ * TS, T)
                K.P.barrier()
                K.release(base)
                if stop_after == "E2B":
                    break
                GA = alloc_scanA(K)
                GB = alloc_scanB(K)
                for d_ in range(2):
                    run_interleaved(K, [(scan_A(K, C, A, GA, S, d_, T), "a"), (scan_B(K, C, A, GB, S, d_, T), "b")])
                K.P.barrier()
                K.release(base)
                if stop_after == "E3B":
                    break
                A = alloc_tl(K, C)
                for s_ in range(NS):
                    t0 = s_ * TS
                    load_h(K, C, A, hT, t0)
                    even_out(K, C, A, W, S, j, t0)
                    ffn_ple(K, C, A, W, li, p_in[li], t0)
                    if last:
                        final_out(K, C, A, out, t0)
                    else:
                        store_h(K, C, A, hT, t0)
                K.P.barrier()
                K.release(base)
            first = False
        K.P.finalize_and_emit()
    return nc


N_CORES = 8


def kernel(**inputs):
    inp = {k: np.asarray(v) for k, v in inputs.items()}
    B, T = inp["x"].shape[0], inp["x"].shape[1]
    pack = build_packs(inp)
    ep = np.stack([even_pack(inp, 0), even_pack(inp, 1)])
    op = np.stack([odd_pack(inp, 0), odd_pack(inp, 1)])
    cp = pack.array()
    nc = build_program(T, [0, 1, 2, 3], pack)
    wts = {nm: np.ascontiguousarray(inp[nm], dtype=np.float32) for nm in WEIGHT_SHAPES}
    in_maps = []
    for c in range(N_CORES):
        b = c % B
        m = {"x": np.ascontiguousarray(inp["x"][b], dtype=np.float32), "p": np.ascontiguousarray(inp["p"][:, b], dtype=np.float32),
             "cpack": cp, "oddpack": op, "evenpack": ep}
        m.update(wts)
        in_maps.append(m)
    res = run_bass_kernel_spmd(nc, in_maps, core_ids=list(range(N_CORES)))
    out = np.stack([np.asarray(res.results[b]["out"], dtype=np.float32) for b in range(B)])
    return out
```
